# Optimizing a Trainium2 kernel written in Bass

```python
import jax, jax.numpy as jnp
from jax import lax
import numpy as np

D_MODEL = 1024
BATCH = 8
SEQ = 8192
DEPTH = 1
DEC_BATCH = 4
DEC_SEQ = 8192
PAST_LEN = 128

N_META = 16
GRID_W = 64
HEAD_DIM = 64
N_Q_HEADS = 8
N_KV_HEADS = 2
ATTN_WIDTH = N_Q_HEADS * HEAD_DIM
KV_WIDTH = N_KV_HEADS * HEAD_DIM
LRU_WIDTH = D_MODEL - ATTN_WIDTH
LRU_BLOCKS = 8
LRU_BLOCK_W = LRU_WIDTH // LRU_BLOCKS
IN_WIDTH = ATTN_WIDTH + 2 * KV_WIDTH + 2 * LRU_WIDTH
CONV_W = 4
CONV_LEFT = 2
LRU_C = 8.0
ROPE_AXIS_DIM = HEAD_DIM // 2
ROPE_THETA = 10000.0
Q_BLOCK = 128
FFN_HIDDEN = -(-8 * D_MODEL // (3 * 256)) * 256
EPS = 1e-6

kernel_name = "hymba_griffin_axial_gqa_encoder"


def rms_norm(x, g):
    xf = x.astype(jnp.float32)
    y = xf * lax.rsqrt(jnp.mean(xf * xf, axis=-1, keepdims=True) + EPS)
    return (y * g.astype(jnp.float32)).astype(x.dtype)


def axial_rope_tables(n_tokens):
    n_rows = n_tokens // GRID_W
    row = jnp.repeat(jnp.arange(n_rows), GRID_W).astype(jnp.float32)
    col = jnp.tile(jnp.arange(GRID_W), n_rows).astype(jnp.float32)
    freqs = ROPE_THETA ** (-jnp.arange(0, ROPE_AXIS_DIM, 2, dtype=jnp.float32) / ROPE_AXIS_DIM)
    ang = jnp.concatenate([row[:, None] * freqs, col[:, None] * freqs], axis=-1)
    ang = jnp.concatenate([jnp.zeros((N_META, ang.shape[1]), jnp.float32), ang], axis=0)
    return jnp.cos(ang), jnp.sin(ang)


def rotate_half_split(xp, c, s):
    h = xp.shape[-1] // 2
    x1, x2 = xp[..., :h], xp[..., h:]
    c = c[None, :, None, :]
    s = s[None, :, None, :]
    return jnp.concatenate([x1 * c - x2 * s, x2 * c + x1 * s], axis=-1)


def apply_axial_rope(x, cos, sin):
    xf = x.astype(jnp.float32)
    hf = ROPE_AXIS_DIM // 2
    xr = rotate_half_split(xf[..., :ROPE_AXIS_DIM], cos[:, :hf], sin[:, :hf])
    xc = rotate_half_split(xf[..., ROPE_AXIS_DIM:], cos[:, hf:], sin[:, hf:])
    return jnp.concatenate([xr, xc], axis=-1).astype(x.dtype)


def gqa_attention(q, k, v):
    B, L = q.shape[0], q.shape[1]
    G = N_Q_HEADS // N_KV_HEADS
    q = (q * (HEAD_DIM ** -0.5)).reshape(B, L, N_KV_HEADS, G, HEAD_DIM)

    def block(qb):
        s = jnp.einsum('bqkgd,bskd->bkgqs', qb, k).astype(jnp.float32)
        p = jax.nn.softmax(s, axis=-1).astype(v.dtype)
        return jnp.einsum('bkgqs,bskd->bqkgd', p, v)

    o_meta = block(q[:, :N_META]).reshape(B, N_META, ATTN_WIDTH)
    n_real = L - N_META
    nb = n_real // Q_BLOCK
    q_real = jnp.moveaxis(q[:, N_META:].reshape(B, nb, Q_BLOCK, N_KV_HEADS, G, HEAD_DIM), 1, 0)
    o_real = lax.map(block, q_real)
    o_real = jnp.moveaxis(o_real, 0, 1).reshape(B, n_real, ATTN_WIDTH)
    return jnp.concatenate([o_meta, o_real], axis=1)


def centred_depthwise_conv(x, w, b):
    L = x.shape[1]
    xp = jnp.pad(x, ((0, 0), (CONV_LEFT, CONV_W - 1 - CONV_LEFT), (0, 0)))
    out = b
    for j in range(CONV_W):
        out = out + xp[:, j:j + L] * w[j]
    return out


def _lin_combine(e1, e2):
    a1, b1 = e1
    a2, b2 = e2
    return a1 * a2, a2 * b1 + b2


def rg_lru_bidirectional(x, w_a, b_a, w_x, b_x, lam):
    B, L, W = x.shape
    xf = x.astype(jnp.float32)
    xb = xf.reshape(B, L, LRU_BLOCKS, LRU_BLOCK_W)
    ga = jnp.einsum('blhi,dhij->dblhj', xb, w_a.astype(jnp.float32)).reshape(2, B, L, W)
    gx = jnp.einsum('blhi,dhij->dblhj', xb, w_x.astype(jnp.float32)).reshape(2, B, L, W)
    r = jax.nn.sigmoid(ga + b_a.astype(jnp.float32)[:, None, None, :])
    i = jax.nn.sigmoid(gx + b_x.astype(jnp.float32)[:, None, None, :])
    log_a = -LRU_C * jax.nn.softplus(-lam.astype(jnp.float32))[:, None, None, :] * r
    a = jnp.exp(log_a)
    u = jnp.sqrt(-jnp.expm1(2.0 * log_a)) * (i * xf[None])
    h_f = lax.associative_scan(_lin_combine, (a[0], u[0]), axis=1)[1]
    h_b = lax.associative_scan(_lin_combine, (a[1], u[1]), axis=1, reverse=True)[1]
    return (h_f + h_b).astype(x.dtype)


def trunk(x, meta_tokens, norm_mix_g, w_in, q_norm_g, k_norm_g, conv_w, conv_b,
          lru_w_a, lru_b_a, lru_w_x, lru_b_x, lru_lam, attn_out_g, lru_out_g, w_out,
          norm_ffn_g, w_gate_up, w_down, final_norm_g):
    B, n_tok = x.shape[0], x.shape[1]
    cos, sin = axial_rope_tables(n_tok)
    h = jnp.concatenate([jnp.broadcast_to(meta_tokens.astype(x.dtype)[None], (B, N_META, D_MODEL)), x], axis=1)
    L = h.shape[1]
    splits = np.cumsum([ATTN_WIDTH, KV_WIDTH, KV_WIDTH, LRU_WIDTH]).tolist()
    for l in range(DEPTH):
        xn = rms_norm(h, norm_mix_g[l])
        proj = xn @ w_in[l]
        q, k, v, lru_in, lru_gate = jnp.split(proj, splits, axis=-1)
        q = rms_norm(q.reshape(B, L, N_Q_HEADS, HEAD_DIM), q_norm_g[l])
        k = rms_norm(k.reshape(B, L, N_KV_HEADS, HEAD_DIM), k_norm_g[l])
        v = v.reshape(B, L, N_KV_HEADS, HEAD_DIM)
        q = apply_axial_rope(q, cos, sin)
        k = apply_axial_rope(k, cos, sin)
        attn_o = rms_norm(gqa_attention(q, k, v), attn_out_g[l])
        c = centred_depthwise_conv(lru_in, conv_w[l], conv_b[l])
        rec = rg_lru_bidirectional(c, lru_w_a[l], lru_b_a[l], lru_w_x[l], lru_b_x[l], lru_lam[l])
        lru_o = rms_norm(rec * jax.nn.gelu(lru_gate), lru_out_g[l])
        h = h + jnp.concatenate([attn_o, lru_o], axis=-1) @ w_out[l]
        xn = rms_norm(h, norm_ffn_g[l])
        g, u = jnp.split(xn @ w_gate_up[l], 2, axis=-1)
        h = h + (jax.nn.silu(g) * u) @ w_down[l]
    h = rms_norm(h, final_norm_g)
    return h[:, N_META:]


def setup_inputs(seed: int = 0) -> dict:
    key = jax.random.key(seed)
    ks = jax.random.split(key, 24)
    f32 = jnp.float32
    nrm = lambda k, shape, scale: scale * jax.random.normal(k, shape, f32)
    gain = lambda k, shape: 1.0 + 0.02 * jax.random.normal(k, shape, f32)
    u = jax.random.uniform(ks[14], (DEPTH, 2, LRU_WIDTH), f32, minval=0.9, maxval=0.999)
    s = u ** (1.0 / LRU_C)
    lru_lam = jnp.log(s) - jnp.log1p(-s)
    return {
        "x_prompt": jax.random.normal(ks[0], (BATCH, SEQ, D_MODEL), f32),
        "x_sample": jax.random.normal(ks[1], (DEC_BATCH, DEC_SEQ, D_MODEL), f32),
        "meta_tokens": nrm(ks[2], (N_META, D_MODEL), 1.0),
        "norm_mix_g": gain(ks[3], (DEPTH, D_MODEL)),
        "w_in": nrm(ks[4], (DEPTH, D_MODEL, IN_WIDTH), D_MODEL ** -0.5),
        "q_norm_g": gain(ks[5], (DEPTH, HEAD_DIM)),
        "k_norm_g": gain(ks[6], (DEPTH, HEAD_DIM)),
        "conv_w": nrm(ks[7], (DEPTH, CONV_W, LRU_WIDTH), CONV_W ** -0.5),
        "conv_b": nrm(ks[8], (DEPTH, LRU_WIDTH), 0.01),
        "lru_w_a": nrm(ks[9], (DEPTH, 2, LRU_BLOCKS, LRU_BLOCK_W, LRU_BLOCK_W), LRU_BLOCK_W ** -0.5),
        "lru_b_a": nrm(ks[10], (DEPTH, 2, LRU_WIDTH), 0.01),
        "lru_w_x": nrm(ks[11], (DEPTH, 2, LRU_BLOCKS, LRU_BLOCK_W, LRU_BLOCK_W), LRU_BLOCK_W ** -0.5),
        "lru_b_x": nrm(ks[12], (DEPTH, 2, LRU_WIDTH), 0.01),
        "lru_lam": lru_lam,
        "attn_out_g": gain(ks[15], (DEPTH, ATTN_WIDTH)),
        "lru_out_g": gain(ks[16], (DEPTH, LRU_WIDTH)),
        "w_out": nrm(ks[17], (DEPTH, D_MODEL, D_MODEL), D_MODEL ** -0.5),
        "norm_ffn_g": gain(ks[18], (DEPTH, D_MODEL)),
        "w_gate_up": nrm(ks[19], (DEPTH, D_MODEL, 2 * FFN_HIDDEN), D_MODEL ** -0.5),
        "w_down": nrm(ks[20], (DEPTH, FFN_HIDDEN, D_MODEL), FFN_HIDDEN ** -0.5),
        "final_norm_g": gain(ks[21], (D_MODEL,)),
    }


def reference(x_prompt, x_sample, meta_tokens, norm_mix_g, w_in, q_norm_g, k_norm_g, conv_w, conv_b,
              lru_w_a, lru_b_a, lru_w_x, lru_b_x, lru_lam, attn_out_g, lru_out_g, w_out,
              norm_ffn_g, w_gate_up, w_down, final_norm_g):
    weights = (meta_tokens, norm_mix_g, w_in, q_norm_g, k_norm_g, conv_w, conv_b,
               lru_w_a, lru_b_a, lru_w_x, lru_b_x, lru_lam, attn_out_g, lru_out_g, w_out,
               norm_ffn_g, w_gate_up, w_down, final_norm_g)
    y_prompt = trunk(x_prompt, *weights)
    y_sample = trunk(x_sample, *weights)
    return (y_prompt, y_sample)
```

```python
import numpy as np
from contextlib import ExitStack
import concourse.bass as bass
import concourse.mybir as mybir
from concourse.bass_utils import run_bass_kernel_spmd

F32 = mybir.dt.float32
BF16 = mybir.dt.bfloat16
AF = mybir.ActivationFunctionType
ALU = mybir.AluOpType
AX = mybir.AxisListType

COMPUTE = ("pe", "act", "dve", "pool")
EPS = 1e-6


class Buf:
    __slots__ = ("name", "last_w", "readers")

    def __init__(self, name):
        self.name = name
        self.last_w = None
        self.readers = []


class Op:
    __slots__ = ("eng", "fn", "deps", "sig", "slot", "cnt")


class Prog:
    def __init__(self, nc):
        self.nc = nc
        self.ops = {e: [] for e in COMPUTE + ("sp",)}
        self.slots = {}
        self.all_ops = []
        self.dry = False

    def op(self, eng, fn, reads=(), writes=(), slot=None):
        if self.dry:
            return None
        o = Op()
        o.eng, o.fn, o.sig, o.slot, o.cnt = eng, fn, 0, slot, 0
        deps = set()
        for b in reads:
            if b.last_w is not None:
                deps.add(b.last_w)
        for b in writes:
            if b.last_w is not None:
                deps.add(b.last_w)
            deps.update(b.readers)
        for b in reads:
            b.readers.append(o)
        for b in writes:
            b.last_w = o
            b.readers = []
        if eng == "pe" and slot is None:
            deps = {d for d in deps if not (d.eng == "pe" and d.slot is None)}
        deps.discard(o)
        o.deps = deps
        if slot is not None:
            c = self.slots.get(slot, 0) + 1
            self.slots[slot] = c
            o.cnt = 16 * c
        self.ops[eng].append(o)
        self.all_ops.append(o)
        return o

    def emit(self, stack, final_slots=()):
        nc = self.nc
        for o in self.all_ops:
            for d in o.deps:
                if d.slot is None:
                    d.sig = -1
        sems = {}
        for e in COMPUTE:
            sems[e] = stack.enter_context(nc.semaphore("s_" + e))
            n = 0
            for o in self.ops[e]:
                if o.sig == -1:
                    n += 1
                    o.sig = n
        for k in self.slots:
            sems["d_" + k] = stack.enter_context(nc.semaphore("d_" + k))
        block = stack.enter_context(nc.Block())

        def run(ename, eng):
            known = {}
            for o in self.ops[ename]:
                need = {}
                for d in o.deps:
                    if d.slot is None:
                        k, v = d.eng, d.sig
                    else:
                        k, v = "d_" + d.slot, d.cnt
                    if need.get(k, 0) < v:
                        need[k] = v
                for k, v in need.items():
                    if known.get(k, 0) < v:
                        eng.wait_ge(sems[k], v)
                        known[k] = v
                ins = o.fn(eng)
                if o.slot is not None:
                    ins.then_inc(sems["d_" + o.slot], 16)
                elif o.sig > 0:
                    ins.then_inc(sems[ename], 1)
            if ename == "sp":
                for k in final_slots:
                    eng.wait_ge(sems["d_" + k], 16 * self.slots[k])

        block.tensor(lambda e: run("pe", e))
        block.scalar(lambda e: run("act", e))
        block.vector(lambda e: run("dve", e))
        block.gpsimd(lambda e: run("pool", e))
        block.sync(lambda e: run("sp", e))


def bcast_last(ap, n):
    return bass.AP(ap.tensor, ap.offset, [list(x) for x in ap.ap] + [[0, n]])


def rev(ap):
    n = ap.shape[1]
    return bass.AP(ap.tensor, ap[:, n - 1:n].offset, [list(ap.ap[0]), [-ap.ap[1][0], n]])


def build(NBLK, NSLOT, dbg=None):
    T = NBLK * 512
    L = T + 16
    NKT = 1 + NBLK * 4
    nc = bass.Bass("TRN2", target_bir_lowering=False)

    def din(name, shape, dt=F32):
        return nc.dram_tensor(name, list(shape), dt, kind="ExternalInput").ap()

    H = NBLK // 2
    TH = H * 512
    xs0 = din("xs0", [T, 1024])
    xs1 = din("xs1", [T, 1024]) if NSLOT > 1 else None
    xloc = din("xloc", [TH, 1024]) if NSLOT > 1 else None
    ropec1_d = din("ropec1", [128, TH]) if NSLOT > 1 else None
    ropes1_d = din("ropes1", [128, TH]) if NSLOT > 1 else None
    selv_d = din("selv", [1, 2]) if NSLOT > 1 else None
    meta = din("meta", [16, 1024])
    win_d = din("win", [1024, 1792])
    wout_d = din("wout", [1024, 1024])
    wgu_d = din("wgu", [22, 128, 2048])
    wdn_d = din("wdn", [22, 128, 1024])
    gv_d = din("gv", [128, 24])
    gfin_d = din("gfin", [1, 1024])
    pv_d = din("pv", [128, 46])
    wgate_d = din("wgate", [128, 16 * 128])
    cmat_d = din("cmat", [128, 3 * 128])
    brow_d = din("brow", [1, 16 * 128])
    ropec_d = din("ropec", [128, L])
    ropes_d = din("ropes", [128, L])
    y0_d = nc.dram_tensor("y0", [T, 1024], F32, kind="ExternalOutput").ap()
    y1_d = nc.dram_tensor("y1", [TH, 1024], F32, kind="ExternalOutput").ap() if NSLOT > 1 else None
    wgu_s = nc.dram_tensor("wgu_s", [22, 128, 2048], BF16, kind="Internal").ap()
    wdn_s = nc.dram_tensor("wdn_s", [8, 128, 22, 128], BF16, kind="Internal").ap()
    dbg_out = {}

    P = Prog(nc)
    with ExitStack() as st:
        def sb(name, shape, dt):
            return st.enter_context(nc.sbuf_tensor(name, list(shape), dt))

        WIN = sb("WIN", [128, 8, 1792], BF16); WINb = Buf("WIN")
        WOUT = sb("WOUT", [128, 8, 1024], BF16); WOUTb = Buf("WOUT")
        KT = sb("KT", [128, L], BF16); KTb = Buf("KT")
        VT = sb("VT", [128, NKT, 192], BF16); VTb = Buf("VT")
        XB = sb("XB", [128, 4, 1024], F32); XBb = [Buf("XB%d" % i) for i in range(4)]
        GF = sb("GF", [128, 1024], F32); GFb = Buf("GF")
        XN = [sb("XN%d" % i, [128, 1024], BF16) for i in range(2)]; XNb = [Buf("XN0"), Buf("XN1")]
        XT = sb("XT", [128, 8, 512], BF16); XTb = [Buf("XT%d" % i) for i in range(4)]
        ROC = sb("ROC", [128, 512], F32); ROCb = Buf("ROC")
        ROS = sb("ROS", [128, 512], F32); ROSb = Buf("ROS")
        LB = sb("LB", [128, 4, 516], F32); LBb = Buf("LB")
        SCR = sb("SCR", [128, 23, 512], BF16); SCRb = [Buf("SCR%d" % i) for i in range(23)]
        QT = [sb("QT%d" % q, [128, 4, 512], BF16) for q in range(2)]; QTb = [[Buf("QT%d_%d" % (q, i)) for i in range(4)] for q in range(2)]
        SQb_t = sb("SQb", [128, 512], BF16); SQbb = Buf("SQb")
        QGb_t = sb("QGb", [128, 512], BF16); QGbb = Buf("QGb")
        RS = sb("RS", [128, 512], F32); RSb = Buf("RS")
        TA = sb("TA", [128, 512], F32); TAb = Buf("TA")
        PT = [sb("PT%d" % i, [128, 1024], BF16) for i in range(2)]; PTb = [Buf("PT0"), Buf("PT1")]; PTc = PTb[1]
        RD = TA; RDb = TAb
        CAT = sb("CAT", [128, 4, 512], BF16); CATb = [Buf("CAT%d" % i) for i in range(4)]
        CATL = [sb("CATL%d" % q, [128, 4, 512], BF16) for q in range(2)]; CATLb = [[Buf("CATL%d_%d" % (q, i)) for i in range(4)] for q in range(2)]
        WGUB = [sb("WGUB%d" % i, [128, 2048], BF16) for i in range(2)]; WGUBb = [Buf("WGUB0"), Buf("WGUB1")]
        WDC = [sb("WDC%d" % i, [128, 22 * 128], BF16) for i in range(2)]; WDCb = [Buf("WDC0"), Buf("WDC1")]
        WG = sb("WG", [128, 16, 128], BF16); WGb = Buf("WG")
        CM = sb("CM", [128, 3, 128], BF16); CMb = Buf("CM")
        IDF = sb("IDF", [128, 128], F32); IDFb = Buf("IDF")
        BROW = sb("BROW", [16, 128], BF16); BROWb = Buf("BROW")
        PV = sb("PV", [128, 46], F32); PVb = Buf("PV")
        GV = sb("GV", [128, 24], F32); GVb = Buf("GV")
        DV = sb("DV", [128, 40], F32); DVb = Buf("DV")
        RST = sb("RST", [128, 4, NBLK + 2], F32); RSTb = Buf("RST")
        SST = sb("SST", [128, 4, NBLK + 2], F32); SSTb = Buf("SST")
        CARF = sb("CARF", [128, 4], F32); CARFb = Buf("CARF")
        FS0 = sb("FS0", [128, 4], F32); FS8 = sb("FS8", [128, 4], F32); FSb = Buf("FS")
        LH0 = sb("LH0", [128, 4, 2], F32); LH8 = sb("LH8", [128, 4, 2], F32); LHb = Buf("LH")
        SSTL = sb("SSTL", [128, 4, max(H, 1)], F32); SSTLb = Buf("SSTL")
        RSTL = sb("RSTL", [128, 4, max(H, 1)], F32); RSTLb = Buf("RSTL")
        SELV = sb("SELV", [128, 2], F32); SELVb = Buf("SELV")
        CARB = sb("CARB", [128, 4], F32); CARBb = Buf("CARB")
        SM = sb("SM", [128, 8, 4], F32); SMb = [Buf("SM%d" % i) for i in range(8)]
        SSN = sb("SSN", [128, 16], F32); SSNb = Buf("SSN")
        RSN = sb("RSN", [128, 16], F32); RSNb = Buf("RSN")
        PSt = [st.enter_context(nc.psum_tensor("PS%d" % i, [128, 1024], F32)) for i in range(4)]
        BK = [PSt[i // 2][:, (i % 2) * 512:(i % 2) * 512 + 512] for i in range(8)]
        BKb = [Buf("BK%d" % i) for i in range(8)]

        PERM, BONES, IDENT = CM[:, 0, :], CM[:, 1, :], CM[:, 2, :]

        def f32t(t):
            return SCR[:, 2 * t:2 * t + 2, :].rearrange("p a b -> p (a b)").bitcast(F32)

        def f32b(t):
            return [SCRb[2 * t], SCRb[2 * t + 1]]

        def dma(out, in_, slot, reads=(), writes=()):
            P.op("sp", lambda e, o=out, i=in_: e.dma_start(out=o, in_=i), reads, writes, slot=slot)

        def mm(out, lhsT, rhs, start, stop, reads, writes):
            P.op("pe", lambda e, o=out, l=lhsT, r=rhs, s=start, t=stop: e.matmul(o, lhsT=l, rhs=r, start=s, stop=t),
                 reads, writes)

        def act(out, in_, func, reads, writes, **kw):
            P.op("act", lambda e, o=out, i=in_, f=func, k=kw: e.activation(out=o, in_=i, func=f, **k), reads, writes)

        def tt(eng, out, in0, in1, op, reads, writes):
            P.op(eng, lambda e, o=out, a=in0, b=in1, p=op: e.tensor_tensor(out=o, in0=a, in1=b, op=p), reads, writes)

        def ts(eng, out, in0, s1, s2, op0, op1, reads, writes):
            if op1 is None:
                P.op(eng, lambda e, o=out, a=in0, x=s1, p=op0: e.tensor_scalar(out=o, in0=a, scalar1=x, scalar2=None, op0=p),
                     reads, writes)
            else:
                P.op(eng, lambda e, o=out, a=in0, x=s1, y=s2, p=op0, q=op1:
                     e.tensor_scalar(out=o, in0=a, scalar1=x, scalar2=y, op0=p, op1=q), reads, writes)

        def stt(out, in0, scalar, in1, op0, op1, reads, writes):
            P.op("dve", lambda e, o=out, a=in0, s=scalar, b=in1, p=op0, q=op1:
                 e.scalar_tensor_tensor(out=o, in0=a, scalar=s, in1=b, op0=p, op1=q), reads, writes)

        def cp(eng, out, in_, reads, writes):
            P.op(eng, lambda e, o=out, i=in_: e.tensor_copy(out=o, in_=i), reads, writes)

        def mset(eng, ap, val, writes):
            P.op(eng, lambda e, a=ap, v=val: e.memset(a, v), (), writes)

        dbg_n = [0]

        def dump(name, ap, reads):
            if dbg is None or name not in dbg:
                return
            shp = list(ap.shape)
            d = nc.dram_tensor("dbg_" + name, shp, ap.dtype, kind="ExternalOutput").ap()
            dbg_out[name] = "dbg_" + name
            dbg_n[0] += 1
            dma(d, ap, "dbg%d" % dbg_n[0], reads=reads)

        dma(PV[:], pv_d, "pv", writes=[PVb])
        dma(GV[:], gv_d, "gv", writes=[GVb])
        if NSLOT > 1:
            dma(SELV[:], selv_d.partition_broadcast(128), "selv", writes=[SELVb])
        dma(GF[:], gfin_d.partition_broadcast(128), "gf", writes=[GFb])
        stg = [XB[:, 0:2, :].rearrange("p a b -> p (a b)"), XB[:, 2:4, :].rearrange("p a b -> p (a b)")]
        stgb = [[XBb[0], XBb[1]], [XBb[2], XBb[3]]]
        dma(stg[0][:, 0:2048], wgate_d, "stg0", writes=stgb[0])
        cp("dve", WG[:].rearrange("p a b -> p (a b)"), stg[0][:, 0:2048], stgb[0], [WGb])
        dma(stg[1][:, 0:384], cmat_d, "stg1", writes=stgb[1])
        cp("dve", CM[:].rearrange("p a b -> p (a b)"), stg[1][:, 0:384], stgb[1], [CMb])
        cp("dve", IDF[:], stg[1][:, 256:384], stgb[1], [IDFb])
        mset("pool", VT[:, :, 64:128], 1.0, [VTb])
        dma(stg[0][0:16, 0:128], brow_d.rearrange("o (k m) -> (o k) m", k=16), "stg0", writes=stgb[0])
        cp("dve", BROW[:], stg[0][0:16, 0:128], stgb[0], [BROWb])
        ts("dve", DV[:, 0:1], PV[:, 0:1], 0.125, None, ALU.mult, None, [PVb], [DVb])
        ts("dve", DV[:, 8:24], PV[:, 22:38], 0.5, None, ALU.mult, None, [PVb], [DVb])
        act(DV[:, 24:32], PV[:, 38:46], AF.Exp, [PVb], [DVb], scale=-1.0)
        act(DV[:, 24:32], DV[:, 24:32], AF.Ln, [DVb], [DVb], bias=1.0)
        ts("dve", DV[:, 24:32], DV[:, 24:32], -4.0, None, ALU.mult, None, [DVb], [DVb])
        ts("dve", DV[:, 32:40], DV[:, 24:32], 2.0, None, ALU.mult, None, [DVb], [DVb])
        GQ8, GK = DV[:, 0:1], PV[:, 1:2]
        si = 0
        for kc in range(8):
            s_ = si % 2; si += 1
            dma(stg[s_][:, 0:1792], win_d[kc * 128:(kc + 1) * 128, :], "stg%d" % s_, writes=stgb[s_])
            ts("dve" if kc % 2 == 0 else "pool", WIN[:, kc, :], stg[s_][:, 0:1792], GV[:, kc:kc + 1], None, ALU.mult, None,
               stgb[s_] + [GVb], [WINb])
        for kc in range(8):
            s_ = si % 2; si += 1
            dma(stg[s_][:, 0:1024], wout_d[kc * 128:(kc + 1) * 128, :], "stg%d" % s_, writes=stgb[s_])
            ts("dve" if kc % 2 == 0 else "pool", WOUT[:, kc, :], stg[s_][:, 0:1024], GV[:, 8 + kc:9 + kc], None, ALU.mult, None,
               stgb[s_] + [GVb], [WOUTb])
        WGSb = Buf("wgu_s")
        WDSb = Buf("wdn_s")

        def prep_ffn():
            stq = [QT[q][:].rearrange("p a b -> p (a b)").bitcast(F32) for q in range(2)]
            stqb = [QTb[0], QTb[1]]
            u = 0
            for j in range(22):
                b_ = j % 2
                for h in range(2):
                    q_ = u % 2; u += 1
                    dma(stq[q_][:, :], wgu_d[j][:, h * 1024:(h + 1) * 1024], "stq%d" % q_, writes=stqb[q_])
                    tt("dve", WGUB[b_][:, h * 1024:(h + 1) * 1024].rearrange("p (k c) -> p k c", k=4),
                       stq[q_][:, :].rearrange("p (k c) -> p k c", k=4),
                       bcast_last(GV[:, 16 + 4 * h:20 + 4 * h], 256), ALU.mult, stqb[q_] + [GVb], [WGUBb[b_]])
                    yield
                dma(wgu_s[j], WGUB[b_][:], "wgs%d" % b_, reads=[WGUBb[b_]], writes=[WGSb])
            for j in range(22):
                b_ = j % 2
                q_ = u % 2; u += 1
                dma(stq[q_][:, :], wdn_d[j], "stq%d" % q_, writes=stqb[q_])
                cp("pool", WDC[b_][:, 0:1024], stq[q_][:, :], stqb[q_], [WDCb[b_]])
                dma(wdn_s.rearrange("c p j e -> p c j e")[:, :, j, :], WDC[b_][:, 0:1024].rearrange("p (c e) -> p c e", c=8),
                    "wds%d" % b_, reads=[WDCb[b_]], writes=[WDSb])
                yield

        prep_gen = prep_ffn()

        smi = [0]

        def small():
            i = smi[0] % 7
            smi[0] += 1
            return SM[:, i, :], SMb[i]

        xni = [0]

        def norm_transpose(xtile, xbuf, np_, ti, bank):
            sm, smb = small()
            xn = xni[0] % 2
            xni[0] += 1
            act(XN[xn][:np_, :], xtile, AF.Square, [xbuf], [XNb[xn], smb], accum_out=sm[:np_, 0:1])
            act(sm[:np_, 1:2], sm[:np_, 0:1], AF.Ln, [smb], [smb], scale=1.0 / 1024, bias=EPS)
            act(sm[:np_, 2:3], sm[:np_, 1:2], AF.Exp, [smb], [smb], scale=-0.5)
            ts("dve", XN[xn][:np_, :], xtile, sm[:np_, 2:3], None, ALU.mult, None, [xbuf, smb], [XNb[xn]])
            tps = BK[bank].bitcast(BF16)
            for kc in range(8):
                P.op("pe", lambda e, o=tps[:, kc * 128:kc * 128 + np_], i=XN[xn][:np_, kc * 128:(kc + 1) * 128],
                     d=IDENT[:np_, :np_]: e.transpose(out=o, in_=i, identity=d), [XNb[xn], CMb], [BKb[bank]])
            cp("dve", XT[:, :, ti * 128:ti * 128 + np_],
               tps.rearrange("p (k c) -> p k c", k=8)[:, :, 0:np_], [BKb[bank]], [XTb[ti]])

        def proj_fm(c0, n, bank, ntile):
            for kc in range(8):
                mm(BK[bank][:, 0:n], WIN[:, kc, c0:c0 + 128], XT[:, kc, 0:n], kc == 0, kc == 7,
                   [WINb] + XTb[:ntile], [BKb[bank]])

        def proj_fm_g(c0, n, bank, ntile):
            for kc in range(8):
                mm(BK[bank][:, 0:n], WIN[:, kc, c0:c0 + 128], XT[:, kc, 0:n], kc == 0, kc == 7,
                   [WINb] + XTb[:ntile], [BKb[bank]])
                if kc == 3:
                    yield

        def qk_rope(bank, n, gvec, gbufs, dst, dstb, b2, b3):
            act(SQb_t[:, 0:n], BK[bank][:, 0:n], AF.Square, [BKb[bank]], [SQbb])
            act(QGb_t[:, 0:n], BK[bank][:, 0:n], AF.Identity, [BKb[bank]] + gbufs, [QGbb], scale=gvec)
            mm(BK[b2][:, 0:n], BONES, SQb_t[:, 0:n], True, True, [CMb, SQbb], [BKb[b2]])
            mm(BK[b3][:, 0:n], PERM, QGb_t[:, 0:n], True, True, [CMb, QGbb], [BKb[b3]])
            act(RS[:, 0:n], BK[b2][:, 0:n], AF.Ln, [BKb[b2]], [RSb], scale=1.0 / 64, bias=EPS)
            act(RS[:, 0:n], RS[:, 0:n], AF.Exp, [RSb], [RSb], scale=-0.5)
            tt("dve", TA[:, 0:n], QGb_t[:, 0:n], ROC[:, 0:n], ALU.mult, [QGbb, ROCb], [TAb])
            TB_, TBb_ = f32t(10), f32b(10)
            tt("dve", TB_[:, 0:n], BK[b3][:, 0:n], ROS[:, 0:n], ALU.mult, [BKb[b3], ROSb], TBb_)
            tt("dve", TA[:, 0:n], TA[:, 0:n], TB_[:, 0:n], ALU.add, [TAb] + TBb_, [TAb])
            tt("dve", dst, TA[:, 0:n], RS[:, 0:n], ALU.mult, [TAb, RSb], dstb)

        def cwb_of(cwslot):
            if cwslot == 22:
                return SCR[:, 22, :], [SCRb[22]]
            if cwslot == -1:
                return PT[0][:, 0:512], [PTb[0]]
            if cwslot == -2:
                return PT[1][:, 0:512], [PTb[1]]
            return PT[1][:, 512:1024], [PTc]

        def cw_of(cw):
            if cw < 100:
                return f32t(cw), f32b(cw)
            h = cw - 100
            return (CATL[0][:, 2 * h:2 * h + 2, :].rearrange("p a b -> p (a b)").bitcast(F32),
                    [CATLb[0][2 * h], CATLb[0][2 * h + 1]])

        def conv_chunk(c, n, cw=8, cwslot=22):
            CW, CWb_ = cw_of(cw)
            ts("dve", CW[:, 0:n], LB[:, c, 0:n], PV[:, 2 + c:3 + c], PV[:, 18 + c:19 + c], ALU.mult, ALU.add,
               [LBb, PVb], CWb_)
            for j in range(1, 4):
                stt(CW[:, 0:n], LB[:, c, j:j + n], PV[:, 2 + j * 4 + c:3 + j * 4 + c], CW[:, 0:n], ALU.mult, ALU.add,
                    [LBb, PVb] + CWb_, CWb_)
            cwb_ap, cwb_b = cwb_of(cwslot)
            cp("pool", cwb_ap[:, 0:n], CW[:, 0:n], CWb_, cwb_b)

        def gates(c, d, n, tset=None, cw=8, cwslot=22):
            o = (0 if d == 0 else 4) if tset is None else tset
            T0, T1, T2 = f32t(o), f32t(o + 1), f32t(o + 2)
            T0b, T1b, T2b = f32b(o), f32b(o + 1), f32b(o + 2)
            CW, CWb_ = cw_of(cw)
            cwb_ap, cwb_b = cwb_of(cwslot)
            ix = d * 4 + c
            ia, ixx = (d * 2 + 0) * 4 + c, (d * 2 + 1) * 4 + c
            TH = SCR[:, 2 * o:2 * o + 4, :].rearrange("p a b -> p (a b)").bitcast(F32).rearrange("p (t m) -> p t m", t=2)
            PS2 = PSt[3].rearrange("p (t m) -> p t m", t=2)

            def stage1():
                mm(BK[6][:, 0:n], WG[:, ia, :], cwb_ap[:, 0:n], True, False, [WGb] + cwb_b, [BKb[6]])
                mm(BK[6][:, 0:n], BROW[:, :], bcast_last(IDENT[0:16, ia:ia + 1], n)[:, 0, :], False, True, [BROWb, CMb], [BKb[6]])
                mm(BK[7][:, 0:n], WG[:, ixx, :], cwb_ap[:, 0:n], True, False, [WGb] + cwb_b, [BKb[7]])
                mm(BK[7][:, 0:n], BROW[:, :], bcast_last(IDENT[0:16, ixx:ixx + 1], n)[:, 0, :], False, True, [BROWb, CMb], [BKb[7]])
                act(TH[:, :, 0:n], PS2[:, :, 0:n], AF.Tanh, [BKb[6], BKb[7]], T0b + T1b, scale=0.5)

            steps = [
                lambda: act(T2[:, 0:n], T0[:, 0:n], AF.Exp, T0b + [DVb], T2b, scale=DV[:, 24 + ix:25 + ix], bias=DV[:, 24 + ix:25 + ix]),
                lambda: act(T0[:, 0:n], T0[:, 0:n], AF.Exp, T0b + [DVb], T0b, scale=DV[:, 32 + ix:33 + ix], bias=DV[:, 32 + ix:33 + ix]),
                lambda: stt(T1[:, 0:n], T1[:, 0:n], 1.0, CW[:, 0:n], ALU.add, ALU.mult, T1b + CWb_, T1b),
                lambda: act(T0[:, 0:n], T0[:, 0:n], AF.Ln, T0b, T0b, scale=-1.0, bias=1.0 + 2.0 ** -23),
                lambda: act(T0[:, 0:n], T0[:, 0:n], AF.Exp, T0b, T0b, scale=0.5),
                lambda: stt(T1[:, 0:n], T0[:, 0:n], 0.5, T1[:, 0:n], ALU.mult, ALU.mult, T0b + T1b, T1b),
            ]
            return stage1, steps, (T1, T1b, T2, T2b)

        def scan(out, a, u, init, reads, writes):
            P.op("dve", lambda e, o=out, x=a, y=u, i=init: e.tensor_tensor_scan(out=o, data0=x, data1=y, initial=i,
                                                                               op0=ALU.mult, op1=ALU.add), reads, writes)

        def load_x(src, k):
            if k == 0:
                dma(XB[0:16, 0, :], meta, "xb0", writes=[XBb[0]])
            else:
                for i in range(4):
                    r0 = (k - 1) * 512 + i * 128
                    dma(XB[:, i, :], src[r0:r0 + 128, :], "xb%d" % i, writes=[XBb[i]])

        def load_block(src, k, with_rope=True, with_x=True, tabs=None, local=False):
            if with_x:
                load_x(src, k)
            if k == 0:
                n, s0 = 16, 0
            else:
                n, s0 = 512, 16 + (k - 1) * 512
            if with_rope:
                tc_, ts_ = tabs if tabs is not None else (ropec_d, ropes_d)
                t0 = (k - 1) * 512 if local else s0
                dma(ROC[:, 0:n], tc_[:, t0:t0 + n], "roc", writes=[ROCb])
                dma(ROS[:, 0:n], ts_[:, t0:t0 + n], "ros", writes=[ROSb])
            return n, s0

        for sl in range(NSLOT):
            xsrc = xs0 if sl == 0 else xs1
            local = (sl == 1)
            X2 = xloc if local else xs0
            Y2 = y1_d if local else y0_d
            TAB2 = (ropec1_d, ropes1_d) if local else (ropec_d, ropes_d)
            KLAST = H if local else NBLK
            mset("dve", CARB[:], 0.0, [CARBb])
            mset("dve", RST[:, :, NBLK + 1:NBLK + 2], 0.0, [RSTb])
            load_x(xsrc, NBLK)
            CWC = [(8, 22), (9, -1), (100, -2), (101, -3)]

            def p1_AB(k):
                n, s0 = load_block(xsrc, k, with_rope=False, with_x=False)
                ntile = 1 if k == 0 else 4
                for i in range(ntile):
                    norm_transpose(XB[:min(n, 128), i, :], XBb[i], min(n, 128), i, 4 + (i % 2))
                    yield
                if k >= 1:
                    load_x(xsrc, k - 1)
                dma(ROC[:, 0:n], ropec_d[:, s0:s0 + n], "roc", writes=[ROCb])
                dma(ROS[:, 0:n], ropes_d[:, s0:s0 + n], "ros", writes=[ROSb])
                if k >= 1:
                    if k == NBLK:
                        mset("dve", LB[:, :, 512:515], 0.0, [LBb])
                    else:
                        cp("dve", LB[:, :, 512:515], LB[:, :, 0:3], [LBb], [LBb])
                    for c in range(4):
                        proj_fm(768 + c * 128, 512, 5, 4)
                        act(LB[:, c, 0:512], BK[5][:, 0:512], AF.Copy, [BKb[5]], [LBb])
                        yield
                    cp("dve", RST[:, :, k:k + 1], LB[:, :, 0:1], [LBb], [RSTb])
                proj_fm(512, n, 0, ntile)
                yield
                qk_rope(0, n, GK, [PVb], KT[:, s0:s0 + n], [KTb], 1, 2)
                yield
                for i in range(ntile):
                    np_ = min(n, 128)
                    j = 0 if k == 0 else 1 + (k - 1) * 4 + i
                    bnk = 3 if i % 2 == 0 else 2
                    for kc in range(8):
                        mm(BK[bnk][:np_, 0:128], XT[:, kc, i * 128:i * 128 + np_], WIN[:, kc, 640:768], kc == 0, kc == 7,
                           [WINb, XTb[i]], [BKb[bnk]])
                    cp("dve", VT[:np_, j, 0:64], BK[bnk][:np_, 0:64], [BKb[bnk]], [VTb])
                    cp("dve", VT[:np_, j, 128:192], BK[bnk][:np_, 64:128], [BKb[bnk]], [VTb])
                    yield

            def p1_C(k):
                for c in range(4):
                    conv_chunk(c, 512, CWC[c][0], CWC[c][1])
                    yield
                for c0 in (0, 2):
                    cfg = [(c0, 4, 7), (c0 + 1, 0, 3)]
                    gs = []
                    for (c, tset, hb) in cfg:
                        st1, steps, res_ = gates(c, 1, 512, tset, CWC[c][0], CWC[c][1])
                        st1()
                        gs.append((steps, res_))
                        yield
                    for f_, g_ in zip(gs[0][0], gs[1][0]):
                        f_()
                        g_()
                        yield
                    for (c, tset, hb), (steps, (U, Ub, A, Ab)) in zip(cfg, gs):
                        if k == NBLK:
                            mset("dve", U[:, 510:512], 0.0, Ub)
                        HB, HBb = f32t(hb), f32b(hb)
                        scan(rev(HB[:, 0:512]), rev(A[:, 0:512]), rev(U[:, 0:512]), CARB[:, c:c + 1], Ab + Ub + [CARBb], HBb)
                        cp("dve", CARB[:, c:c + 1], HB[:, 0:1], HBb, [CARBb])
                        cp("dve", SST[:, c, k:k + 1], HB[:, 510:511], HBb, [SSTb])
                    yield

            for _ in p1_AB(NBLK):
                pass
            for k in range(NBLK, 0, -1):
                g1, g2 = p1_C(k), p1_AB(k - 1)
                a1 = a2 = True
                while a1 or a2:
                    if a1:
                        a1 = next(g1, "STOP") != "STOP"
                    if a2:
                        a2 = next(g2, "STOP") != "STOP"
                if sl == 0:
                    for _ in range(4):
                        next(prep_gen, None)
            dump("KT", KT[:], [KTb])
            dump("VT", VT[:].rearrange("p a b -> p (a b)"), [VTb])
            dump("SST", SST[:].rearrange("p a b -> p (a b)"), [SSTb])
            dump("RST", RST[:].rearrange("p a b -> p (a b)"), [RSTb])

            if sl == 0:
                for _ in prep_gen:
                    pass
            mset("dve", CARF[:], 0.0, [CARFb])

            def genA(k, lite=False):
                if lite:
                    n, s0 = load_block(xsrc, k, with_rope=False, with_x=True)
                else:
                    n, s0 = load_block(X2, k, with_rope=(k > 0), with_x=(k < 3), tabs=TAB2, local=local)
                ntile = 1 if k == 0 else 4
                nprev = 0 if k == 0 else (16 if k == 1 else 512)
                qp = k % 2
                for i in range(ntile):
                    norm_transpose(XB[:min(n, 128), i, :], XBb[i], min(n, 128), i, 6 + (i % 2))
                    yield
                if k == 0:
                    mset("dve", LB[:, :, 0:2], 0.0, [LBb])
                elif local and not lite and k == 1:
                    pass
                else:
                    cp("dve", LB[:, :, 0:2], LB[:, :, nprev:nprev + 2], [LBb], [LBb])
                for c in range(4):
                    bk = 6 + (c % 2)
                    yield from proj_fm_g(768 + c * 128, n, bk, ntile)
                    yield
                    act(LB[:, c, 2:2 + n], BK[bk][:, 0:n], AF.Copy, [BKb[bk]], [LBb])
                if local and not lite:
                    cp("dve", LB[:, :, 2 + n:3 + n], RSTL[:, :, k - 1:k], [RSTLb, LBb], [LBb])
                else:
                    cp("dve", LB[:, :, 2 + n:3 + n], RST[:, :, k + 1:k + 2], [RSTb, LBb], [LBb])
                for c in range(4):
                    conv_chunk(c, n)
                    yield
                    if k > 0 and not lite:
                        yield from proj_fm_g(c * 128, n, 6, ntile)
                        yield
                        qk_rope(6, n, GQ8, [DVb], QT[qp][:, c, 0:n], [QTb[qp][c]], 7, 6)
                        yield
                    HF, HFb = f32t(3), f32b(3)
                    HB, HBb = f32t(7), f32b(7)
                    GG, GGb = f32t(9), f32b(9)
                    GT, GTb = f32t(10), f32b(10)
                    s1f, stf, (U, Ub, A, Ab) = gates(c, 0, n)
                    s1f()
                    yield
                    if k == 0 or lite:
                        for f_ in stf:
                            f_()
                        scan(HF[:, 0:n], A[:, 0:n], U[:, 0:n], CARF[:, c:c + 1], Ab + Ub + [CARFb], HFb)
                        cp("dve", CARF[:, c:c + 1], HF[:, n - 1:n], HFb, [CARFb])
                        yield
                        continue
                    s1b, stb, (U2, U2b, A2, A2b) = gates(c, 1, n)
                    s1b()
                    yield
                    for f_, g_ in zip(stf, stb):
                        f_()
                        g_()
                    yield
                    scan(HF[:, 0:n], A[:, 0:n], U[:, 0:n], CARF[:, c:c + 1], Ab + Ub + [CARFb], HFb)
                    cp("dve", CARF[:, c:c + 1], HF[:, n - 1:n], HFb, [CARFb])
                    sst_ap, sst_b = (SSTL[:, c, k - 1:k], SSTLb) if local else (SST[:, c, k:k + 1], SSTb)
                    scan(rev(HB[:, 0:n]), rev(A2[:, 0:n]), rev(U2[:, 0:n]), sst_ap, A2b + U2b + [sst_b], HBb)
                    yield from proj_fm_g(1280 + c * 128, n, 6, ntile)
                    yield
                    act(GT[:, 0:n], BK[6][:, 0:n], AF.Square, [BKb[6]], GTb)
                    tt("pool", HF[:, 0:n], HF[:, 0:n], HB[:, 0:n], ALU.add, HFb + HBb, HFb)
                    ts("dve", GT[:, 0:n], GT[:, 0:n], 0.044715, 1.0, ALU.mult, ALU.add, GTb, GTb)
                    tt("dve", GT[:, 0:n], GT[:, 0:n], BK[6][:, 0:n], ALU.mult, GTb + [BKb[6]], GTb)
                    act(GT[:, 0:n], GT[:, 0:n], AF.Tanh, GTb, GTb, scale=0.7978845608028654)
                    stt(GG[:, 0:n], GT[:, 0:n], 1.0, BK[6][:, 0:n], ALU.add, ALU.mult, GTb + [BKb[6]], GGb)
                    stt(CATL[qp][:, c, 0:n], GG[:, 0:n], 0.5, HF[:, 0:n], ALU.mult, ALU.mult, GGb + HFb, [CATLb[qp][c]])
                    yield

            def genATT(k):
                qp = k % 2
                for c in range(4):
                    def qk(j):
                        kp = 16 if j == 0 else 128
                        c0 = 0 if j == 0 else 16 + (j - 1) * 128
                        par = j % 2
                        S = PSt[par]
                        mm(S[:kp, 0:512], KT[0:64, c0:c0 + kp], QT[qp][0:64, c, :], True, True, [KTb, QTb[qp][c]],
                           [BKb[2 * par], BKb[2 * par + 1]])
                        mm(S[:kp, 512:1024], KT[64:128, c0:c0 + kp], QT[qp][64:128, c, :], True, True, [KTb, QTb[qp][c]],
                           [BKb[2 * par], BKb[2 * par + 1]])

                    def ex(j):
                        kp = 16 if j == 0 else 128
                        par = j % 2
                        S = PSt[par]
                        act(PT[par][:kp, :], S[:kp, :], AF.Exp, [BKb[2 * par], BKb[2 * par + 1]], [PTb[par]])

                    def pv(j):
                        kp = 16 if j == 0 else 128
                        par = j % 2
                        st_, sp_ = (j == 0), (j == NKT - 1)
                        mm(BK[4], VT[:kp, j, 0:128], PT[par][:kp, 0:512], st_, sp_, [VTb, PTb[par]], [BKb[4]])
                        mm(BK[5], VT[:kp, j, 64:192], PT[par][:kp, 512:1024], st_, sp_, [VTb, PTb[par]], [BKb[5]])

                    qk(0)
                    for j in range(NKT):
                        if j + 1 < NKT:
                            qk(j + 1)
                        ex(j)
                        if j >= 1:
                            pv(j - 1)
                        yield
                    pv(NKT - 1)
                    P.op("dve", lambda e: e.reciprocal(out=RD[0:64, :], in_=BK[4][64:128, :]), [BKb[4]], [RDb])
                    P.op("dve", lambda e: e.reciprocal(out=RD[64:128, :], in_=BK[5][0:64, :]), [BKb[5]], [RDb])
                    tt("dve", CAT[0:64, c, :], BK[4][0:64, :], RD[0:64, :], ALU.mult, [BKb[4], RDb], [CATb[c]])
                    tt("dve", CAT[64:128, c, :], BK[5][64:128, :], RD[64:128, :], ALU.mult, [BKb[5], RDb], [CATb[c]])

            def genC(k):
                qp = k % 2
                for i in range(4):
                    r0 = (k - 1) * 512 + i * 128
                    dma(XB[:, i, :], X2[r0:r0 + 128, :], "xb%d" % i, writes=[XBb[i]])

                def catc(ch):
                    return (CAT[:, ch, :], CATb[ch]) if ch < 4 else (CATL[qp][:, ch - 4, :], CATLb[qp][ch - 4])
                for i in range(4):
                    for g in range(2):
                        bnk = 6 + g
                        for cc in range(4):
                            ct, cb = catc(g * 4 + cc)
                            mm(BK[bnk][:, 0:128], ct[:, i * 128:(i + 1) * 128], ct[:, i * 128:(i + 1) * 128],
                               cc == 0, cc == 3, [cb], [BKb[bnk]])
                        tmp, tmpb = (TA, TAb) if g == 0 else (RS, RSb)
                        tt("dve", tmp[:, 0:128], BK[bnk][:, 0:128], IDF[:], ALU.mult, [BKb[bnk], IDFb], [tmpb])
                        P.op("dve", lambda e, o=SSN[:, i * 2 + g:i * 2 + g + 1], t_=tmp: e.reduce_sum(out=o, in_=t_[:, 0:128], axis=AX.X),
                             [tmpb], [SSNb])
                    yield
                act(RSN[:, 0:8], SSN[:, 0:8], AF.Ln, [SSNb], [RSNb], scale=1.0 / 512, bias=EPS)
                act(RSN[:, 0:8], RSN[:, 0:8], AF.Exp, [RSNb], [RSNb], scale=-0.5)
                for i in range(4):
                    for hc in range(2):
                        for cc in range(4):
                            ct, cb = catc(cc)
                            mm(BK[6], ct[:, i * 128:(i + 1) * 128], WOUT[:, cc, hc * 512:(hc + 1) * 512], cc == 0, cc == 3,
                               [cb, WOUTb], [BKb[6]])
                        yield
                        for cc in range(4, 8):
                            ct, cb = catc(cc)
                            mm(BK[7], ct[:, i * 128:(i + 1) * 128], WOUT[:, cc, hc * 512:(hc + 1) * 512], cc == 4, cc == 7,
                               [cb, WOUTb], [BKb[7]])
                        xh = XB[:, i, hc * 512:(hc + 1) * 512]
                        stt(xh, BK[6], RSN[:, 2 * i:2 * i + 1], xh, ALU.mult, ALU.add, [BKb[6], RSNb, XBb[i]], [XBb[i]])
                        stt(xh, BK[7], RSN[:, 2 * i + 1:2 * i + 2], xh, ALU.mult, ALU.add, [BKb[7], RSNb, XBb[i]], [XBb[i]])
                        yield
                for i in range(4):
                    norm_transpose(XB[:, i, :], XBb[i], 128, i, 6 + (i % 2))
                    yield
                for j in range(22):
                    b_ = j % 2
                    dma(WGUB[b_][:], wgu_s[j], "wgu%d" % b_, reads=[WGSb], writes=[WGUBb[b_]])
                    for kc in range(8):
                        mm(BK[6], WGUB[b_][:, kc * 256:kc * 256 + 128], XT[:, kc, :], kc == 0, kc == 7,
                           [WGUBb[b_]] + XTb, [BKb[6]])
                        if kc % 4 == 3:
                            yield
                    for kc in range(8):
                        mm(BK[7], WGUB[b_][:, kc * 256 + 128:kc * 256 + 256], XT[:, kc, :], kc == 0, kc == 7,
                           [WGUBb[b_]] + XTb, [BKb[7]])
                        if kc == 3:
                            yield
                    tmp, tmpb = (TA, TAb) if b_ == 0 else (RS, RSb)
                    act(tmp[:], BK[6], AF.Tanh, [BKb[6]], [tmpb], scale=0.5)
                    stt(tmp[:], tmp[:], 1.0, BK[6], ALU.add, ALU.mult, [tmpb, BKb[6]], [tmpb])
                    stt(SCR[:, j, :], tmp[:], 0.5, BK[7], ALU.mult, ALU.mult, [tmpb, BKb[7]], [SCRb[j]])
                    yield
                for cc in range(8):
                    b_ = cc % 2
                    dma(WDC[b_][:], wdn_s[cc].rearrange("p j e -> p (j e)"), "wdn%d" % b_, reads=[WDSb], writes=[WDCb[b_]])
                    for j in range(22):
                        mm(BK[6], WDC[b_][:, j * 128:(j + 1) * 128], SCR[:, j, :], j == 0, j == 21, [WDCb[b_], SCRb[j]], [BKb[6]])
                        if j % 4 == 3:
                            yield
                    act(RS[:], BK[6], AF.Copy, [BKb[6]], [RSb])
                    for i in range(4):
                        P.op("pe", lambda e, o=BK[7][:, i * 128:(i + 1) * 128], i_=RS[:, i * 128:(i + 1) * 128]:
                             e.transpose(out=o, in_=i_, identity=IDF[:]), [RSb, IDFb], [BKb[7]])
                    tt("dve", XB[:, :, cc * 128:(cc + 1) * 128], BK[7].rearrange("p (i e) -> p i e", i=4),
                       XB[:, :, cc * 128:(cc + 1) * 128], ALU.add, [BKb[7]] + XBb, XBb)
                    yield
                for i in range(4):
                    sm, smb = small()
                    xn = xni[0] % 2
                    xni[0] += 1
                    act(XN[xn][:], XB[:, i, :], AF.Square, [XBb[i]], [XNb[xn], smb], accum_out=sm[:, 0:1])
                    act(sm[:, 1:2], sm[:, 0:1], AF.Ln, [smb], [smb], scale=1.0 / 1024, bias=EPS)
                    act(sm[:, 2:3], sm[:, 1:2], AF.Exp, [smb], [smb], scale=-0.5)
                    stt(XB[:, i, :], XB[:, i, :], sm[:, 2:3], GF[:], ALU.mult, ALU.mult, [XBb[i], smb, GFb], [XBb[i]])
                    r0 = (k - 1) * 512 + i * 128
                    dma(Y2[r0:r0 + 128, :], XB[:, i, :], "y%d" % i, reads=[XBb[i]])
                    if k + 2 <= KLAST:
                        r2 = (k + 1) * 512 + i * 128
                        dma(XB[:, i, :], X2[r2:r2 + 128, :], "xb%d" % i, writes=[XBb[i]])
                    yield

            def lane(w):
                if w - 1 >= 1:
                    yield from genC(w - 1)
                if w + 1 <= KLAST:
                    yield from genA(w + 1)

            def drain(g):
                for _ in g:
                    pass

            def count(g):
                P.dry = True
                n_ = sum(1 for _ in g)
                P.dry = False
                return n_

            if local:
                for k in range(0, H + 1):
                    drain(genA(k, lite=True))
                    if k == 0:
                        cp("dve", FS0[:], CARF[:], [CARFb], [FSb])
                        cp("dve", LH0[:], LB[:, :, 16:18], [LBb], [LHb])
                    if k == H:
                        cp("dve", FS8[:], CARF[:], [CARFb], [FSb])
                        cp("dve", LH8[:], LB[:, :, 512:514], [LBb], [LHb])
                sa, sb_ = SELV[:, 0:1], SELV[:, 1:2]
                ts("dve", CARF[:], FS0[:], sa, None, ALU.mult, None, [FSb, SELVb], [CARFb])
                stt(CARF[:], FS8[:], sb_, CARF[:], ALU.mult, ALU.add, [FSb, SELVb, CARFb], [CARFb])
                ts("dve", LB[:, :, 0:2], LH0[:], sa, None, ALU.mult, None, [LHb, SELVb], [LBb])
                stt(LB[:, :, 0:2], LH8[:], sb_, LB[:, :, 0:2], ALU.mult, ALU.add, [LHb, SELVb, LBb], [LBb])
                ts("dve", SSTL[:], SST[:, :, 1:1 + H], sa, None, ALU.mult, None, [SSTb, SELVb], [SSTLb])
                stt(SSTL[:], SST[:, :, 1 + H:1 + 2 * H], sb_, SSTL[:], ALU.mult, ALU.add, [SSTb, SELVb, SSTLb], [SSTLb])
                ts("dve", RSTL[:], RST[:, :, 2:2 + H], sa, None, ALU.mult, None, [RSTb, SELVb], [RSTLb])
                stt(RSTL[:], RST[:, :, 2 + H:2 + 2 * H], sb_, RSTL[:], ALU.mult, ALU.add, [RSTb, SELVb, RSTLb], [RSTLb])
            else:
                drain(genA(0))
            drain(genA(1))
            for w in range(1, KLAST + 1):
                nl = count(lane(w))
                na = 4 * NKT
                ln = lane(w)
                done = 0
                for step, _ in enumerate(genATT(w), 1):
                    tgt = (step * nl) // na
                    while done < tgt:
                        next(ln, None)
                        done += 1
                drain(ln)
            drain(genC(KLAST))

        P.emit(st, final_slots=["y0", "y1", "y2", "y3"] + ["dbg%d" % (i + 1) for i in range(dbg_n[0])])
    return nc, dbg_out


def rope_tables(L):
    f32 = np.float32
    t = np.arange(L - 16)
    row = (t // 64).astype(f32)
    col = (t % 64).astype(f32)
    freqs = (f32(10000.0) ** (-np.arange(0, 32, 2, dtype=f32) / f32(32))).astype(f32)
    C = np.ones((128, L), f32)
    S = np.zeros((128, L), f32)
    for p in range(128):
        dm = p % 64
        pos = row if dm // 32 == 0 else col
        ang = (pos * freqs[dm % 16]).astype(f32)
        C[p, 16:] = np.cos(ang)
        S[p, 16:] = np.sin(ang)
    return C, S


def host_layout(inp):
    f32 = np.float32
    w_in = np.asarray(inp["w_in"])[0]
    qcols = np.array([(c + 4 * h) * 64 + d for c in range(4) for h in range(2) for d in range(64)])
    win = np.ascontiguousarray(np.concatenate([w_in[:, qcols], w_in[:, 512:]], axis=1), dtype=f32)
    rowmap = np.array([(kc + 4 * (p // 64)) * 64 + p % 64 if kc < 4 else 512 + (kc - 4) * 128 + p
                       for kc in range(8) for p in range(128)])
    w_out = np.asarray(inp["w_out"])[0]
    wout = np.ascontiguousarray(w_out[rowmap, :], dtype=f32)
    gcat = np.concatenate([np.asarray(inp["attn_out_g"])[0], np.asarray(inp["lru_out_g"])[0]])[rowmap]
    pk = lambda v: np.asarray(v, f32).reshape(8, 128).T
    gv = np.ascontiguousarray(np.concatenate([pk(np.asarray(inp["norm_mix_g"])[0]), pk(gcat),
                                              pk(np.asarray(inp["norm_ffn_g"])[0])], axis=1), dtype=f32)
    wgu_o = np.asarray(inp["w_gate_up"])[0]
    g4 = wgu_o[:, :2816].reshape(8, 128, 22, 128)
    u4 = wgu_o[:, 2816:].reshape(8, 128, 22, 128)
    wgu = np.stack([g4, u4], axis=0).transpose(3, 2, 1, 0, 4)
    wgu = np.ascontiguousarray(wgu.reshape(22, 128, 2048), dtype=f32)
    wdn = np.ascontiguousarray(np.asarray(inp["w_down"])[0].reshape(22, 128, 1024), dtype=f32)
    pv = np.zeros((128, 46), f32)
    p = np.arange(128)
    pv[:, 0] = np.asarray(inp["q_norm_g"])[0][p % 64]
    pv[:, 1] = np.asarray(inp["k_norm_g"])[0][p % 64]
    cw = np.asarray(inp["conv_w"])[0]
    cb = np.asarray(inp["conv_b"])[0]
    for c in range(4):
        for j in range(4):
            pv[:, 2 + j * 4 + c] = cw[j, c * 128 + p]
        pv[:, 18 + c] = cb[c * 128 + p]
        for d in range(2):
            pv[:, 22 + d * 4 + c] = np.asarray(inp["lru_b_a"])[0][d, c * 128 + p]
            pv[:, 30 + d * 4 + c] = np.asarray(inp["lru_b_x"])[0][d, c * 128 + p]
            pv[:, 38 + d * 4 + c] = np.asarray(inp["lru_lam"])[0][d, c * 128 + p]
    wgate = np.zeros((128, 16, 128), f32)
    wa = np.asarray(inp["lru_w_a"])[0]
    wx = np.asarray(inp["lru_w_x"])[0]
    for d in range(2):
        for ax, w in enumerate((wa, wx)):
            for c in range(4):
                for hb in range(2):
                    wgate[hb * 64:(hb + 1) * 64, (d * 2 + ax) * 4 + c, hb * 64:(hb + 1) * 64] = w[d, 2 * c + hb]
    brow = np.zeros((1, 16, 128), f32)
    for d in range(2):
        for ax, bb in enumerate((np.asarray(inp['lru_b_a'])[0], np.asarray(inp['lru_b_x'])[0])):
            for c in range(4):
                brow[0, (d * 2 + ax) * 4 + c, :] = bb[d, c * 128:(c + 1) * 128]
    cmat = np.zeros((128, 3, 128), f32)
    for m in range(128):
        if (m % 64) % 32 < 16:
            cmat[m + 16, 0, m] = -1.0
        else:
            cmat[m - 16, 0, m] = 1.0
        cmat[(m // 64) * 64:(m // 64) * 64 + 64, 1, m] = 1.0
        cmat[m, 2, m] = 1.0
    return dict(win=win, wout=wout, wgu=wgu, wdn=wdn, gv=gv,
                gfin=np.asarray(inp["final_norm_g"], f32).reshape(1, 1024),
                pv=pv, brow=brow.reshape(1, 2048), wgate=wgate.reshape(128, 2048), cmat=cmat.reshape(128, 384),
                meta=np.ascontiguousarray(np.asarray(inp["meta_tokens"], f32)))


_CACHE = {}


def run(cores, shared, NBLK, NSLOT, dbg=None):
    key = (NBLK, NSLOT)
    if key not in _CACHE:
        _CACHE[key] = build(NBLK, NSLOT, dbg)
    nc, dbg_out = _CACHE[key]
    C, S = rope_tables(NBLK * 512 + 16)
    TH = (NBLK // 2) * 512
    in_maps = []
    for cd in cores:
        m = dict(shared)
        m["xs0"] = np.ascontiguousarray(cd["x0"], dtype=np.float32)
        m["ropec"] = C
        m["ropes"] = S
        if NSLOT > 1:
            hh = cd["half"]
            x1 = np.asarray(cd["x1"], dtype=np.float32)
            m["xs1"] = np.ascontiguousarray(x1)
            m["xloc"] = np.ascontiguousarray(x1[hh * TH:(hh + 1) * TH])
            m["ropec1"] = np.ascontiguousarray(C[:, 16 + hh * TH:16 + (hh + 1) * TH])
            m["ropes1"] = np.ascontiguousarray(S[:, 16 + hh * TH:16 + (hh + 1) * TH])
            m["selv"] = np.array([[1.0 - hh, float(hh)]], dtype=np.float32)
        in_maps.append(m)
    res = run_bass_kernel_spmd(nc, in_maps, core_ids=list(range(len(cores))))
    return res.results, dbg_out


def kernel(**inputs):
    xp = np.asarray(inputs["x_prompt"], np.float32)
    xsm = np.asarray(inputs["x_sample"], np.float32)
    shared = host_layout(inputs)
    cores = [dict(x0=xp[ci], x1=xsm[ci // 2], half=ci % 2) for ci in range(8)]
    results, _ = run(cores, shared, 16, 2)
    yp = np.stack([results[ci]["y0"] for ci in range(8)], axis=0)
    ys = np.stack([np.concatenate([results[2 * j]["y1"], results[2 * j + 1]["y1"]], axis=0) for j in range(4)], axis=0)
    return (yp, ys)
```

```python
import numpy as np
from contextlib import ExitStack
import concourse.bass as bass
import concourse.mybir as mybir
from concourse.bass_utils import run_bass_kernel_spmd

F32 = mybir.dt.float32
BF16 = mybir.dt.bfloat16
AF = mybir.ActivationFunctionType
ALU = mybir.AluOpType
AX = mybir.AxisListType

COMPUTE = ("pe", "act", "dve", "pool")
EPS = 1e-6


class Buf:
    __slots__ = ("name", "last_w", "readers")

    def __init__(self, name):
        self.name = name
        self.last_w = None
        self.readers = []


class Op:
    __slots__ = ("eng", "fn", "deps", "sig", "slot", "cnt")


class Prog:
    def __init__(self, nc):
        self.nc = nc
        self.ops = {e: [] for e in COMPUTE + ("sp",)}
        self.slots = {}
        self.all_ops = []
        self.dry = False

    def op(self, eng, fn, reads=(), writes=(), slot=None):
        if self.dry:
            return None
        o = Op()
        o.eng, o.fn, o.sig, o.slot, o.cnt = eng, fn, 0, slot, 0
        deps = set()
        for b in reads:
            if b.last_w is not None:
                deps.add(b.last_w)
        for b in writes:
            if b.last_w is not None:
                deps.add(b.last_w)
            deps.update(b.readers)
        for b in reads:
            b.readers.append(o)
        for b in writes:
            b.last_w = o
            b.readers = []
        if eng == "pe" and slot is None:
            deps = {d for d in deps if not (d.eng == "pe" and d.slot is None)}
        deps.discard(o)
        o.deps = deps
        if slot is not None:
            c = self.slots.get(slot, 0) + 1
            self.slots[slot] = c
            o.cnt = 16 * c
        self.ops[eng].append(o)
        self.all_ops.append(o)
        return o

    def emit(self, stack, final_slots=()):
        nc = self.nc
        for o in self.all_ops:
            for d in o.deps:
                if d.slot is None:
                    d.sig = -1
        sems = {}
        for e in COMPUTE:
            sems[e] = stack.enter_context(nc.semaphore("s_" + e))
            n = 0
            for o in self.ops[e]:
                if o.sig == -1:
                    n += 1
                    o.sig = n
        for k in self.slots:
            sems["d_" + k] = stack.enter_context(nc.semaphore("d_" + k))
        block = stack.enter_context(nc.Block())

        def run(ename, eng):
            known = {}
            for o in self.ops[ename]:
                need = {}
                for d in o.deps:
                    if d.slot is None:
                        k, v = d.eng, d.sig
                    else:
                        k, v = "d_" + d.slot, d.cnt
                    if need.get(k, 0) < v:
                        need[k] = v
                for k, v in need.items():
                    if known.get(k, 0) < v:
                        eng.wait_ge(sems[k], v)
                        known[k] = v
                ins = o.fn(eng)
                if o.slot is not None:
                    ins.then_inc(sems["d_" + o.slot], 16)
                elif o.sig > 0:
                    ins.then_inc(sems[ename], 1)
            if ename == "sp":
                for k in final_slots:
                    eng.wait_ge(sems["d_" + k], 16 * self.slots[k])

        block.tensor(lambda e: run("pe", e))
        block.scalar(lambda e: run("act", e))
        block.vector(lambda e: run("dve", e))
        block.gpsimd(lambda e: run("pool", e))
        block.sync(lambda e: run("sp", e))


def bcast_last(ap, n):
    return bass.AP(ap.tensor, ap.offset, [list(x) for x in ap.ap] + [[0, n]])


def rev(ap):
    n = ap.shape[1]
    return bass.AP(ap.tensor, ap[:, n - 1:n].offset, [list(ap.ap[0]), [-ap.ap[1][0], n]])


def build(NBLK, NSLOT, dbg=None):
    T = NBLK * 512
    L = T + 16
    NKT = 1 + NBLK * 4
    nc = bass.Bass("TRN2", target_bir_lowering=False)

    def din(name, shape, dt=F32):
        return nc.dram_tensor(name, list(shape), dt, kind="ExternalInput").ap()

    H = NBLK // 2
    TH = H * 512
    xs0 = din("xs0", [T, 1024])
    xs1 = din("xs1", [T, 1024]) if NSLOT > 1 else None
    xloc = din("xloc", [TH, 1024]) if NSLOT > 1 else None
    ropec1_d = din("ropec1", [128, TH]) if NSLOT > 1 else None
    ropes1_d = din("ropes1", [128, TH]) if NSLOT > 1 else None
    selv_d = din("selv", [1, 2]) if NSLOT > 1 else None
    meta = din("meta", [16, 1024])
    win_d = din("win", [1024, 1792])
    wout_d = din("wout", [1024, 1024])
    wgu_d = din("wgu", [22, 128, 2048])
    wdn_d = din("wdn", [22, 128, 1024])
    gv_d = din("gv", [128, 24])
    gfin_d = din("gfin", [1, 1024])
    pv_d = din("pv", [128, 46])
    wgate_d = din("wgate", [128, 16 * 128])
    cmat_d = din("cmat", [128, 3 * 128])
    brow_d = din("brow", [1, 16 * 128])
    ropec_d = din("ropec", [128, L])
    ropes_d = din("ropes", [128, L])
    y0_d = nc.dram_tensor("y0", [T, 1024], F32, kind="ExternalOutput").ap()
    y1_d = nc.dram_tensor("y1", [TH, 1024], F32, kind="ExternalOutput").ap() if NSLOT > 1 else None
    wgu_s = nc.dram_tensor("wgu_s", [22, 128, 2048], BF16, kind="Internal").ap()
    wdn_s = nc.dram_tensor("wdn_s", [8, 128, 22, 128], BF16, kind="Internal").ap()
    dbg_out = {}

    P = Prog(nc)
    with ExitStack() as st:
        def sb(name, shape, dt):
            return st.enter_context(nc.sbuf_tensor(name, list(shape), dt))

        WIN = sb("WIN", [128, 8, 1792], BF16); WINb = Buf("WIN")
        WOUT = sb("WOUT", [128, 8, 1024], BF16); WOUTb = Buf("WOUT")
        KT = sb("KT", [128, L], BF16); KTb = Buf("KT")
        VT = sb("VT", [128, NKT, 192], BF16); VTb = Buf("VT")
        XB = sb("XB", [128, 4, 1024], F32); XBb = [Buf("XB%d" % i) for i in range(4)]
        GF = sb("GF", [128, 1024], F32); GFb = Buf("GF")
        XN = [sb("XN%d" % i, [128, 1024], BF16) for i in range(2)]; XNb = [Buf("XN0"), Buf("XN1")]
        XT = sb("XT", [128, 8, 512], BF16); XTb = [Buf("XT%d" % i) for i in range(4)]
        ROC = sb("ROC", [128, 512], F32); ROCb = Buf("ROC")
        ROS = sb("ROS", [128, 512], F32); ROSb = Buf("ROS")
        LB = sb("LB", [128, 4, 516], F32); LBb = Buf("LB")
        SCR = sb("SCR", [128, 23, 512], BF16); SCRb = [Buf("SCR%d" % i) for i in range(23)]
        QT = [sb("QT%d" % q, [128, 4, 512], BF16) for q in range(2)]; QTb = [[Buf("QT%d_%d" % (q, i)) for i in range(4)] for q in range(2)]
        SQb_t = sb("SQb", [128, 512], BF16); SQbb = Buf("SQb")
        QGb_t = sb("QGb", [128, 512], BF16); QGbb = Buf("QGb")
        RS = sb("RS", [128, 512], F32); RSb = Buf("RS")
        TA = sb("TA", [128, 512], F32); TAb = Buf("TA")
        PT = [sb("PT%d" % i, [128, 1024], BF16) for i in range(2)]; PTb = [Buf("PT0"), Buf("PT1")]; PTc = PTb[1]
        RD = TA; RDb = TAb
        CAT = sb("CAT", [128, 4, 512], BF16); CATb = [Buf("CAT%d" % i) for i in range(4)]
        CATL = [sb("CATL%d" % q, [128, 4, 512], BF16) for q in range(2)]; CATLb = [[Buf("CATL%d_%d" % (q, i)) for i in range(4)] for q in range(2)]
        WGUB = [sb("WGUB%d" % i, [128, 2048], BF16) for i in range(2)]; WGUBb = [Buf("WGUB0"), Buf("WGUB1")]
        WDC = [sb("WDC%d" % i, [128, 22 * 128], BF16) for i in range(2)]; WDCb = [Buf("WDC0"), Buf("WDC1")]
        WG = sb("WG", [128, 16, 128], BF16); WGb = Buf("WG")
        CM = sb("CM", [128, 3, 128], BF16); CMb = Buf("CM")
        IDF = sb("IDF", [128, 128], F32); IDFb = Buf("IDF")
        BROW = sb("BROW", [16, 128], BF16); BROWb = Buf("BROW")
        PV = sb("PV", [128, 46], F32); PVb = Buf("PV")
        GV = sb("GV", [128, 24], F32); GVb = Buf("GV")
        DV = sb("DV", [128, 40], F32); DVb = Buf("DV")
        RST = sb("RST", [128, 4, NBLK + 2], F32); RSTb = Buf("RST")
        SST = sb("SST", [128, 4, NBLK + 2], F32); SSTb = Buf("SST")
        CARF = sb("CARF", [128, 4], F32); CARFb = Buf("CARF")
        FS0 = sb("FS0", [128, 4], F32); FS8 = sb("FS8", [128, 4], F32); FSb = Buf("FS")
        LH0 = sb("LH0", [128, 4, 2], F32); LH8 = sb("LH8", [128, 4, 2], F32); LHb = Buf("LH")
        SSTL = sb("SSTL", [128, 4, max(H, 1)], F32); SSTLb = Buf("SSTL")
        RSTL = sb("RSTL", [128, 4, max(H, 1)], F32); RSTLb = Buf("RSTL")
        SELV = sb("SELV", [128, 2], F32); SELVb = Buf("SELV")
        CARB = sb("CARB", [128, 4], F32); CARBb = Buf("CARB")
        SM = sb("SM", [128, 8, 4], F32); SMb = [Buf("SM%d" % i) for i in range(8)]
        SSN = sb("SSN", [128, 16], F32); SSNb = Buf("SSN")
        RSN = sb("RSN", [128, 16], F32); RSNb = Buf("RSN")
        PSt = [st.enter_context(nc.psum_tensor("PS%d" % i, [128, 1024], F32)) for i in range(4)]
        BK = [PSt[i // 2][:, (i % 2) * 512:(i % 2) * 512 + 512] for i in range(8)]
        BKb = [Buf("BK%d" % i) for i in range(8)]

        PERM, BONES, IDENT = CM[:, 0, :], CM[:, 1, :], CM[:, 2, :]

        def f32t(t):
            return SCR[:, 2 * t:2 * t + 2, :].rearrange("p a b -> p (a b)").bitcast(F32)

        def f32b(t):
            return [SCRb[2 * t], SCRb[2 * t + 1]]

        def dma(out, in_, slot, reads=(), writes=()):
            P.op("sp", lambda e, o=out, i=in_: e.dma_start(out=o, in_=i), reads, writes, slot=slot)

        def mm(out, lhsT, rhs, start, stop, reads, writes):
            P.op("pe", lambda e, o=out, l=lhsT, r=rhs, s=start, t=stop: e.matmul(o, lhsT=l, rhs=r, start=s, stop=t),
                 reads, writes)

        def act(out, in_, func, reads, writes, **kw):
            P.op("act", lambda e, o=out, i=in_, f=func, k=kw: e.activation(out=o, in_=i, func=f, **k), reads, writes)

        def tt(eng, out, in0, in1, op, reads, writes):
            P.op(eng, lambda e, o=out, a=in0, b=in1, p=op: e.tensor_tensor(out=o, in0=a, in1=b, op=p), reads, writes)

        def ts(eng, out, in0, s1, s2, op0, op1, reads, writes):
            if op1 is None:
                P.op(eng, lambda e, o=out, a=in0, x=s1, p=op0: e.tensor_scalar(out=o, in0=a, scalar1=x, scalar2=None, op0=p),
                     reads, writes)
            else:
                P.op(eng, lambda e, o=out, a=in0, x=s1, y=s2, p=op0, q=op1:
                     e.tensor_scalar(out=o, in0=a, scalar1=x, scalar2=y, op0=p, op1=q), reads, writes)

        def stt(out, in0, scalar, in1, op0, op1, reads, writes):
            P.op("dve", lambda e, o=out, a=in0, s=scalar, b=in1, p=op0, q=op1:
                 e.scalar_tensor_tensor(out=o, in0=a, scalar=s, in1=b, op0=p, op1=q), reads, writes)

        def cp(eng, out, in_, reads, writes):
            P.op(eng, lambda e, o=out, i=in_: e.tensor_copy(out=o, in_=i), reads, writes)

        def mset(eng, ap, val, writes):
            P.op(eng, lambda e, a=ap, v=val: e.memset(a, v), (), writes)

        dbg_n = [0]

        def dump(name, ap, reads):
            if dbg is None or name not in dbg:
                return
            shp = list(ap.shape)
            d = nc.dram_tensor("dbg_" + name, shp, ap.dtype, kind="ExternalOutput").ap()
            dbg_out[name] = "dbg_" + name
            dbg_n[0] += 1
            dma(d, ap, "dbg%d" % dbg_n[0], reads=reads)

        dma(PV[:], pv_d, "pv", writes=[PVb])
        dma(GV[:], gv_d, "gv", writes=[GVb])
        if NSLOT > 1:
            dma(SELV[:], selv_d.partition_broadcast(128), "selv", writes=[SELVb])
        dma(GF[:], gfin_d.partition_broadcast(128), "gf", writes=[GFb])
        stg = [XB[:, 0:2, :].rearrange("p a b -> p (a b)"), XB[:, 2:4, :].rearrange("p a b -> p (a b)")]
        stgb = [[XBb[0], XBb[1]], [XBb[2], XBb[3]]]
        dma(stg[0][:, 0:2048], wgate_d, "stg0", writes=stgb[0])
        cp("dve", WG[:].rearrange("p a b -> p (a b)"), stg[0][:, 0:2048], stgb[0], [WGb])
        dma(stg[1][:, 0:384], cmat_d, "stg1", writes=stgb[1])
        cp("dve", CM[:].rearrange("p a b -> p (a b)"), stg[1][:, 0:384], stgb[1], [CMb])
        cp("dve", IDF[:], stg[1][:, 256:384], stgb[1], [IDFb])
        mset("pool", VT[:, :, 64:128], 1.0, [VTb])
        dma(stg[0][0:16, 0:128], brow_d.rearrange("o (k m) -> (o k) m", k=16), "stg0", writes=stgb[0])
        cp("dve", BROW[:], stg[0][0:16, 0:128], stgb[0], [BROWb])
        ts("dve", DV[:, 0:1], PV[:, 0:1], 0.125, None, ALU.mult, None, [PVb], [DVb])
        ts("dve", DV[:, 8:24], PV[:, 22:38], 0.5, None, ALU.mult, None, [PVb], [DVb])
        act(DV[:, 24:32], PV[:, 38:46], AF.Exp, [PVb], [DVb], scale=-1.0)
        act(DV[:, 24:32], DV[:, 24:32], AF.Ln, [DVb], [DVb], bias=1.0)
        ts("dve", DV[:, 24:32], DV[:, 24:32], -4.0, None, ALU.mult, None, [DVb], [DVb])
        ts("dve", DV[:, 32:40], DV[:, 24:32], 2.0, None, ALU.mult, None, [DVb], [DVb])
        GQ8, GK = DV[:, 0:1], PV[:, 1:2]
        si = 0
        for kc in range(8):
            s_ = si % 2; si += 1
            dma(stg[s_][:, 0:1792], win_d[kc * 128:(kc + 1) * 128, :], "stg%d" % s_, writes=stgb[s_])
            ts("dve" if kc % 2 == 0 else "pool", WIN[:, kc, :], stg[s_][:, 0:1792], GV[:, kc:kc + 1], None, ALU.mult, None,
               stgb[s_] + [GVb], [WINb])
        for kc in range(8):
            s_ = si % 2; si += 1
            dma(stg[s_][:, 0:1024], wout_d[kc * 128:(kc + 1) * 128, :], "stg%d" % s_, writes=stgb[s_])
            ts("dve" if kc % 2 == 0 else "pool", WOUT[:, kc, :], stg[s_][:, 0:1024], GV[:, 8 + kc:9 + kc], None, ALU.mult, None,
               stgb[s_] + [GVb], [WOUTb])
        WGSb = Buf("wgu_s")
        WDSb = Buf("wdn_s")

        def prep_ffn():
            stq = [QT[q][:].rearrange("p a b -> p (a b)").bitcast(F32) for q in range(2)]
            stqb = [QTb[0], QTb[1]]
            u = 0
            for j in range(22):
                b_ = j % 2
                for h in range(2):
                    q_ = u % 2; u += 1
                    dma(stq[q_][:, :], wgu_d[j][:, h * 1024:(h + 1) * 1024], "stq%d" % q_, writes=stqb[q_])
                    tt("dve", WGUB[b_][:, h * 1024:(h + 1) * 1024].rearrange("p (k c) -> p k c", k=4),
                       stq[q_][:, :].rearrange("p (k c) -> p k c", k=4),
                       bcast_last(GV[:, 16 + 4 * h:20 + 4 * h], 256), ALU.mult, stqb[q_] + [GVb], [WGUBb[b_]])
                    yield
                dma(wgu_s[j], WGUB[b_][:], "wgs%d" % b_, reads=[WGUBb[b_]], writes=[WGSb])
            for j in range(22):
                b_ = j % 2
                q_ = u % 2; u += 1
                dma(stq[q_][:, :], wdn_d[j], "stq%d" % q_, writes=stqb[q_])
                cp("pool", WDC[b_][:, 0:1024], stq[q_][:, :], stqb[q_], [WDCb[b_]])
                dma(wdn_s.rearrange("c p j e -> p c j e")[:, :, j, :], WDC[b_][:, 0:1024].rearrange("p (c e) -> p c e", c=8),
                    "wds%d" % b_, reads=[WDCb[b_]], writes=[WDSb])
                yield

        prep_gen = prep_ffn()

        smi = [0]

        def small():
            i = smi[0] % 7
            smi[0] += 1
            return SM[:, i, :], SMb[i]

        xni = [0]

        def norm_transpose(xtile, xbuf, np_, ti, bank):
            sm, smb = small()
            xn = xni[0] % 2
            xni[0] += 1
            act(XN[xn][:np_, :], xtile, AF.Square, [xbuf], [XNb[xn], smb], accum_out=sm[:np_, 0:1])
            act(sm[:np_, 1:2], sm[:np_, 0:1], AF.Ln, [smb], [smb], scale=1.0 / 1024, bias=EPS)
            act(sm[:np_, 2:3], sm[:np_, 1:2], AF.Exp, [smb], [smb], scale=-0.5)
            ts("dve", XN[xn][:np_, :], xtile, sm[:np_, 2:3], None, ALU.mult, None, [xbuf, smb], [XNb[xn]])
            tps = BK[bank].bitcast(BF16)
            for kc in range(8):
                P.op("pe", lambda e, o=tps[:, kc * 128:kc * 128 + np_], i=XN[xn][:np_, kc * 128:(kc + 1) * 128],
                     d=IDENT[:np_, :np_]: e.transpose(out=o, in_=i, identity=d), [XNb[xn], CMb], [BKb[bank]])
            cp("dve", XT[:, :, ti * 128:ti * 128 + np_],
               tps.rearrange("p (k c) -> p k c", k=8)[:, :, 0:np_], [BKb[bank]], [XTb[ti]])

        def proj_fm(c0, n, bank, ntile):
            for kc in range(8):
                mm(BK[bank][:, 0:n], WIN[:, kc, c0:c0 + 128], XT[:, kc, 0:n], kc == 0, kc == 7,
                   [WINb] + XTb[:ntile], [BKb[bank]])

        def proj_fm_g(c0, n, bank, ntile):
            for kc in range(8):
                mm(BK[bank][:, 0:n], WIN[:, kc, c0:c0 + 128], XT[:, kc, 0:n], kc == 0, kc == 7,
                   [WINb] + XTb[:ntile], [BKb[bank]])
                if kc == 3:
                    yield

        def qk_rope(bank, n, gvec, gbufs, dst, dstb, b2, b3):
            act(SQb_t[:, 0:n], BK[bank][:, 0:n], AF.Square, [BKb[bank]], [SQbb])
            act(QGb_t[:, 0:n], BK[bank][:, 0:n], AF.Identity, [BKb[bank]] + gbufs, [QGbb], scale=gvec)
            mm(BK[b2][:, 0:n], BONES, SQb_t[:, 0:n], True, True, [CMb, SQbb], [BKb[b2]])
            mm(BK[b3][:, 0:n], PERM, QGb_t[:, 0:n], True, True, [CMb, QGbb], [BKb[b3]])
            act(RS[:, 0:n], BK[b2][:, 0:n], AF.Ln, [BKb[b2]], [RSb], scale=1.0 / 64, bias=EPS)
            act(RS[:, 0:n], RS[:, 0:n], AF.Exp, [RSb], [RSb], scale=-0.5)
            tt("dve", TA[:, 0:n], QGb_t[:, 0:n], ROC[:, 0:n], ALU.mult, [QGbb, ROCb], [TAb])
            TB_, TBb_ = f32t(10), f32b(10)
            tt("dve", TB_[:, 0:n], BK[b3][:, 0:n], ROS[:, 0:n], ALU.mult, [BKb[b3], ROSb], TBb_)
            tt("dve", TA[:, 0:n], TA[:, 0:n], TB_[:, 0:n], ALU.add, [TAb] + TBb_, [TAb])
            tt("dve", dst, TA[:, 0:n], RS[:, 0:n], ALU.mult, [TAb, RSb], dstb)

        def cwb_of(cwslot):
            if cwslot == 22:
                return SCR[:, 22, :], [SCRb[22]]
            if cwslot == -1:
                return PT[0][:, 0:512], [PTb[0]]
            if cwslot == -2:
                return PT[1][:, 0:512], [PTb[1]]
            return PT[1][:, 512:1024], [PTc]

        def cw_of(cw):
            if cw < 100:
                return f32t(cw), f32b(cw)
            h = cw - 100
            return (CATL[0][:, 2 * h:2 * h + 2, :].rearrange("p a b -> p (a b)").bitcast(F32),
                    [CATLb[0][2 * h], CATLb[0][2 * h + 1]])

        def conv_chunk(c, n, cw=8, cwslot=22):
            CW, CWb_ = cw_of(cw)
            ts("dve", CW[:, 0:n], LB[:, c, 0:n], PV[:, 2 + c:3 + c], PV[:, 18 + c:19 + c], ALU.mult, ALU.add,
               [LBb, PVb], CWb_)
            for j in range(1, 4):
                stt(CW[:, 0:n], LB[:, c, j:j + n], PV[:, 2 + j * 4 + c:3 + j * 4 + c], CW[:, 0:n], ALU.mult, ALU.add,
                    [LBb, PVb] + CWb_, CWb_)
            cwb_ap, cwb_b = cwb_of(cwslot)
            cp("pool", cwb_ap[:, 0:n], CW[:, 0:n], CWb_, cwb_b)

        def gates(c, d, n, tset=None, cw=8, cwslot=22):
            o = (0 if d == 0 else 4) if tset is None else tset
            T0, T1, T2 = f32t(o), f32t(o + 1), f32t(o + 2)
            T0b, T1b, T2b = f32b(o), f32b(o + 1), f32b(o + 2)
            CW, CWb_ = cw_of(cw)
            cwb_ap, cwb_b = cwb_of(cwslot)
            ix = d * 4 + c
            ia, ixx = (d * 2 + 0) * 4 + c, (d * 2 + 1) * 4 + c
            TH = SCR[:, 2 * o:2 * o + 4, :].rearrange("p a b -> p (a b)").bitcast(F32).rearrange("p (t m) -> p t m", t=2)
            PS2 = PSt[3].rearrange("p (t m) -> p t m", t=2)

            def stage1():
                mm(BK[6][:, 0:n], WG[:, ia, :], cwb_ap[:, 0:n], True, False, [WGb] + cwb_b, [BKb[6]])
                mm(BK[6][:, 0:n], BROW[:, :], bcast_last(IDENT[0:16, ia:ia + 1], n)[:, 0, :], False, True, [BROWb, CMb], [BKb[6]])
                mm(BK[7][:, 0:n], WG[:, ixx, :], cwb_ap[:, 0:n], True, False, [WGb] + cwb_b, [BKb[7]])
                mm(BK[7][:, 0:n], BROW[:, :], bcast_last(IDENT[0:16, ixx:ixx + 1], n)[:, 0, :], False, True, [BROWb, CMb], [BKb[7]])
                act(TH[:, :, 0:n], PS2[:, :, 0:n], AF.Tanh, [BKb[6], BKb[7]], T0b + T1b, scale=0.5)

            steps = [
                lambda: act(T2[:, 0:n], T0[:, 0:n], AF.Exp, T0b + [DVb], T2b, scale=DV[:, 24 + ix:25 + ix], bias=DV[:, 24 + ix:25 + ix]),
                lambda: act(T0[:, 0:n], T0[:, 0:n], AF.Exp, T0b + [DVb], T0b, scale=DV[:, 32 + ix:33 + ix], bias=DV[:, 32 + ix:33 + ix]),
                lambda: stt(T1[:, 0:n], T1[:, 0:n], 1.0, CW[:, 0:n], ALU.add, ALU.mult, T1b + CWb_, T1b),
                lambda: act(T0[:, 0:n], T0[:, 0:n], AF.Ln, T0b, T0b, scale=-1.0, bias=1.0 + 2.0 ** -23),
                lambda: act(T0[:, 0:n], T0[:, 0:n], AF.Exp, T0b, T0b, scale=0.5),
                lambda: stt(T1[:, 0:n], T0[:, 0:n], 0.5, T1[:, 0:n], ALU.mult, ALU.mult, T0b + T1b, T1b),
            ]
            return stage1, steps, (T1, T1b, T2, T2b)

        def scan(out, a, u, init, reads, writes):
            P.op("dve", lambda e, o=out, x=a, y=u, i=init: e.tensor_tensor_scan(out=o, data0=x, data1=y, initial=i,
                                                                               op0=ALU.mult, op1=ALU.add), reads, writes)

        def load_x(src, k):
            if k == 0:
                dma(XB[0:16, 0, :], meta, "xb0", writes=[XBb[0]])
            else:
                for i in range(4):
                    r0 = (k - 1) * 512 + i * 128
                    dma(XB[:, i, :], src[r0:r0 + 128, :], "xb%d" % i, writes=[XBb[i]])

        def load_block(src, k, with_rope=True, with_x=True, tabs=None, local=False):
            if with_x:
                load_x(src, k)
            if k == 0:
                n, s0 = 16, 0
            else:
                n, s0 = 512, 16 + (k - 1) * 512
            if with_rope:
                tc_, ts_ = tabs if tabs is not None else (ropec_d, ropes_d)
                t0 = (k - 1) * 512 if local else s0
                dma(ROC[:, 0:n], tc_[:, t0:t0 + n], "roc", writes=[ROCb])
                dma(ROS[:, 0:n], ts_[:, t0:t0 + n], "ros", writes=[ROSb])
            return n, s0

        for sl in range(NSLOT):
            xsrc = xs0 if sl == 0 else xs1
            local = (sl == 1)
            X2 = xloc if local else xs0
            Y2 = y1_d if local else y0_d
            TAB2 = (ropec1_d, ropes1_d) if local else (ropec_d, ropes_d)
            KLAST = H if local else NBLK
            mset("dve", CARB[:], 0.0, [CARBb])
            mset("dve", RST[:, :, NBLK + 1:NBLK + 2], 0.0, [RSTb])
            load_x(xsrc, NBLK)
            CWC = [(8, 22), (9, -1), (100, -2), (101, -3)]

            def p1_AB(k):
                n, s0 = load_block(xsrc, k, with_rope=False, with_x=False)
                ntile = 1 if k == 0 else 4
                for i in range(ntile):
                    norm_transpose(XB[:min(n, 128), i, :], XBb[i], min(n, 128), i, 4 + (i % 2))
                    yield
                if k >= 1:
                    load_x(xsrc, k - 1)
                dma(ROC[:, 0:n], ropec_d[:, s0:s0 + n], "roc", writes=[ROCb])
                dma(ROS[:, 0:n], ropes_d[:, s0:s0 + n], "ros", writes=[ROSb])
                if k >= 1:
                    if k == NBLK:
                        mset("dve", LB[:, :, 512:515], 0.0, [LBb])
                    else:
                        cp("dve", LB[:, :, 512:515], LB[:, :, 0:3], [LBb], [LBb])
                    for c in range(4):
                        proj_fm(768 + c * 128, 512, 5, 4)
                        act(LB[:, c, 0:512], BK[5][:, 0:512], AF.Copy, [BKb[5]], [LBb])
                        yield
                    cp("dve", RST[:, :, k:k + 1], LB[:, :, 0:1], [LBb], [RSTb])
                proj_fm(512, n, 0, ntile)
                yield
                qk_rope(0, n, GK, [PVb], KT[:, s0:s0 + n], [KTb], 1, 2)
                yield
                for i in range(ntile):
                    np_ = min(n, 128)
                    j = 0 if k == 0 else 1 + (k - 1) * 4 + i
                    bnk = 3 if i % 2 == 0 else 2
                    for kc in range(8):
                        mm(BK[bnk][:np_, 0:128], XT[:, kc, i * 128:i * 128 + np_], WIN[:, kc, 640:768], kc == 0, kc == 7,
                           [WINb, XTb[i]], [BKb[bnk]])
                    cp("dve", VT[:np_, j, 0:64], BK[bnk][:np_, 0:64], [BKb[bnk]], [VTb])
                    cp("dve", VT[:np_, j, 128:192], BK[bnk][:np_, 64:128], [BKb[bnk]], [VTb])
                    yield

            def p1_C(k):
                for c in range(4):
                    conv_chunk(c, 512, CWC[c][0], CWC[c][1])
                    yield
                for c0 in (0, 2):
                    cfg = [(c0, 4, 7), (c0 + 1, 0, 3)]
                    gs = []
                    for (c, tset, hb) in cfg:
                        st1, steps, res_ = gates(c, 1, 512, tset, CWC[c][0], CWC[c][1])
                        st1()
                        gs.append((steps, res_))
                        yield
                    for f_, g_ in zip(gs[0][0], gs[1][0]):
                        f_()
                        g_()
                        yield
                    for (c, tset, hb), (steps, (U, Ub, A, Ab)) in zip(cfg, gs):
                        if k == NBLK:
                            mset("dve", U[:, 510:512], 0.0, Ub)
                        HB, HBb = f32t(hb), f32b(hb)
                        scan(rev(HB[:, 0:512]), rev(A[:, 0:512]), rev(U[:, 0:512]), CARB[:, c:c + 1], Ab + Ub + [CARBb], HBb)
                        cp("dve", CARB[:, c:c + 1], HB[:, 0:1], HBb, [CARBb])
                        cp("dve", SST[:, c, k:k + 1], HB[:, 510:511], HBb, [SSTb])
                    yield

            for _ in p1_AB(NBLK):
                pass
            for k in range(NBLK, 0, -1):
                g1, g2 = p1_C(k), p1_AB(k - 1)
                a1 = a2 = True
                while a1 or a2:
                    if a1:
                        a1 = next(g1, "STOP") != "STOP"
                    if a2:
                        a2 = next(g2, "STOP") != "STOP"
                if sl == 0:
                    for _ in range(4):
                        next(prep_gen, None)
            dump("KT", KT[:], [KTb])
            dump("VT", VT[:].rearrange("p a b -> p (a b)"), [VTb])
            dump("SST", SST[:].rearrange("p a b -> p (a b)"), [SSTb])
            dump("RST", RST[:].rearrange("p a b -> p (a b)"), [RSTb])

            if sl == 0:
                for _ in prep_gen:
                    pass
            mset("dve", CARF[:], 0.0, [CARFb])

            def genA(k, lite=False):
                if lite:
                    n, s0 = load_block(xsrc, k, with_rope=False, with_x=True)
                else:
                    n, s0 = load_block(X2, k, with_rope=(k > 0), with_x=(k < 3), tabs=TAB2, local=local)
                ntile = 1 if k == 0 else 4
                nprev = 0 if k == 0 else (16 if k == 1 else 512)
                qp = k % 2
                for i in range(ntile):
                    norm_transpose(XB[:min(n, 128), i, :], XBb[i], min(n, 128), i, 6 + (i % 2))
                    yield
                if k == 0:
                    mset("dve", LB[:, :, 0:2], 0.0, [LBb])
                elif local and not lite and k == 1:
                    pass
                else:
                    cp("dve", LB[:, :, 0:2], LB[:, :, nprev:nprev + 2], [LBb], [LBb])
                for c in range(4):
                    bk = 6 + (c % 2)
                    yield from proj_fm_g(768 + c * 128, n, bk, ntile)
                    yield
                    act(LB[:, c, 2:2 + n], BK[bk][:, 0:n], AF.Copy, [BKb[bk]], [LBb])
                if local and not lite:
                    cp("dve", LB[:, :, 2 + n:3 + n], RSTL[:, :, k - 1:k], [RSTLb, LBb], [LBb])
                else:
                    cp("dve", LB[:, :, 2 + n:3 + n], RST[:, :, k + 1:k + 2], [RSTb, LBb], [LBb])
                for c in range(4):
                    conv_chunk(c, n)
                    yield
                    if k > 0 and not lite:
                        yield from proj_fm_g(c * 128, n, 6, ntile)
                        yield
                        qk_rope(6, n, GQ8, [DVb], QT[qp][:, c, 0:n], [QTb[qp][c]], 7, 6)
                        yield
                    HF, HFb = f32t(3), f32b(3)
                    HB, HBb = f32t(7), f32b(7)
                    GG, GGb = f32t(9), f32b(9)
                    GT, GTb = f32t(10), f32b(10)
                    s1f, stf, (U, Ub, A, Ab) = gates(c, 0, n)
                    s1f()
                    yield
                    if k == 0 or lite:
                        for f_ in stf:
                            f_()
                        scan(HF[:, 0:n], A[:, 0:n], U[:, 0:n], CARF[:, c:c + 1], Ab + Ub + [CARFb], HFb)
                        cp("dve", CARF[:, c:c + 1], HF[:, n - 1:n], HFb, [CARFb])
                        yield
                        continue
                    s1b, stb, (U2, U2b, A2, A2b) = gates(c, 1, n)
                    s1b()
                    yield
                    for f_, g_ in zip(stf, stb):
                        f_()
                        g_()
                    yield
                    scan(HF[:, 0:n], A[:, 0:n], U[:, 0:n], CARF[:, c:c + 1], Ab + Ub + [CARFb], HFb)
                    cp("dve", CARF[:, c:c + 1], HF[:, n - 1:n], HFb, [CARFb])
                    sst_ap, sst_b = (SSTL[:, c, k - 1:k], SSTLb) if local else (SST[:, c, k:k + 1], SSTb)
                    scan(rev(HB[:, 0:n]), rev(A2[:, 0:n]), rev(U2[:, 0:n]), sst_ap, A2b + U2b + [sst_b], HBb)
                    yield from proj_fm_g(1280 + c * 128, n, 6, ntile)
                    yield
                    act(GT[:, 0:n], BK[6][:, 0:n], AF.Square, [BKb[6]], GTb)
                    tt("pool", HF[:, 0:n], HF[:, 0:n], HB[:, 0:n], ALU.add, HFb + HBb, HFb)
                    ts("dve", GT[:, 0:n], GT[:, 0:n], 0.044715, 1.0, ALU.mult, ALU.add, GTb, GTb)
                    tt("dve", GT[:, 0:n], GT[:, 0:n], BK[6][:, 0:n], ALU.mult, GTb + [BKb[6]], GTb)
                    act(GT[:, 0:n], GT[:, 0:n], AF.Tanh, GTb, GTb, scale=0.7978845608028654)
                    stt(GG[:, 0:n], GT[:, 0:n], 1.0, BK[6][:, 0:n], ALU.add, ALU.mult, GTb + [BKb[6]], GGb)
                    stt(CATL[qp][:, c, 0:n], GG[:, 0:n], 0.5, HF[:, 0:n], ALU.mult, ALU.mult, GGb + HFb, [CATLb[qp][c]])
                    yield

            def genLite(k):
                n, s0 = load_block(xsrc, k, with_rope=False, with_x=(k == 0))
                ntile = 1 if k == 0 else 4
                nprev = 0 if k == 0 else (16 if k == 1 else 512)
                for i in range(ntile):
                    norm_transpose(XB[:min(n, 128), i, :], XBb[i], min(n, 128), i, 4 + (i % 2))
                if k + 1 <= H:
                    load_x(xsrc, k + 1)
                if k == 0:
                    mset("dve", LB[:, :, 0:2], 0.0, [LBb])
                else:
                    cp("dve", LB[:, :, 0:2], LB[:, :, nprev:nprev + 2], [LBb], [LBb])
                for c in range(4):
                    bk = 2 + (c % 2)
                    proj_fm(768 + c * 128, n, bk, ntile)
                    act(LB[:, c, 2:2 + n], BK[bk][:, 0:n], AF.Copy, [BKb[bk]], [LBb])
                cp("dve", LB[:, :, 2 + n:3 + n], RST[:, :, k + 1:k + 2], [RSTb, LBb], [LBb])
                for c0 in (0, 2):
                    cfg = [(c0, 0, 8, 22, 3), (c0 + 1, 4, 9, -1, 7)]
                    for (c, tset, cw, cws, hf) in cfg:
                        conv_chunk(c, n, cw, cws)
                    gs = []
                    for (c, tset, cw, cws, hf) in cfg:
                        st1, steps, res_ = gates(c, 0, n, tset, cw, cws)
                        st1()
                        gs.append((steps, res_))
                    for f_, g_ in zip(gs[0][0], gs[1][0]):
                        f_()
                        g_()
                    for (c, tset, cw, cws, hf), (steps, (U, Ub, A, Ab)) in zip(cfg, gs):
                        HF, HFb = f32t(hf), f32b(hf)
                        scan(HF[:, 0:n], A[:, 0:n], U[:, 0:n], CARF[:, c:c + 1], Ab + Ub + [CARFb], HFb)
                        cp("dve", CARF[:, c:c + 1], HF[:, n - 1:n], HFb, [CARFb])
                yield

            def genATT(k):
                qp = k % 2
                for c in range(4):
                    def qk(j):
                        kp = 16 if j == 0 else 128
                        c0 = 0 if j == 0 else 16 + (j - 1) * 128
                        par = j % 2
                        S = PSt[par]
                        mm(S[:kp, 0:512], KT[0:64, c0:c0 + kp], QT[qp][0:64, c, :], True, True, [KTb, QTb[qp][c]],
                           [BKb[2 * par], BKb[2 * par + 1]])
                        mm(S[:kp, 512:1024], KT[64:128, c0:c0 + kp], QT[qp][64:128, c, :], True, True, [KTb, QTb[qp][c]],
                           [BKb[2 * par], BKb[2 * par + 1]])

                    def ex(j):
                        kp = 16 if j == 0 else 128
                        par = j % 2
                        S = PSt[par]
                        act(PT[par][:kp, :], S[:kp, :], AF.Exp, [BKb[2 * par], BKb[2 * par + 1]], [PTb[par]])

                    def pv(j):
                        kp = 16 if j == 0 else 128
                        par = j % 2
                        st_, sp_ = (j == 0), (j == NKT - 1)
                        mm(BK[4], VT[:kp, j, 0:128], PT[par][:kp, 0:512], st_, sp_, [VTb, PTb[par]], [BKb[4]])
                        mm(BK[5], VT[:kp, j, 64:192], PT[par][:kp, 512:1024], st_, sp_, [VTb, PTb[par]], [BKb[5]])

                    qk(0)
                    for j in range(NKT):
                        if j + 1 < NKT:
                            qk(j + 1)
                        ex(j)
                        if j >= 1:
                            pv(j - 1)
                        yield
                    pv(NKT - 1)
                    P.op("dve", lambda e: e.reciprocal(out=RD[0:64, :], in_=BK[4][64:128, :]), [BKb[4]], [RDb])
                    P.op("dve", lambda e: e.reciprocal(out=RD[64:128, :], in_=BK[5][0:64, :]), [BKb[5]], [RDb])
                    tt("dve", CAT[0:64, c, :], BK[4][0:64, :], RD[0:64, :], ALU.mult, [BKb[4], RDb], [CATb[c]])
                    tt("dve", CAT[64:128, c, :], BK[5][64:128, :], RD[64:128, :], ALU.mult, [BKb[5], RDb], [CATb[c]])

            def genC(k):
                qp = k % 2
                for i in range(4):
                    r0 = (k - 1) * 512 + i * 128
                    dma(XB[:, i, :], X2[r0:r0 + 128, :], "xb%d" % i, writes=[XBb[i]])

                def catc(ch):
                    return (CAT[:, ch, :], CATb[ch]) if ch < 4 else (CATL[qp][:, ch - 4, :], CATLb[qp][ch - 4])
                for i in range(4):
                    for g in range(2):
                        bnk = 6 + g
                        for cc in range(4):
                            ct, cb = catc(g * 4 + cc)
                            mm(BK[bnk][:, 0:128], ct[:, i * 128:(i + 1) * 128], ct[:, i * 128:(i + 1) * 128],
                               cc == 0, cc == 3, [cb], [BKb[bnk]])
                        tmp, tmpb = (TA, TAb) if g == 0 else (RS, RSb)
                        tt("dve", tmp[:, 0:128], BK[bnk][:, 0:128], IDF[:], ALU.mult, [BKb[bnk], IDFb], [tmpb])
                        P.op("dve", lambda e, o=SSN[:, i * 2 + g:i * 2 + g + 1], t_=tmp: e.reduce_sum(out=o, in_=t_[:, 0:128], axis=AX.X),
                             [tmpb], [SSNb])
                    yield
                act(RSN[:, 0:8], SSN[:, 0:8], AF.Ln, [SSNb], [RSNb], scale=1.0 / 512, bias=EPS)
                act(RSN[:, 0:8], RSN[:, 0:8], AF.Exp, [RSNb], [RSNb], scale=-0.5)
                for i in range(4):
                    for hc in range(2):
                        for cc in range(4):
                            ct, cb = catc(cc)
                            mm(BK[6], ct[:, i * 128:(i + 1) * 128], WOUT[:, cc, hc * 512:(hc + 1) * 512], cc == 0, cc == 3,
                               [cb, WOUTb], [BKb[6]])
                        yield
                        for cc in range(4, 8):
                            ct, cb = catc(cc)
                            mm(BK[7], ct[:, i * 128:(i + 1) * 128], WOUT[:, cc, hc * 512:(hc + 1) * 512], cc == 4, cc == 7,
                               [cb, WOUTb], [BKb[7]])
                        xh = XB[:, i, hc * 512:(hc + 1) * 512]
                        stt(xh, BK[6], RSN[:, 2 * i:2 * i + 1], xh, ALU.mult, ALU.add, [BKb[6], RSNb, XBb[i]], [XBb[i]])
                        stt(xh, BK[7], RSN[:, 2 * i + 1:2 * i + 2], xh, ALU.mult, ALU.add, [BKb[7], RSNb, XBb[i]], [XBb[i]])
                        yield
                for i in range(4):
                    norm_transpose(XB[:, i, :], XBb[i], 128, i, 6 + (i % 2))
                    yield
                for j in range(22):
                    b_ = j % 2
                    dma(WGUB[b_][:], wgu_s[j], "wgu%d" % b_, reads=[WGSb], writes=[WGUBb[b_]])
                    for kc in range(8):
                        mm(BK[6], WGUB[b_][:, kc * 256:kc * 256 + 128], XT[:, kc, :], kc == 0, kc == 7,
                           [WGUBb[b_]] + XTb, [BKb[6]])
                        if kc % 4 == 3:
                            yield
                    for kc in range(8):
                        mm(BK[7], WGUB[b_][:, kc * 256 + 128:kc * 256 + 256], XT[:, kc, :], kc == 0, kc == 7,
                           [WGUBb[b_]] + XTb, [BKb[7]])
                        if kc == 3:
                            yield
                    tmp, tmpb = (TA, TAb) if b_ == 0 else (RS, RSb)
                    act(tmp[:], BK[6], AF.Tanh, [BKb[6]], [tmpb], scale=0.5)
                    stt(tmp[:], tmp[:], 1.0, BK[6], ALU.add, ALU.mult, [tmpb, BKb[6]], [tmpb])
                    stt(SCR[:, j, :], tmp[:], 0.5, BK[7], ALU.mult, ALU.mult, [tmpb, BKb[7]], [SCRb[j]])
                    yield
                for cc in range(8):
                    b_ = cc % 2
                    dma(WDC[b_][:], wdn_s[cc].rearrange("p j e -> p (j e)"), "wdn%d" % b_, reads=[WDSb], writes=[WDCb[b_]])
                    for j in range(22):
                        mm(BK[6], WDC[b_][:, j * 128:(j + 1) * 128], SCR[:, j, :], j == 0, j == 21, [WDCb[b_], SCRb[j]], [BKb[6]])
                        if j % 4 == 3:
                            yield
                    act(RS[:], BK[6], AF.Copy, [BKb[6]], [RSb])
                    for i in range(4):
                        P.op("pe", lambda e, o=BK[7][:, i * 128:(i + 1) * 128], i_=RS[:, i * 128:(i + 1) * 128]:
                             e.transpose(out=o, in_=i_, identity=IDF[:]), [RSb, IDFb], [BKb[7]])
                    tt("dve", XB[:, :, cc * 128:(cc + 1) * 128], BK[7].rearrange("p (i e) -> p i e", i=4),
                       XB[:, :, cc * 128:(cc + 1) * 128], ALU.add, [BKb[7]] + XBb, XBb)
                    yield
                for i in range(4):
                    sm, smb = small()
                    xn = xni[0] % 2
                    xni[0] += 1
                    act(XN[xn][:], XB[:, i, :], AF.Square, [XBb[i]], [XNb[xn], smb], accum_out=sm[:, 0:1])
                    act(sm[:, 1:2], sm[:, 0:1], AF.Ln, [smb], [smb], scale=1.0 / 1024, bias=EPS)
                    act(sm[:, 2:3], sm[:, 1:2], AF.Exp, [smb], [smb], scale=-0.5)
                    stt(XB[:, i, :], XB[:, i, :], sm[:, 2:3], GF[:], ALU.mult, ALU.mult, [XBb[i], smb, GFb], [XBb[i]])
                    r0 = (k - 1) * 512 + i * 128
                    dma(Y2[r0:r0 + 128, :], XB[:, i, :], "y%d" % i, reads=[XBb[i]])
                    if k + 2 <= KLAST:
                        r2 = (k + 1) * 512 + i * 128
                        dma(XB[:, i, :], X2[r2:r2 + 128, :], "xb%d" % i, writes=[XBb[i]])
                    yield

            def lane(w):
                if w - 1 >= 1:
                    yield from genC(w - 1)
                if w + 1 <= KLAST:
                    yield from genA(w + 1)

            def drain(g):
                for _ in g:
                    pass

            def count(g):
                P.dry = True
                n_ = sum(1 for _ in g)
                P.dry = False
                return n_

            if local:
                for k in range(0, H + 1):
                    drain(genLite(k))
                    if k == 0:
                        cp("dve", FS0[:], CARF[:], [CARFb], [FSb])
                        cp("dve", LH0[:], LB[:, :, 16:18], [LBb], [LHb])
                    if k == H:
                        cp("dve", FS8[:], CARF[:], [CARFb], [FSb])
                        cp("dve", LH8[:], LB[:, :, 512:514], [LBb], [LHb])
                sa, sb_ = SELV[:, 0:1], SELV[:, 1:2]
                ts("dve", CARF[:], FS0[:], sa, None, ALU.mult, None, [FSb, SELVb], [CARFb])
                stt(CARF[:], FS8[:], sb_, CARF[:], ALU.mult, ALU.add, [FSb, SELVb, CARFb], [CARFb])
                ts("dve", LB[:, :, 0:2], LH0[:], sa, None, ALU.mult, None, [LHb, SELVb], [LBb])
                stt(LB[:, :, 0:2], LH8[:], sb_, LB[:, :, 0:2], ALU.mult, ALU.add, [LHb, SELVb, LBb], [LBb])
                ts("dve", SSTL[:], SST[:, :, 1:1 + H], sa, None, ALU.mult, None, [SSTb, SELVb], [SSTLb])
                stt(SSTL[:], SST[:, :, 1 + H:1 + 2 * H], sb_, SSTL[:], ALU.mult, ALU.add, [SSTb, SELVb, SSTLb], [SSTLb])
                ts("dve", RSTL[:], RST[:, :, 2:2 + H], sa, None, ALU.mult, None, [RSTb, SELVb], [RSTLb])
                stt(RSTL[:], RST[:, :, 2 + H:2 + 2 * H], sb_, RSTL[:], ALU.mult, ALU.add, [RSTb, SELVb, RSTLb], [RSTLb])
            else:
                drain(genA(0))
            drain(genA(1))
            for w in range(1, KLAST + 1):
                nl = count(lane(w))
                na = 4 * NKT
                ln = lane(w)
                done = 0
                for step, _ in enumerate(genATT(w), 1):
                    tgt = (step * nl) // na
                    while done < tgt:
                        next(ln, None)
                        done += 1
                drain(ln)
            drain(genC(KLAST))

        P.emit(st, final_slots=["y0", "y1", "y2", "y3"] + ["dbg%d" % (i + 1) for i in range(dbg_n[0])])
    return nc, dbg_out


def rope_tables(L):
    f32 = np.float32
    t = np.arange(L - 16)
    row = (t // 64).astype(f32)
    col = (t % 64).astype(f32)
    freqs = (f32(10000.0) ** (-np.arange(0, 32, 2, dtype=f32) / f32(32))).astype(f32)
    C = np.ones((128, L), f32)
    S = np.zeros((128, L), f32)
    for p in range(128):
        dm = p % 64
        pos = row if dm // 32 == 0 else col
        ang = (pos * freqs[dm % 16]).astype(f32)
        C[p, 16:] = np.cos(ang)
        S[p, 16:] = np.sin(ang)
    return C, S


def host_layout(inp):
    f32 = np.float32
    w_in = np.asarray(inp["w_in"])[0]
    qcols = np.array([(c + 4 * h) * 64 + d for c in range(4) for h in range(2) for d in range(64)])
    win = np.ascontiguousarray(np.concatenate([w_in[:, qcols], w_in[:, 512:]], axis=1), dtype=f32)
    rowmap = np.array([(kc + 4 * (p // 64)) * 64 + p % 64 if kc < 4 else 512 + (kc - 4) * 128 + p
                       for kc in range(8) for p in range(128)])
    w_out = np.asarray(inp["w_out"])[0]
    wout = np.ascontiguousarray(w_out[rowmap, :], dtype=f32)
    gcat = np.concatenate([np.asarray(inp["attn_out_g"])[0], np.asarray(inp["lru_out_g"])[0]])[rowmap]
    pk = lambda v: np.asarray(v, f32).reshape(8, 128).T
    gv = np.ascontiguousarray(np.concatenate([pk(np.asarray(inp["norm_mix_g"])[0]), pk(gcat),
                                              pk(np.asarray(inp["norm_ffn_g"])[0])], axis=1), dtype=f32)
    wgu_o = np.asarray(inp["w_gate_up"])[0]
    g4 = wgu_o[:, :2816].reshape(8, 128, 22, 128)
    u4 = wgu_o[:, 2816:].reshape(8, 128, 22, 128)
    wgu = np.stack([g4, u4], axis=0).transpose(3, 2, 1, 0, 4)
    wgu = np.ascontiguousarray(wgu.reshape(22, 128, 2048), dtype=f32)
    wdn = np.ascontiguousarray(np.asarray(inp["w_down"])[0].reshape(22, 128, 1024), dtype=f32)
    pv = np.zeros((128, 46), f32)
    p = np.arange(128)
    pv[:, 0] = np.asarray(inp["q_norm_g"])[0][p % 64]
    pv[:, 1] = np.asarray(inp["k_norm_g"])[0][p % 64]
    cw = np.asarray(inp["conv_w"])[0]
    cb = np.asarray(inp["conv_b"])[0]
    for c in range(4):
        for j in range(4):
            pv[:, 2 + j * 4 + c] = cw[j, c * 128 + p]
        pv[:, 18 + c] = cb[c * 128 + p]
        for d in range(2):
            pv[:, 22 + d * 4 + c] = np.asarray(inp["lru_b_a"])[0][d, c * 128 + p]
            pv[:, 30 + d * 4 + c] = np.asarray(inp["lru_b_x"])[0][d, c * 128 + p]
            pv[:, 38 + d * 4 + c] = np.asarray(inp["lru_lam"])[0][d, c * 128 + p]
    wgate = np.zeros((128, 16, 128), f32)
    wa = np.asarray(inp["lru_w_a"])[0]
    wx = np.asarray(inp["lru_w_x"])[0]
    for d in range(2):
        for ax, w in enumerate((wa, wx)):
            for c in range(4):
                for hb in range(2):
                    wgate[hb * 64:(hb + 1) * 64, (d * 2 + ax) * 4 + c, hb * 64:(hb + 1) * 64] = w[d, 2 * c + hb]
    brow = np.zeros((1, 16, 128), f32)
    for d in range(2):
        for ax, bb in enumerate((np.asarray(inp['lru_b_a'])[0], np.asarray(inp['lru_b_x'])[0])):
            for c in range(4):
                brow[0, (d * 2 + ax) * 4 + c, :] = bb[d, c * 128:(c + 1) * 128]
    cmat = np.zeros((128, 3, 128), f32)
    for m in range(128):
        if (m % 64) % 32 < 16:
            cmat[m + 16, 0, m] = -1.0
        else:
            cmat[m - 16, 0, m] = 1.0
        cmat[(m // 64) * 64:(m // 64) * 64 + 64, 1, m] = 1.0
        cmat[m, 2, m] = 1.0
    return dict(win=win, wout=wout, wgu=wgu, wdn=wdn, gv=gv,
                gfin=np.asarray(inp["final_norm_g"], f32).reshape(1, 1024),
                pv=pv, brow=brow.reshape(1, 2048), wgate=wgate.reshape(128, 2048), cmat=cmat.reshape(128, 384),
                meta=np.ascontiguousarray(np.asarray(inp["meta_tokens"], f32)))


_CACHE = {}


def run(cores, shared, NBLK, NSLOT, dbg=None):
    key = (NBLK, NSLOT)
    if key not in _CACHE:
        _CACHE[key] = build(NBLK, NSLOT, dbg)
    nc, dbg_out = _CACHE[key]
    C, S = rope_tables(NBLK * 512 + 16)
    TH = (NBLK // 2) * 512
    in_maps = []
    for cd in cores:
        m = dict(shared)
        m["xs0"] = np.ascontiguousarray(cd["x0"], dtype=np.float32)
        m["ropec"] = C
        m["ropes"] = S
        if NSLOT > 1:
            hh = cd["half"]
            x1 = np.asarray(cd["x1"], dtype=np.float32)
            m["xs1"] = np.ascontiguousarray(x1)
            m["xloc"] = np.ascontiguousarray(x1[hh * TH:(hh + 1) * TH])
            m["ropec1"] = np.ascontiguousarray(C[:, 16 + hh * TH:16 + (hh + 1) * TH])
            m["ropes1"] = np.ascontiguousarray(S[:, 16 + hh * TH:16 + (hh + 1) * TH])
            m["selv"] = np.array([[1.0 - hh, float(hh)]], dtype=np.float32)
        in_maps.append(m)
    res = run_bass_kernel_spmd(nc, in_maps, core_ids=list(range(len(cores))))
    return res.results, dbg_out


def kernel(**inputs):
    xp = np.asarray(inputs["x_prompt"], np.float32)
    xsm = np.asarray(inputs["x_sample"], np.float32)
    shared = host_layout(inputs)
    cores = [dict(x0=xp[ci], x1=xsm[ci // 2], half=ci % 2) for ci in range(8)]
    results, _ = run(cores, shared, 16, 2)
    yp = np.stack([results[ci]["y0"] for ci in range(8)], axis=0)
    ys = np.stack([np.concatenate([results[2 * j]["y1"], results[2 * j + 1]["y1"]], axis=0) for j in range(4)], axis=0)
    return (yp, ys)
```

```python
import numpy as np
from contextlib import ExitStack
import concourse.bass as bass
import concourse.mybir as mybir
from concourse.bass_utils import run_bass_kernel_spmd

F32 = mybir.dt.float32
BF16 = mybir.dt.bfloat16
AF = mybir.ActivationFunctionType
ALU = mybir.AluOpType
AX = mybir.AxisListType

COMPUTE = ("pe", "act", "dve", "pool")
EPS = 1e-6


class Buf:
    __slots__ = ("name", "last_w", "readers")

    def __init__(self, name):
        self.name = name
        self.last_w = None
        self.readers = []


class Op:
    __slots__ = ("eng", "fn", "deps", "sig", "slot", "cnt")


class Prog:
    def __init__(self, nc):
        self.nc = nc
        self.ops = {e: [] for e in COMPUTE + ("sp",)}
        self.slots = {}
        self.all_ops = []
        self.dry = False

    def op(self, eng, fn, reads=(), writes=(), slot=None):
        if self.dry:
            return None
        o = Op()
        o.eng, o.fn, o.sig, o.slot, o.cnt = eng, fn, 0, slot, 0
        deps = set()
        for b in reads:
            if b.last_w is not None:
                deps.add(b.last_w)
        for b in writes:
            if b.last_w is not None:
                deps.add(b.last_w)
            deps.update(b.readers)
        for b in reads:
            b.readers.append(o)
        for b in writes:
            b.last_w = o
            b.readers = []
        if eng == "pe" and slot is None:
            deps = {d for d in deps if not (d.eng == "pe" and d.slot is None)}
        deps.discard(o)
        o.deps = deps
        if slot is not None:
            c = self.slots.get(slot, 0) + 1
            self.slots[slot] = c
            o.cnt = 16 * c
        self.ops[eng].append(o)
        self.all_ops.append(o)
        return o

    def emit(self, stack, final_slots=()):
        nc = self.nc
        for o in self.all_ops:
            for d in o.deps:
                if d.slot is None:
                    d.sig = -1
        sems = {}
        for e in COMPUTE:
            sems[e] = stack.enter_context(nc.semaphore("s_" + e))
            n = 0
            for o in self.ops[e]:
                if o.sig == -1:
                    n += 1
                    o.sig = n
        for k in self.slots:
            sems["d_" + k] = stack.enter_context(nc.semaphore("d_" + k))
        block = stack.enter_context(nc.Block())

        def run(ename, eng):
            known = {}
            for o in self.ops[ename]:
                need = {}
                for d in o.deps:
                    if d.slot is None:
                        k, v = d.eng, d.sig
                    else:
                        k, v = "d_" + d.slot, d.cnt
                    if need.get(k, 0) < v:
                        need[k] = v
                for k, v in need.items():
                    if known.get(k, 0) < v:
                        eng.wait_ge(sems[k], v)
                        known[k] = v
                ins = o.fn(eng)
                if o.slot is not None:
                    ins.then_inc(sems["d_" + o.slot], 16)
                elif o.sig > 0:
                    ins.then_inc(sems[ename], 1)
            if ename == "sp":
                for k in final_slots:
                    eng.wait_ge(sems["d_" + k], 16 * self.slots[k])

        block.tensor(lambda e: run("pe", e))
        block.scalar(lambda e: run("act", e))
        block.vector(lambda e: run("dve", e))
        block.gpsimd(lambda e: run("pool", e))
        block.sync(lambda e: run("sp", e))


def bcast_last(ap, n):
    return bass.AP(ap.tensor, ap.offset, [list(x) for x in ap.ap] + [[0, n]])


def rev(ap):
    n = ap.shape[1]
    return bass.AP(ap.tensor, ap[:, n - 1:n].offset, [list(ap.ap[0]), [-ap.ap[1][0], n]])


def build(NBLK, NSLOT, dbg=None):
    T = NBLK * 512
    L = T + 16
    NKT = 1 + NBLK * 4
    nc = bass.Bass("TRN2", target_bir_lowering=False)

    def din(name, shape, dt=F32):
        return nc.dram_tensor(name, list(shape), dt, kind="ExternalInput").ap()

    H = NBLK // 2
    TH = H * 512
    xs0 = din("xs0", [T, 1024])
    xs1 = din("xs1", [T, 1024]) if NSLOT > 1 else None
    xloc = din("xloc", [TH, 1024]) if NSLOT > 1 else None
    ropec1_d = din("ropec1", [128, TH]) if NSLOT > 1 else None
    ropes1_d = din("ropes1", [128, TH]) if NSLOT > 1 else None
    selv_d = din("selv", [1, 2]) if NSLOT > 1 else None
    meta = din("meta", [16, 1024])
    win_d = din("win", [1024, 1792])
    wout_d = din("wout", [1024, 1024])
    wgu_d = din("wgu", [22, 128, 2048])
    wdn_d = din("wdn", [22, 128, 1024])
    gv_d = din("gv", [128, 24])
    gfin_d = din("gfin", [1, 1024])
    pv_d = din("pv", [128, 46])
    wgate_d = din("wgate", [128, 16 * 128])
    cmat_d = din("cmat", [128, 3 * 128])
    brow_d = din("brow", [1, 16 * 128])
    ropec_d = din("ropec", [128, L])
    ropes_d = din("ropes", [128, L])
    y0_d = nc.dram_tensor("y0", [T, 1024], F32, kind="ExternalOutput").ap()
    y1_d = nc.dram_tensor("y1", [TH, 1024], F32, kind="ExternalOutput").ap() if NSLOT > 1 else None
    wgu_s = nc.dram_tensor("wgu_s", [22, 128, 2048], BF16, kind="Internal").ap()
    wdn_s = nc.dram_tensor("wdn_s", [8, 128, 22, 128], BF16, kind="Internal").ap()
    dbg_out = {}

    P = Prog(nc)
    with ExitStack() as st:
        def sb(name, shape, dt):
            return st.enter_context(nc.sbuf_tensor(name, list(shape), dt))

        WIN = sb("WIN", [128, 8, 1792], BF16); WINb = Buf("WIN")
        WOUT = sb("WOUT", [128, 8, 1024], BF16); WOUTb = Buf("WOUT")
        KT = sb("KT", [128, L], BF16); KTb = Buf("KT")
        VT = sb("VT", [128, NKT, 192], BF16); VTb = Buf("VT")
        XB = sb("XB", [128, 4, 1024], F32); XBb = [Buf("XB%d" % i) for i in range(4)]
        GF = sb("GF", [128, 1024], F32); GFb = Buf("GF")
        XN = [sb("XN%d" % i, [128, 1024], BF16) for i in range(2)]; XNb = [Buf("XN0"), Buf("XN1")]
        XT = sb("XT", [128, 8, 512], BF16); XTb = [Buf("XT%d" % i) for i in range(4)]
        ROC = sb("ROC", [128, 512], F32); ROCb = Buf("ROC")
        ROS = sb("ROS", [128, 512], F32); ROSb = Buf("ROS")
        LB = sb("LB", [128, 4, 516], F32); LBb = Buf("LB")
        SCR = sb("SCR", [128, 23, 512], BF16); SCRb = [Buf("SCR%d" % i) for i in range(23)]
        QT = [sb("QT%d" % q, [128, 4, 512], BF16) for q in range(2)]; QTb = [[Buf("QT%d_%d" % (q, i)) for i in range(4)] for q in range(2)]
        SQb_t = sb("SQb", [128, 512], BF16); SQbb = Buf("SQb")
        QGb_t = sb("QGb", [128, 512], BF16); QGbb = Buf("QGb")
        RS = sb("RS", [128, 512], F32); RSb = Buf("RS")
        TA = sb("TA", [128, 512], F32); TAb = Buf("TA")
        PT = [sb("PT%d" % i, [128, 1024], BF16) for i in range(2)]; PTb = [Buf("PT0"), Buf("PT1")]; PTc = PTb[1]
        RD = TA; RDb = TAb
        CAT = sb("CAT", [128, 4, 512], BF16); CATb = [Buf("CAT%d" % i) for i in range(4)]
        CATL = [sb("CATL%d" % q, [128, 4, 512], BF16) for q in range(2)]; CATLb = [[Buf("CATL%d_%d" % (q, i)) for i in range(4)] for q in range(2)]
        WGUB = [sb("WGUB%d" % i, [128, 2048], BF16) for i in range(2)]; WGUBb = [Buf("WGUB0"), Buf("WGUB1")]
        WDC = [sb("WDC%d" % i, [128, 22 * 128], BF16) for i in range(2)]; WDCb = [Buf("WDC0"), Buf("WDC1")]
        WG = sb("WG", [128, 16, 128], BF16); WGb = Buf("WG")
        CM = sb("CM", [128, 3, 128], BF16); CMb = Buf("CM")
        IDF = sb("IDF", [128, 128], F32); IDFb = Buf("IDF")
        BROW = sb("BROW", [16, 128], BF16); BROWb = Buf("BROW")
        PV = sb("PV", [128, 46], F32); PVb = Buf("PV")
        GV = sb("GV", [128, 24], F32); GVb = Buf("GV")
        DV = sb("DV", [128, 40], F32); DVb = Buf("DV")
        RST = sb("RST", [128, 4, NBLK + 2], F32); RSTb = Buf("RST")
        SST = sb("SST", [128, 4, NBLK + 2], F32); SSTb = Buf("SST")
        CARF = sb("CARF", [128, 4], F32); CARFb = Buf("CARF")
        FS0 = sb("FS0", [128, 4], F32); FS8 = sb("FS8", [128, 4], F32); FSb = Buf("FS")
        LH0 = sb("LH0", [128, 4, 2], F32); LH8 = sb("LH8", [128, 4, 2], F32); LHb = Buf("LH")
        SSTL = sb("SSTL", [128, 4, max(H, 1)], F32); SSTLb = Buf("SSTL")
        RSTL = sb("RSTL", [128, 4, max(H, 1)], F32); RSTLb = Buf("RSTL")
        SELV = sb("SELV", [128, 2], F32); SELVb = Buf("SELV")
        CARB = sb("CARB", [128, 4], F32); CARBb = Buf("CARB")
        SM = sb("SM", [128, 8, 4], F32); SMb = [Buf("SM%d" % i) for i in range(8)]
        SSN = sb("SSN", [128, 16], F32); SSNb = Buf("SSN")
        RSN = sb("RSN", [128, 16], F32); RSNb = Buf("RSN")
        PSt = [st.enter_context(nc.psum_tensor("PS%d" % i, [128, 1024], F32)) for i in range(4)]
        BK = [PSt[i // 2][:, (i % 2) * 512:(i % 2) * 512 + 512] for i in range(8)]
        BKb = [Buf("BK%d" % i) for i in range(8)]

        PERM, BONES, IDENT = CM[:, 0, :], CM[:, 1, :], CM[:, 2, :]

        def f32t(t):
            return SCR[:, 2 * t:2 * t + 2, :].rearrange("p a b -> p (a b)").bitcast(F32)

        def f32b(t):
            return [SCRb[2 * t], SCRb[2 * t + 1]]

        def dma(out, in_, slot, reads=(), writes=()):
            P.op("sp", lambda e, o=out, i=in_: e.dma_start(out=o, in_=i), reads, writes, slot=slot)

        def mm(out, lhsT, rhs, start, stop, reads, writes):
            P.op("pe", lambda e, o=out, l=lhsT, r=rhs, s=start, t=stop: e.matmul(o, lhsT=l, rhs=r, start=s, stop=t),
                 reads, writes)

        def act(out, in_, func, reads, writes, **kw):
            P.op("act", lambda e, o=out, i=in_, f=func, k=kw: e.activation(out=o, in_=i, func=f, **k), reads, writes)

        def tt(eng, out, in0, in1, op, reads, writes):
            P.op(eng, lambda e, o=out, a=in0, b=in1, p=op: e.tensor_tensor(out=o, in0=a, in1=b, op=p), reads, writes)

        def ts(eng, out, in0, s1, s2, op0, op1, reads, writes):
            if op1 is None:
                P.op(eng, lambda e, o=out, a=in0, x=s1, p=op0: e.tensor_scalar(out=o, in0=a, scalar1=x, scalar2=None, op0=p),
                     reads, writes)
            else:
                P.op(eng, lambda e, o=out, a=in0, x=s1, y=s2, p=op0, q=op1:
                     e.tensor_scalar(out=o, in0=a, scalar1=x, scalar2=y, op0=p, op1=q), reads, writes)

        def stt(out, in0, scalar, in1, op0, op1, reads, writes):
            P.op("dve", lambda e, o=out, a=in0, s=scalar, b=in1, p=op0, q=op1:
                 e.scalar_tensor_tensor(out=o, in0=a, scalar=s, in1=b, op0=p, op1=q), reads, writes)

        def cp(eng, out, in_, reads, writes):
            P.op(eng, lambda e, o=out, i=in_: e.tensor_copy(out=o, in_=i), reads, writes)

        def mset(eng, ap, val, writes):
            P.op(eng, lambda e, a=ap, v=val: e.memset(a, v), (), writes)

        dbg_n = [0]

        def dump(name, ap, reads):
            if dbg is None or name not in dbg:
                return
            shp = list(ap.shape)
            d = nc.dram_tensor("dbg_" + name, shp, ap.dtype, kind="ExternalOutput").ap()
            dbg_out[name] = "dbg_" + name
            dbg_n[0] += 1
            dma(d, ap, "dbg%d" % dbg_n[0], reads=reads)

        dma(PV[:], pv_d, "pv", writes=[PVb])
        dma(GV[:], gv_d, "gv", writes=[GVb])
        if NSLOT > 1:
            dma(SELV[:], selv_d.partition_broadcast(128), "selv", writes=[SELVb])
        dma(GF[:], gfin_d.partition_broadcast(128), "gf", writes=[GFb])
        stg = [XB[:, 0:2, :].rearrange("p a b -> p (a b)"), XB[:, 2:4, :].rearrange("p a b -> p (a b)")]
        stgb = [[XBb[0], XBb[1]], [XBb[2], XBb[3]]]
        dma(stg[0][:, 0:2048], wgate_d, "stg0", writes=stgb[0])
        cp("dve", WG[:].rearrange("p a b -> p (a b)"), stg[0][:, 0:2048], stgb[0], [WGb])
        dma(stg[1][:, 0:384], cmat_d, "stg1", writes=stgb[1])
        cp("dve", CM[:].rearrange("p a b -> p (a b)"), stg[1][:, 0:384], stgb[1], [CMb])
        cp("dve", IDF[:], stg[1][:, 256:384], stgb[1], [IDFb])
        mset("pool", VT[:, :, 64:128], 1.0, [VTb])
        dma(stg[0][0:16, 0:128], brow_d.rearrange("o (k m) -> (o k) m", k=16), "stg0", writes=stgb[0])
        cp("dve", BROW[:], stg[0][0:16, 0:128], stgb[0], [BROWb])
        ts("dve", DV[:, 0:1], PV[:, 0:1], 0.125, None, ALU.mult, None, [PVb], [DVb])
        ts("dve", DV[:, 8:24], PV[:, 22:38], 0.5, None, ALU.mult, None, [PVb], [DVb])
        act(DV[:, 24:32], PV[:, 38:46], AF.Exp, [PVb], [DVb], scale=-1.0)
        act(DV[:, 24:32], DV[:, 24:32], AF.Ln, [DVb], [DVb], bias=1.0)
        ts("dve", DV[:, 24:32], DV[:, 24:32], -4.0, None, ALU.mult, None, [DVb], [DVb])
        ts("dve", DV[:, 32:40], DV[:, 24:32], 2.0, None, ALU.mult, None, [DVb], [DVb])
        GQ8, GK = DV[:, 0:1], PV[:, 1:2]
        si = 0
        for kc in range(8):
            s_ = si % 2; si += 1
            dma(stg[s_][:, 0:1792], win_d[kc * 128:(kc + 1) * 128, :], "stg%d" % s_, writes=stgb[s_])
            ts("dve" if kc % 2 == 0 else "pool", WIN[:, kc, :], stg[s_][:, 0:1792], GV[:, kc:kc + 1], None, ALU.mult, None,
               stgb[s_] + [GVb], [WINb])
        for kc in range(8):
            s_ = si % 2; si += 1
            dma(stg[s_][:, 0:1024], wout_d[kc * 128:(kc + 1) * 128, :], "stg%d" % s_, writes=stgb[s_])
            ts("dve" if kc % 2 == 0 else "pool", WOUT[:, kc, :], stg[s_][:, 0:1024], GV[:, 8 + kc:9 + kc], None, ALU.mult, None,
               stgb[s_] + [GVb], [WOUTb])
        WGSb = Buf("wgu_s")
        WDSb = Buf("wdn_s")

        def prep_ffn():
            stq = [QT[q][:].rearrange("p a b -> p (a b)").bitcast(F32) for q in range(2)]
            stqb = [QTb[0], QTb[1]]
            u = 0
            for j in range(22):
                b_ = j % 2
                for h in range(2):
                    q_ = u % 2; u += 1
                    dma(stq[q_][:, :], wgu_d[j][:, h * 1024:(h + 1) * 1024], "stq%d" % q_, writes=stqb[q_])
                    tt("dve", WGUB[b_][:, h * 1024:(h + 1) * 1024].rearrange("p (k c) -> p k c", k=4),
                       stq[q_][:, :].rearrange("p (k c) -> p k c", k=4),
                       bcast_last(GV[:, 16 + 4 * h:20 + 4 * h], 256), ALU.mult, stqb[q_] + [GVb], [WGUBb[b_]])
                    yield
                dma(wgu_s[j], WGUB[b_][:], "wgs%d" % b_, reads=[WGUBb[b_]], writes=[WGSb])
            for j in range(22):
                b_ = j % 2
                q_ = u % 2; u += 1
                dma(stq[q_][:, :], wdn_d[j], "stq%d" % q_, writes=stqb[q_])
                cp("pool", WDC[b_][:, 0:1024], stq[q_][:, :], stqb[q_], [WDCb[b_]])
                dma(wdn_s.rearrange("c p j e -> p c j e")[:, :, j, :], WDC[b_][:, 0:1024].rearrange("p (c e) -> p c e", c=8),
                    "wds%d" % b_, reads=[WDCb[b_]], writes=[WDSb])
                yield

        prep_gen = prep_ffn()

        smi = [0]

        def small():
            i = smi[0] % 7
            smi[0] += 1
            return SM[:, i, :], SMb[i]

        xni = [0]

        def norm_transpose(xtile, xbuf, np_, ti, bank):
            sm, smb = small()
            xn = xni[0] % 2
            xni[0] += 1
            act(XN[xn][:np_, :], xtile, AF.Square, [xbuf], [XNb[xn], smb], accum_out=sm[:np_, 0:1])
            act(sm[:np_, 1:2], sm[:np_, 0:1], AF.Ln, [smb], [smb], scale=1.0 / 1024, bias=EPS)
            act(sm[:np_, 2:3], sm[:np_, 1:2], AF.Exp, [smb], [smb], scale=-0.5)
            ts("dve", XN[xn][:np_, :], xtile, sm[:np_, 2:3], None, ALU.mult, None, [xbuf, smb], [XNb[xn]])
            tps = BK[bank].bitcast(BF16)
            for kc in range(8):
                P.op("pe", lambda e, o=tps[:, kc * 128:kc * 128 + np_], i=XN[xn][:np_, kc * 128:(kc + 1) * 128],
                     d=IDENT[:np_, :np_]: e.transpose(out=o, in_=i, identity=d), [XNb[xn], CMb], [BKb[bank]])
            cp("dve", XT[:, :, ti * 128:ti * 128 + np_],
               tps.rearrange("p (k c) -> p k c", k=8)[:, :, 0:np_], [BKb[bank]], [XTb[ti]])

        def proj_fm(c0, n, bank, ntile):
            for kc in range(8):
                mm(BK[bank][:, 0:n], WIN[:, kc, c0:c0 + 128], XT[:, kc, 0:n], kc == 0, kc == 7,
                   [WINb] + XTb[:ntile], [BKb[bank]])

        def proj_fm_g(c0, n, bank, ntile):
            for kc in range(8):
                mm(BK[bank][:, 0:n], WIN[:, kc, c0:c0 + 128], XT[:, kc, 0:n], kc == 0, kc == 7,
                   [WINb] + XTb[:ntile], [BKb[bank]])
                if kc == 3:
                    yield

        def qk_rope(bank, n, gvec, gbufs, dst, dstb, b2, b3):
            act(SQb_t[:, 0:n], BK[bank][:, 0:n], AF.Square, [BKb[bank]], [SQbb])
            act(QGb_t[:, 0:n], BK[bank][:, 0:n], AF.Identity, [BKb[bank]] + gbufs, [QGbb], scale=gvec)
            mm(BK[b2][:, 0:n], BONES, SQb_t[:, 0:n], True, True, [CMb, SQbb], [BKb[b2]])
            mm(BK[b3][:, 0:n], PERM, QGb_t[:, 0:n], True, True, [CMb, QGbb], [BKb[b3]])
            act(RS[:, 0:n], BK[b2][:, 0:n], AF.Ln, [BKb[b2]], [RSb], scale=1.0 / 64, bias=EPS)
            act(RS[:, 0:n], RS[:, 0:n], AF.Exp, [RSb], [RSb], scale=-0.5)
            tt("dve", TA[:, 0:n], QGb_t[:, 0:n], ROC[:, 0:n], ALU.mult, [QGbb, ROCb], [TAb])
            TB_, TBb_ = f32t(10), f32b(10)
            tt("dve", TB_[:, 0:n], BK[b3][:, 0:n], ROS[:, 0:n], ALU.mult, [BKb[b3], ROSb], TBb_)
            tt("dve", TA[:, 0:n], TA[:, 0:n], TB_[:, 0:n], ALU.add, [TAb] + TBb_, [TAb])
            tt("dve", dst, TA[:, 0:n], RS[:, 0:n], ALU.mult, [TAb, RSb], dstb)

        def cwb_of(cwslot):
            if cwslot == 22:
                return SCR[:, 22, :], [SCRb[22]]
            if cwslot == -1:
                return PT[0][:, 0:512], [PTb[0]]
            if cwslot == -2:
                return PT[1][:, 0:512], [PTb[1]]
            return PT[1][:, 512:1024], [PTc]

        def cw_of(cw):
            if cw < 100:
                return f32t(cw), f32b(cw)
            h = cw - 100
            return (CATL[0][:, 2 * h:2 * h + 2, :].rearrange("p a b -> p (a b)").bitcast(F32),
                    [CATLb[0][2 * h], CATLb[0][2 * h + 1]])

        def conv_chunk(c, n, cw=8, cwslot=22):
            CW, CWb_ = cw_of(cw)
            ts("dve", CW[:, 0:n], LB[:, c, 0:n], PV[:, 2 + c:3 + c], PV[:, 18 + c:19 + c], ALU.mult, ALU.add,
               [LBb, PVb], CWb_)
            for j in range(1, 4):
                stt(CW[:, 0:n], LB[:, c, j:j + n], PV[:, 2 + j * 4 + c:3 + j * 4 + c], CW[:, 0:n], ALU.mult, ALU.add,
                    [LBb, PVb] + CWb_, CWb_)
            cwb_ap, cwb_b = cwb_of(cwslot)
            cp("pool", cwb_ap[:, 0:n], CW[:, 0:n], CWb_, cwb_b)

        def gates(c, d, n, tset=None, cw=8, cwslot=22, split=False):
            o = (0 if d == 0 else 4) if tset is None else tset
            T0, T1, T2 = f32t(o), f32t(o + 1), f32t(o + 2)
            T0b, T1b, T2b = f32b(o), f32b(o + 1), f32b(o + 2)
            CW, CWb_ = cw_of(cw)
            cwb_ap, cwb_b = cwb_of(cwslot)
            ix = d * 4 + c
            ia, ixx = (d * 2 + 0) * 4 + c, (d * 2 + 1) * 4 + c
            TH = SCR[:, 2 * o:2 * o + 4, :].rearrange("p a b -> p (a b)").bitcast(F32).rearrange("p (t m) -> p t m", t=2)
            PS2 = PSt[3].rearrange("p (t m) -> p t m", t=2)

            def stage1():
                mm(BK[6][:, 0:n], WG[:, ia, :], cwb_ap[:, 0:n], True, False, [WGb] + cwb_b, [BKb[6]])
                mm(BK[6][:, 0:n], BROW[:, :], bcast_last(IDENT[0:16, ia:ia + 1], n)[:, 0, :], False, True, [BROWb, CMb], [BKb[6]])
                mm(BK[7][:, 0:n], WG[:, ixx, :], cwb_ap[:, 0:n], True, False, [WGb] + cwb_b, [BKb[7]])
                mm(BK[7][:, 0:n], BROW[:, :], bcast_last(IDENT[0:16, ixx:ixx + 1], n)[:, 0, :], False, True, [BROWb, CMb], [BKb[7]])
                if not split:
                    stage1_tanh()

            def stage1_tanh():
                act(TH[:, :, 0:n], PS2[:, :, 0:n], AF.Tanh, [BKb[6], BKb[7]], T0b + T1b, scale=0.5)

            stage1.tanh = stage1_tanh

            steps = [
                lambda: act(T2[:, 0:n], T0[:, 0:n], AF.Exp, T0b + [DVb], T2b, scale=DV[:, 24 + ix:25 + ix], bias=DV[:, 24 + ix:25 + ix]),
                lambda: act(T0[:, 0:n], T0[:, 0:n], AF.Exp, T0b + [DVb], T0b, scale=DV[:, 32 + ix:33 + ix], bias=DV[:, 32 + ix:33 + ix]),
                lambda: stt(T1[:, 0:n], T1[:, 0:n], 1.0, CW[:, 0:n], ALU.add, ALU.mult, T1b + CWb_, T1b),
                lambda: act(T0[:, 0:n], T0[:, 0:n], AF.Ln, T0b, T0b, scale=-1.0, bias=1.0 + 2.0 ** -23),
                lambda: act(T0[:, 0:n], T0[:, 0:n], AF.Exp, T0b, T0b, scale=0.5),
                lambda: stt(T1[:, 0:n], T0[:, 0:n], 0.5, T1[:, 0:n], ALU.mult, ALU.mult, T0b + T1b, T1b),
            ]
            return stage1, steps, (T1, T1b, T2, T2b)

        def scan(out, a, u, init, reads, writes):
            P.op("dve", lambda e, o=out, x=a, y=u, i=init: e.tensor_tensor_scan(out=o, data0=x, data1=y, initial=i,
                                                                               op0=ALU.mult, op1=ALU.add), reads, writes)

        def load_x(src, k):
            if k == 0:
                dma(XB[0:16, 0, :], meta, "xb0", writes=[XBb[0]])
            else:
                for i in range(4):
                    r0 = (k - 1) * 512 + i * 128
                    dma(XB[:, i, :], src[r0:r0 + 128, :], "xb%d" % i, writes=[XBb[i]])

        def load_block(src, k, with_rope=True, with_x=True, tabs=None, local=False):
            if with_x:
                load_x(src, k)
            if k == 0:
                n, s0 = 16, 0
            else:
                n, s0 = 512, 16 + (k - 1) * 512
            if with_rope:
                tc_, ts_ = tabs if tabs is not None else (ropec_d, ropes_d)
                t0 = (k - 1) * 512 if local else s0
                dma(ROC[:, 0:n], tc_[:, t0:t0 + n], "roc", writes=[ROCb])
                dma(ROS[:, 0:n], ts_[:, t0:t0 + n], "ros", writes=[ROSb])
            return n, s0

        for sl in range(NSLOT):
            xsrc = xs0 if sl == 0 else xs1
            local = (sl == 1)
            X2 = xloc if local else xs0
            Y2 = y1_d if local else y0_d
            TAB2 = (ropec1_d, ropes1_d) if local else (ropec_d, ropes_d)
            KLAST = H if local else NBLK
            mset("dve", CARB[:], 0.0, [CARBb])
            mset("dve", RST[:, :, NBLK + 1:NBLK + 2], 0.0, [RSTb])
            load_x(xsrc, NBLK)
            CWC = [(8, 22), (9, -1), (100, -2), (101, -3)]

            def p1_AB(k):
                n, s0 = load_block(xsrc, k, with_rope=False, with_x=False)
                ntile = 1 if k == 0 else 4
                for i in range(ntile):
                    norm_transpose(XB[:min(n, 128), i, :], XBb[i], min(n, 128), i, 4 + (i % 2))
                    yield
                if k >= 1:
                    load_x(xsrc, k - 1)
                dma(ROC[:, 0:n], ropec_d[:, s0:s0 + n], "roc", writes=[ROCb])
                dma(ROS[:, 0:n], ropes_d[:, s0:s0 + n], "ros", writes=[ROSb])
                if k >= 1:
                    if k == NBLK:
                        mset("dve", LB[:, :, 512:515], 0.0, [LBb])
                    else:
                        cp("dve", LB[:, :, 512:515], LB[:, :, 0:3], [LBb], [LBb])
                    for c in range(4):
                        proj_fm(768 + c * 128, 512, 5, 4)
                        act(LB[:, c, 0:512], BK[5][:, 0:512], AF.Copy, [BKb[5]], [LBb])
                        yield
                    cp("dve", RST[:, :, k:k + 1], LB[:, :, 0:1], [LBb], [RSTb])
                proj_fm(512, n, 0, ntile)
                yield
                qk_rope(0, n, GK, [PVb], KT[:, s0:s0 + n], [KTb], 1, 2)
                yield
                for i in range(ntile):
                    np_ = min(n, 128)
                    j = 0 if k == 0 else 1 + (k - 1) * 4 + i
                    bnk = 3 if i % 2 == 0 else 2
                    for kc in range(8):
                        mm(BK[bnk][:np_, 0:128], XT[:, kc, i * 128:i * 128 + np_], WIN[:, kc, 640:768], kc == 0, kc == 7,
                           [WINb, XTb[i]], [BKb[bnk]])
                    cp("dve", VT[:np_, j, 0:64], BK[bnk][:np_, 0:64], [BKb[bnk]], [VTb])
                    cp("dve", VT[:np_, j, 128:192], BK[bnk][:np_, 64:128], [BKb[bnk]], [VTb])
                    yield

            def p1_C(k):
                for c in range(4):
                    conv_chunk(c, 512, CWC[c][0], CWC[c][1])
                    yield
                for c0 in (0, 2):
                    cfg = [(c0, 4, 7), (c0 + 1, 0, 3)]
                    gs = []
                    for (c, tset, hb) in cfg:
                        st1, steps, res_ = gates(c, 1, 512, tset, CWC[c][0], CWC[c][1])
                        st1()
                        gs.append((steps, res_))
                        yield
                    for f_, g_ in zip(gs[0][0], gs[1][0]):
                        f_()
                        g_()
                        yield
                    for (c, tset, hb), (steps, (U, Ub, A, Ab)) in zip(cfg, gs):
                        if k == NBLK:
                            mset("dve", U[:, 510:512], 0.0, Ub)
                        HB, HBb = f32t(hb), f32b(hb)
                        scan(rev(HB[:, 0:512]), rev(A[:, 0:512]), rev(U[:, 0:512]), CARB[:, c:c + 1], Ab + Ub + [CARBb], HBb)
                        cp("dve", CARB[:, c:c + 1], HB[:, 0:1], HBb, [CARBb])
                        cp("dve", SST[:, c, k:k + 1], HB[:, 510:511], HBb, [SSTb])
                    yield

            for _ in p1_AB(NBLK):
                pass
            for k in range(NBLK, 0, -1):
                g1, g2 = p1_C(k), p1_AB(k - 1)
                a1 = a2 = True
                while a1 or a2:
                    if a1:
                        a1 = next(g1, "STOP") != "STOP"
                    if a2:
                        a2 = next(g2, "STOP") != "STOP"
                if sl == 0:
                    for _ in range(4):
                        next(prep_gen, None)
            dump("KT", KT[:], [KTb])
            dump("VT", VT[:].rearrange("p a b -> p (a b)"), [VTb])
            dump("SST", SST[:].rearrange("p a b -> p (a b)"), [SSTb])
            dump("RST", RST[:].rearrange("p a b -> p (a b)"), [RSTb])

            if sl == 0:
                for _ in prep_gen:
                    pass
            mset("dve", CARF[:], 0.0, [CARFb])

            def genA(k, lite=False):
                if lite:
                    n, s0 = load_block(xsrc, k, with_rope=False, with_x=True)
                else:
                    n, s0 = load_block(X2, k, with_rope=(k > 0), with_x=(k < 3), tabs=TAB2, local=local)
                ntile = 1 if k == 0 else 4
                nprev = 0 if k == 0 else (16 if k == 1 else 512)
                qp = k % 2
                for i in range(ntile):
                    norm_transpose(XB[:min(n, 128), i, :], XBb[i], min(n, 128), i, 6 + (i % 2))
                    yield
                if k == 0:
                    mset("dve", LB[:, :, 0:2], 0.0, [LBb])
                elif local and not lite and k == 1:
                    pass
                else:
                    cp("dve", LB[:, :, 0:2], LB[:, :, nprev:nprev + 2], [LBb], [LBb])
                for c in range(4):
                    bk = 6 + (c % 2)
                    yield from proj_fm_g(768 + c * 128, n, bk, ntile)
                    yield
                    act(LB[:, c, 2:2 + n], BK[bk][:, 0:n], AF.Copy, [BKb[bk]], [LBb])
                if local and not lite:
                    cp("dve", LB[:, :, 2 + n:3 + n], RSTL[:, :, k - 1:k], [RSTLb, LBb], [LBb])
                else:
                    cp("dve", LB[:, :, 2 + n:3 + n], RST[:, :, k + 1:k + 2], [RSTb, LBb], [LBb])
                for c in range(4):
                    conv_chunk(c, n)
                    yield
                    if k > 0 and not lite:
                        yield from proj_fm_g(c * 128, n, 6, ntile)
                        yield
                        qk_rope(6, n, GQ8, [DVb], QT[qp][:, c, 0:n], [QTb[qp][c]], 7, 6)
                        yield
                    HF, HFb = f32t(3), f32b(3)
                    HB, HBb = f32t(7), f32b(7)
                    GG, GGb = f32t(9), f32b(9)
                    GT, GTb = f32t(10), f32b(10)
                    s1f, stf, (U, Ub, A, Ab) = gates(c, 0, n, split=True)
                    s1f()
                    yield
                    s1f.tanh()
                    if k == 0 or lite:
                        for f_ in stf:
                            f_()
                        scan(HF[:, 0:n], A[:, 0:n], U[:, 0:n], CARF[:, c:c + 1], Ab + Ub + [CARFb], HFb)
                        cp("dve", CARF[:, c:c + 1], HF[:, n - 1:n], HFb, [CARFb])
                        yield
                        continue
                    s1b, stb, (U2, U2b, A2, A2b) = gates(c, 1, n, split=True)
                    s1b()
                    yield
                    s1b.tanh()
                    for f_, g_ in zip(stf, stb):
                        f_()
                        g_()
                    yield
                    scan(HF[:, 0:n], A[:, 0:n], U[:, 0:n], CARF[:, c:c + 1], Ab + Ub + [CARFb], HFb)
                    cp("dve", CARF[:, c:c + 1], HF[:, n - 1:n], HFb, [CARFb])
                    sst_ap, sst_b = (SSTL[:, c, k - 1:k], SSTLb) if local else (SST[:, c, k:k + 1], SSTb)
                    scan(rev(HB[:, 0:n]), rev(A2[:, 0:n]), rev(U2[:, 0:n]), sst_ap, A2b + U2b + [sst_b], HBb)
                    yield from proj_fm_g(1280 + c * 128, n, 6, ntile)
                    yield
                    act(GT[:, 0:n], BK[6][:, 0:n], AF.Square, [BKb[6]], GTb)
                    tt("pool", HF[:, 0:n], HF[:, 0:n], HB[:, 0:n], ALU.add, HFb + HBb, HFb)
                    ts("dve", GT[:, 0:n], GT[:, 0:n], 0.044715, 1.0, ALU.mult, ALU.add, GTb, GTb)
                    tt("dve", GT[:, 0:n], GT[:, 0:n], BK[6][:, 0:n], ALU.mult, GTb + [BKb[6]], GTb)
                    act(GT[:, 0:n], GT[:, 0:n], AF.Tanh, GTb, GTb, scale=0.7978845608028654)
                    stt(GG[:, 0:n], GT[:, 0:n], 1.0, BK[6][:, 0:n], ALU.add, ALU.mult, GTb + [BKb[6]], GGb)
                    stt(CATL[qp][:, c, 0:n], GG[:, 0:n], 0.5, HF[:, 0:n], ALU.mult, ALU.mult, GGb + HFb, [CATLb[qp][c]])
                    yield

            def genLite(k):
                n, s0 = load_block(xsrc, k, with_rope=False, with_x=(k == 0))
                ntile = 1 if k == 0 else 4
                nprev = 0 if k == 0 else (16 if k == 1 else 512)
                for i in range(ntile):
                    norm_transpose(XB[:min(n, 128), i, :], XBb[i], min(n, 128), i, 4 + (i % 2))
                if k + 1 <= H:
                    load_x(xsrc, k + 1)
                if k == 0:
                    mset("dve", LB[:, :, 0:2], 0.0, [LBb])
                else:
                    cp("dve", LB[:, :, 0:2], LB[:, :, nprev:nprev + 2], [LBb], [LBb])
                for c in range(4):
                    bk = 2 + (c % 2)
                    proj_fm(768 + c * 128, n, bk, ntile)
                    act(LB[:, c, 2:2 + n], BK[bk][:, 0:n], AF.Copy, [BKb[bk]], [LBb])
                cp("dve", LB[:, :, 2 + n:3 + n], RST[:, :, k + 1:k + 2], [RSTb, LBb], [LBb])
                for c0 in (0, 2):
                    cfg = [(c0, 0, 8, 22, 3), (c0 + 1, 4, 9, -1, 7)]
                    for (c, tset, cw, cws, hf) in cfg:
                        conv_chunk(c, n, cw, cws)
                    gs = []
                    for (c, tset, cw, cws, hf) in cfg:
                        st1, steps, res_ = gates(c, 0, n, tset, cw, cws)
                        st1()
                        gs.append((steps, res_))
                    for f_, g_ in zip(gs[0][0], gs[1][0]):
                        f_()
                        g_()
                    for (c, tset, cw, cws, hf), (steps, (U, Ub, A, Ab)) in zip(cfg, gs):
                        HF, HFb = f32t(hf), f32b(hf)
                        scan(HF[:, 0:n], A[:, 0:n], U[:, 0:n], CARF[:, c:c + 1], Ab + Ub + [CARFb], HFb)
                        cp("dve", CARF[:, c:c + 1], HF[:, n - 1:n], HFb, [CARFb])
                yield

            def genATT(k):
                qp = k % 2
                for c in range(4):
                    def qk(j):
                        kp = 16 if j == 0 else 128
                        c0 = 0 if j == 0 else 16 + (j - 1) * 128
                        par = j % 2
                        S = PSt[par]
                        mm(S[:kp, 0:512], KT[0:64, c0:c0 + kp], QT[qp][0:64, c, :], True, True, [KTb, QTb[qp][c]],
                           [BKb[2 * par], BKb[2 * par + 1]])
                        mm(S[:kp, 512:1024], KT[64:128, c0:c0 + kp], QT[qp][64:128, c, :], True, True, [KTb, QTb[qp][c]],
                           [BKb[2 * par], BKb[2 * par + 1]])

                    def ex(j):
                        kp = 16 if j == 0 else 128
                        par = j % 2
                        S = PSt[par]
                        act(PT[par][:kp, :], S[:kp, :], AF.Exp, [BKb[2 * par], BKb[2 * par + 1]], [PTb[par]])

                    def pv(j):
                        kp = 16 if j == 0 else 128
                        par = j % 2
                        st_, sp_ = (j == 0), (j == NKT - 1)
                        mm(BK[4], VT[:kp, j, 0:128], PT[par][:kp, 0:512], st_, sp_, [VTb, PTb[par]], [BKb[4]])
                        mm(BK[5], VT[:kp, j, 64:192], PT[par][:kp, 512:1024], st_, sp_, [VTb, PTb[par]], [BKb[5]])

                    qk(0)
                    for j in range(NKT):
                        if j + 1 < NKT:
                            qk(j + 1)
                        ex(j)
                        if j >= 1:
                            pv(j - 1)
                        yield
                    pv(NKT - 1)
                    P.op("dve", lambda e: e.reciprocal(out=RD[0:64, :], in_=BK[4][64:128, :]), [BKb[4]], [RDb])
                    P.op("dve", lambda e: e.reciprocal(out=RD[64:128, :], in_=BK[5][0:64, :]), [BKb[5]], [RDb])
                    tt("dve", CAT[0:64, c, :], BK[4][0:64, :], RD[0:64, :], ALU.mult, [BKb[4], RDb], [CATb[c]])
                    tt("dve", CAT[64:128, c, :], BK[5][64:128, :], RD[64:128, :], ALU.mult, [BKb[5], RDb], [CATb[c]])

            def genC(k):
                qp = k % 2
                for i in range(4):
                    r0 = (k - 1) * 512 + i * 128
                    dma(XB[:, i, :], X2[r0:r0 + 128, :], "xb%d" % i, writes=[XBb[i]])

                def catc(ch):
                    return (CAT[:, ch, :], CATb[ch]) if ch < 4 else (CATL[qp][:, ch - 4, :], CATLb[qp][ch - 4])
                for i in range(4):
                    for g in range(2):
                        bnk = 6 + g
                        for cc in range(4):
                            ct, cb = catc(g * 4 + cc)
                            mm(BK[bnk][:, 0:128], ct[:, i * 128:(i + 1) * 128], ct[:, i * 128:(i + 1) * 128],
                               cc == 0, cc == 3, [cb], [BKb[bnk]])
                        tmp, tmpb = (TA, TAb) if g == 0 else (RS, RSb)
                        tt("dve", tmp[:, 0:128], BK[bnk][:, 0:128], IDF[:], ALU.mult, [BKb[bnk], IDFb], [tmpb])
                        P.op("dve", lambda e, o=SSN[:, i * 2 + g:i * 2 + g + 1], t_=tmp: e.reduce_sum(out=o, in_=t_[:, 0:128], axis=AX.X),
                             [tmpb], [SSNb])
                    yield
                act(RSN[:, 0:8], SSN[:, 0:8], AF.Ln, [SSNb], [RSNb], scale=1.0 / 512, bias=EPS)
                act(RSN[:, 0:8], RSN[:, 0:8], AF.Exp, [RSNb], [RSNb], scale=-0.5)
                for i in range(4):
                    for hc in range(2):
                        for cc in range(4):
                            ct, cb = catc(cc)
                            mm(BK[6], ct[:, i * 128:(i + 1) * 128], WOUT[:, cc, hc * 512:(hc + 1) * 512], cc == 0, cc == 3,
                               [cb, WOUTb], [BKb[6]])
                        yield
                        for cc in range(4, 8):
                            ct, cb = catc(cc)
                            mm(BK[7], ct[:, i * 128:(i + 1) * 128], WOUT[:, cc, hc * 512:(hc + 1) * 512], cc == 4, cc == 7,
                               [cb, WOUTb], [BKb[7]])
                        xh = XB[:, i, hc * 512:(hc + 1) * 512]
                        stt(xh, BK[6], RSN[:, 2 * i:2 * i + 1], xh, ALU.mult, ALU.add, [BKb[6], RSNb, XBb[i]], [XBb[i]])
                        stt(xh, BK[7], RSN[:, 2 * i + 1:2 * i + 2], xh, ALU.mult, ALU.add, [BKb[7], RSNb, XBb[i]], [XBb[i]])
                        yield
                for i in range(4):
                    norm_transpose(XB[:, i, :], XBb[i], 128, i, 6 + (i % 2))
                    yield
                for j in range(22):
                    b_ = j % 2
                    dma(WGUB[b_][:], wgu_s[j], "wgu%d" % b_, reads=[WGSb], writes=[WGUBb[b_]])
                    for kc in range(8):
                        mm(BK[6], WGUB[b_][:, kc * 256:kc * 256 + 128], XT[:, kc, :], kc == 0, kc == 7,
                           [WGUBb[b_]] + XTb, [BKb[6]])
                        if kc % 4 == 3:
                            yield
                    for kc in range(8):
                        mm(BK[7], WGUB[b_][:, kc * 256 + 128:kc * 256 + 256], XT[:, kc, :], kc == 0, kc == 7,
                           [WGUBb[b_]] + XTb, [BKb[7]])
                        if kc == 3:
                            yield
                    tmp, tmpb = (TA, TAb) if b_ == 0 else (RS, RSb)
                    act(tmp[:], BK[6], AF.Tanh, [BKb[6]], [tmpb], scale=0.5)
                    stt(tmp[:], tmp[:], 1.0, BK[6], ALU.add, ALU.mult, [tmpb, BKb[6]], [tmpb])
                    stt(SCR[:, j, :], tmp[:], 0.5, BK[7], ALU.mult, ALU.mult, [tmpb, BKb[7]], [SCRb[j]])
                    yield
                for cc in range(8):
                    b_ = cc % 2
                    dma(WDC[b_][:], wdn_s[cc].rearrange("p j e -> p (j e)"), "wdn%d" % b_, reads=[WDSb], writes=[WDCb[b_]])
                    for j in range(22):
                        mm(BK[6], WDC[b_][:, j * 128:(j + 1) * 128], SCR[:, j, :], j == 0, j == 21, [WDCb[b_], SCRb[j]], [BKb[6]])
                        if j % 4 == 3:
                            yield
                    yield
                    act(RS[:], BK[6], AF.Copy, [BKb[6]], [RSb])
                    for i in range(4):
                        P.op("pe", lambda e, o=BK[7][:, i * 128:(i + 1) * 128], i_=RS[:, i * 128:(i + 1) * 128]:
                             e.transpose(out=o, in_=i_, identity=IDF[:]), [RSb, IDFb], [BKb[7]])
                    tt("dve", XB[:, :, cc * 128:(cc + 1) * 128], BK[7].rearrange("p (i e) -> p i e", i=4),
                       XB[:, :, cc * 128:(cc + 1) * 128], ALU.add, [BKb[7]] + XBb, XBb)
                    yield
                for i in range(4):
                    sm, smb = small()
                    xn = xni[0] % 2
                    xni[0] += 1
                    act(XN[xn][:], XB[:, i, :], AF.Square, [XBb[i]], [XNb[xn], smb], accum_out=sm[:, 0:1])
                    act(sm[:, 1:2], sm[:, 0:1], AF.Ln, [smb], [smb], scale=1.0 / 1024, bias=EPS)
                    act(sm[:, 2:3], sm[:, 1:2], AF.Exp, [smb], [smb], scale=-0.5)
                    stt(XB[:, i, :], XB[:, i, :], sm[:, 2:3], GF[:], ALU.mult, ALU.mult, [XBb[i], smb, GFb], [XBb[i]])
                    r0 = (k - 1) * 512 + i * 128
                    dma(Y2[r0:r0 + 128, :], XB[:, i, :], "y%d" % i, reads=[XBb[i]])
                    if k + 2 <= KLAST:
                        r2 = (k + 1) * 512 + i * 128
                        dma(XB[:, i, :], X2[r2:r2 + 128, :], "xb%d" % i, writes=[XBb[i]])
                    yield

            def lane(w):
                if w - 1 >= 1:
                    yield from genC(w - 1)
                if w + 1 <= KLAST:
                    yield from genA(w + 1)

            def drain(g):
                for _ in g:
                    pass

            def count(g):
                P.dry = True
                n_ = sum(1 for _ in g)
                P.dry = False
                return n_

            if local:
                for k in range(0, H + 1):
                    drain(genLite(k))
                    if k == 0:
                        cp("dve", FS0[:], CARF[:], [CARFb], [FSb])
                        cp("dve", LH0[:], LB[:, :, 16:18], [LBb], [LHb])
                    if k == H:
                        cp("dve", FS8[:], CARF[:], [CARFb], [FSb])
                        cp("dve", LH8[:], LB[:, :, 512:514], [LBb], [LHb])
                sa, sb_ = SELV[:, 0:1], SELV[:, 1:2]
                ts("dve", CARF[:], FS0[:], sa, None, ALU.mult, None, [FSb, SELVb], [CARFb])
                stt(CARF[:], FS8[:], sb_, CARF[:], ALU.mult, ALU.add, [FSb, SELVb, CARFb], [CARFb])
                ts("dve", LB[:, :, 0:2], LH0[:], sa, None, ALU.mult, None, [LHb, SELVb], [LBb])
                stt(LB[:, :, 0:2], LH8[:], sb_, LB[:, :, 0:2], ALU.mult, ALU.add, [LHb, SELVb, LBb], [LBb])
                ts("dve", SSTL[:], SST[:, :, 1:1 + H], sa, None, ALU.mult, None, [SSTb, SELVb], [SSTLb])
                stt(SSTL[:], SST[:, :, 1 + H:1 + 2 * H], sb_, SSTL[:], ALU.mult, ALU.add, [SSTb, SELVb, SSTLb], [SSTLb])
                ts("dve", RSTL[:], RST[:, :, 2:2 + H], sa, None, ALU.mult, None, [RSTb, SELVb], [RSTLb])
                stt(RSTL[:], RST[:, :, 2 + H:2 + 2 * H], sb_, RSTL[:], ALU.mult, ALU.add, [RSTb, SELVb, RSTLb], [RSTLb])
            else:
                drain(genA(0))
            drain(genA(1))
            for w in range(1, KLAST + 1):
                nl = count(lane(w))
                na = 4 * NKT
                ln = lane(w)
                done = 0
                for step, _ in enumerate(genATT(w), 1):
                    tgt = (step * nl) // na
                    while done < tgt:
                        next(ln, None)
                        done += 1
                drain(ln)
            drain(genC(KLAST))

        P.emit(st, final_slots=["y0", "y1", "y2", "y3"] + ["dbg%d" % (i + 1) for i in range(dbg_n[0])])
    return nc, dbg_out


def rope_tables(L):
    f32 = np.float32
    t = np.arange(L - 16)
    row = (t // 64).astype(f32)
    col = (t % 64).astype(f32)
    freqs = (f32(10000.0) ** (-np.arange(0, 32, 2, dtype=f32) / f32(32))).astype(f32)
    C = np.ones((128, L), f32)
    S = np.zeros((128, L), f32)
    for p in range(128):
        dm = p % 64
        pos = row if dm // 32 == 0 else col
        ang = (pos * freqs[dm % 16]).astype(f32)
        C[p, 16:] = np.cos(ang)
        S[p, 16:] = np.sin(ang)
    return C, S


def host_layout(inp):
    f32 = np.float32
    w_in = np.asarray(inp["w_in"])[0]
    qcols = np.array([(c + 4 * h) * 64 + d for c in range(4) for h in range(2) for d in range(64)])
    win = np.ascontiguousarray(np.concatenate([w_in[:, qcols], w_in[:, 512:]], axis=1), dtype=f32)
    rowmap = np.array([(kc + 4 * (p // 64)) * 64 + p % 64 if kc < 4 else 512 + (kc - 4) * 128 + p
                       for kc in range(8) for p in range(128)])
    w_out = np.asarray(inp["w_out"])[0]
    wout = np.ascontiguousarray(w_out[rowmap, :], dtype=f32)
    gcat = np.concatenate([np.asarray(inp["attn_out_g"])[0], np.asarray(inp["lru_out_g"])[0]])[rowmap]
    pk = lambda v: np.asarray(v, f32).reshape(8, 128).T
    gv = np.ascontiguousarray(np.concatenate([pk(np.asarray(inp["norm_mix_g"])[0]), pk(gcat),
                                              pk(np.asarray(inp["norm_ffn_g"])[0])], axis=1), dtype=f32)
    wgu_o = np.asarray(inp["w_gate_up"])[0]
    g4 = wgu_o[:, :2816].reshape(8, 128, 22, 128)
    u4 = wgu_o[:, 2816:].reshape(8, 128, 22, 128)
    wgu = np.stack([g4, u4], axis=0).transpose(3, 2, 1, 0, 4)
    wgu = np.ascontiguousarray(wgu.reshape(22, 128, 2048), dtype=f32)
    wdn = np.ascontiguousarray(np.asarray(inp["w_down"])[0].reshape(22, 128, 1024), dtype=f32)
    pv = np.zeros((128, 46), f32)
    p = np.arange(128)
    pv[:, 0] = np.asarray(inp["q_norm_g"])[0][p % 64]
    pv[:, 1] = np.asarray(inp["k_norm_g"])[0][p % 64]
    cw = np.asarray(inp["conv_w"])[0]
    cb = np.asarray(inp["conv_b"])[0]
    for c in range(4):
        for j in range(4):
            pv[:, 2 + j * 4 + c] = cw[j, c * 128 + p]
        pv[:, 18 + c] = cb[c * 128 + p]
        for d in range(2):
            pv[:, 22 + d * 4 + c] = np.asarray(inp["lru_b_a"])[0][d, c * 128 + p]
            pv[:, 30 + d * 4 + c] = np.asarray(inp["lru_b_x"])[0][d, c * 128 + p]
            pv[:, 38 + d * 4 + c] = np.asarray(inp["lru_lam"])[0][d, c * 128 + p]
    wgate = np.zeros((128, 16, 128), f32)
    wa = np.asarray(inp["lru_w_a"])[0]
    wx = np.asarray(inp["lru_w_x"])[0]
    for d in range(2):
        for ax, w in enumerate((wa, wx)):
            for c in range(4):
                for hb in range(2):
                    wgate[hb * 64:(hb + 1) * 64, (d * 2 + ax) * 4 + c, hb * 64:(hb + 1) * 64] = w[d, 2 * c + hb]
    brow = np.zeros((1, 16, 128), f32)
    for d in range(2):
        for ax, bb in enumerate((np.asarray(inp['lru_b_a'])[0], np.asarray(inp['lru_b_x'])[0])):
            for c in range(4):
                brow[0, (d * 2 + ax) * 4 + c, :] = bb[d, c * 128:(c + 1) * 128]
    cmat = np.zeros((128, 3, 128), f32)
    for m in range(128):
        if (m % 64) % 32 < 16:
            cmat[m + 16, 0, m] = -1.0
        else:
            cmat[m - 16, 0, m] = 1.0
        cmat[(m // 64) * 64:(m // 64) * 64 + 64, 1, m] = 1.0
        cmat[m, 2, m] = 1.0
    return dict(win=win, wout=wout, wgu=wgu, wdn=wdn, gv=gv,
                gfin=np.asarray(inp["final_norm_g"], f32).reshape(1, 1024),
                pv=pv, brow=brow.reshape(1, 2048), wgate=wgate.reshape(128, 2048), cmat=cmat.reshape(128, 384),
                meta=np.ascontiguousarray(np.asarray(inp["meta_tokens"], f32)))


_CACHE = {}


def run(cores, shared, NBLK, NSLOT, dbg=None):
    key = (NBLK, NSLOT)
    if key not in _CACHE:
        _CACHE[key] = build(NBLK, NSLOT, dbg)
    nc, dbg_out = _CACHE[key]
    C, S = rope_tables(NBLK * 512 + 16)
    TH = (NBLK // 2) * 512
    in_maps = []
    for cd in cores:
        m = dict(shared)
        m["xs0"] = np.ascontiguousarray(cd["x0"], dtype=np.float32)
        m["ropec"] = C
        m["ropes"] = S
        if NSLOT > 1:
            hh = cd["half"]
            x1 = np.asarray(cd["x1"], dtype=np.float32)
            m["xs1"] = np.ascontiguousarray(x1)
            m["xloc"] = np.ascontiguousarray(x1[hh * TH:(hh + 1) * TH])
            m["ropec1"] = np.ascontiguousarray(C[:, 16 + hh * TH:16 + (hh + 1) * TH])
            m["ropes1"] = np.ascontiguousarray(S[:, 16 + hh * TH:16 + (hh + 1) * TH])
            m["selv"] = np.array([[1.0 - hh, float(hh)]], dtype=np.float32)
        in_maps.append(m)
    res = run_bass_kernel_spmd(nc, in_maps, core_ids=list(range(len(cores))))
    return res.results, dbg_out


def kernel(**inputs):
    xp = np.asarray(inputs["x_prompt"], np.float32)
    xsm = np.asarray(inputs["x_sample"], np.float32)
    shared = host_layout(inputs)
    cores = [dict(x0=xp[ci], x1=xsm[ci // 2], half=ci % 2) for ci in range(8)]
    results, _ = run(cores, shared, 16, 2)
    yp = np.stack([results[ci]["y0"] for ci in range(8)], axis=0)
    ys = np.stack([np.concatenate([results[2 * j]["y1"], results[2 * j + 1]["y1"]], axis=0) for j in range(4)], axis=0)
    return (yp, ys)
```

```python
import numpy as np
from contextlib import ExitStack
import concourse.bass as bass
import concourse.mybir as mybir
from concourse.bass_utils import run_bass_kernel_spmd

F32 = mybir.dt.float32
BF16 = mybir.dt.bfloat16
AF = mybir.ActivationFunctionType
ALU = mybir.AluOpType
AX = mybir.AxisListType

COMPUTE = ("pe", "act", "dve", "pool")
EPS = 1e-6


class Buf:
    __slots__ = ("name", "last_w", "readers")

    def __init__(self, name):
        self.name = name
        self.last_w = None
        self.readers = []


class Op:
    __slots__ = ("eng", "fn", "deps", "sig", "slot", "cnt")


class Prog:
    def __init__(self, nc):
        self.nc = nc
        self.ops = {e: [] for e in COMPUTE + ("sp",)}
        self.slots = {}
        self.all_ops = []
        self.dry = False

    def op(self, eng, fn, reads=(), writes=(), slot=None):
        if self.dry:
            return None
        o = Op()
        o.eng, o.fn, o.sig, o.slot, o.cnt = eng, fn, 0, slot, 0
        deps = set()
        for b in reads:
            if b.last_w is not None:
                deps.add(b.last_w)
        for b in writes:
            if b.last_w is not None:
                deps.add(b.last_w)
            deps.update(b.readers)
        for b in reads:
            b.readers.append(o)
        for b in writes:
            b.last_w = o
            b.readers = []
        if eng == "pe" and slot is None:
            deps = {d for d in deps if not (d.eng == "pe" and d.slot is None)}
        deps.discard(o)
        o.deps = deps
        if slot is not None:
            c = self.slots.get(slot, 0) + 1
            self.slots[slot] = c
            o.cnt = 16 * c
        self.ops[eng].append(o)
        self.all_ops.append(o)
        return o

    def emit(self, stack, final_slots=()):
        nc = self.nc
        for o in self.all_ops:
            for d in o.deps:
                if d.slot is None:
                    d.sig = -1
        sems = {}
        for e in COMPUTE:
            sems[e] = stack.enter_context(nc.semaphore("s_" + e))
            n = 0
            for o in self.ops[e]:
                if o.sig == -1:
                    n += 1
                    o.sig = n
        for k in self.slots:
            sems["d_" + k] = stack.enter_context(nc.semaphore("d_" + k))
        block = stack.enter_context(nc.Block())

        def run(ename, eng):
            known = {}
            for o in self.ops[ename]:
                need = {}
                for d in o.deps:
                    if d.slot is None:
                        k, v = d.eng, d.sig
                    else:
                        k, v = "d_" + d.slot, d.cnt
                    if need.get(k, 0) < v:
                        need[k] = v
                for k, v in need.items():
                    if known.get(k, 0) < v:
                        eng.wait_ge(sems[k], v)
                        known[k] = v
                ins = o.fn(eng)
                if o.slot is not None:
                    ins.then_inc(sems["d_" + o.slot], 16)
                elif o.sig > 0:
                    ins.then_inc(sems[ename], 1)
            if ename == "sp":
                for k in final_slots:
                    eng.wait_ge(sems["d_" + k], 16 * self.slots[k])

        block.tensor(lambda e: run("pe", e))
        block.scalar(lambda e: run("act", e))
        block.vector(lambda e: run("dve", e))
        block.gpsimd(lambda e: run("pool", e))
        block.sync(lambda e: run("sp", e))


def bcast_last(ap, n):
    return bass.AP(ap.tensor, ap.offset, [list(x) for x in ap.ap] + [[0, n]])


def rev(ap):
    n = ap.shape[1]
    return bass.AP(ap.tensor, ap[:, n - 1:n].offset, [list(ap.ap[0]), [-ap.ap[1][0], n]])


def build(NBLK, NSLOT, dbg=None):
    T = NBLK * 512
    L = T + 16
    NKT = 1 + NBLK * 4
    nc = bass.Bass("TRN2", target_bir_lowering=False)

    def din(name, shape, dt=F32):
        return nc.dram_tensor(name, list(shape), dt, kind="ExternalInput").ap()

    H = NBLK // 2
    TH = H * 512
    xs0 = din("xs0", [T, 1024])
    xs1 = din("xs1", [T, 1024]) if NSLOT > 1 else None
    xloc = din("xloc", [TH, 1024]) if NSLOT > 1 else None
    ropec1_d = din("ropec1", [128, TH]) if NSLOT > 1 else None
    ropes1_d = din("ropes1", [128, TH]) if NSLOT > 1 else None
    selv_d = din("selv", [1, 2]) if NSLOT > 1 else None
    meta = din("meta", [16, 1024])
    win_d = din("win", [1024, 1792])
    wout_d = din("wout", [1024, 1024])
    wgu_d = din("wgu", [22, 128, 2048])
    wdn_d = din("wdn", [22, 128, 1024])
    gv_d = din("gv", [128, 24])
    gfin_d = din("gfin", [1, 1024])
    pv_d = din("pv", [128, 46])
    wgate_d = din("wgate", [128, 16 * 128])
    cmat_d = din("cmat", [128, 3 * 128])
    brow_d = din("brow", [1, 16 * 128])
    ropec_d = din("ropec", [128, L])
    ropes_d = din("ropes", [128, L])
    y0_d = nc.dram_tensor("y0", [T, 1024], F32, kind="ExternalOutput").ap()
    y1_d = nc.dram_tensor("y1", [TH, 1024], F32, kind="ExternalOutput").ap() if NSLOT > 1 else None
    wgu_s = nc.dram_tensor("wgu_s", [22, 128, 2048], BF16, kind="Internal").ap()
    wdn_s = nc.dram_tensor("wdn_s", [8, 128, 22, 128], BF16, kind="Internal").ap()
    dbg_out = {}

    P = Prog(nc)
    with ExitStack() as st:
        def sb(name, shape, dt):
            return st.enter_context(nc.sbuf_tensor(name, list(shape), dt))

        WIN = sb("WIN", [128, 8, 1792], BF16); WINb = Buf("WIN")
        WOUT = sb("WOUT", [128, 8, 1024], BF16); WOUTb = Buf("WOUT")
        KT = sb("KT", [128, L], BF16); KTb = Buf("KT")
        VT = sb("VT", [128, NKT, 192], BF16); VTb = Buf("VT")
        XB = sb("XB", [128, 4, 1024], F32); XBb = [Buf("XB%d" % i) for i in range(4)]
        GF = sb("GF", [128, 1024], F32); GFb = Buf("GF")
        XN = [sb("XN%d" % i, [128, 1024], BF16) for i in range(2)]; XNb = [Buf("XN0"), Buf("XN1")]
        XT = sb("XT", [128, 8, 512], BF16); XTb = [Buf("XT%d" % i) for i in range(4)]
        ROC = sb("ROC", [128, 512], F32); ROCb = Buf("ROC")
        ROS = sb("ROS", [128, 512], F32); ROSb = Buf("ROS")
        LB = sb("LB", [128, 4, 516], F32); LBb = Buf("LB")
        SCR = sb("SCR", [128, 23, 512], BF16); SCRb = [Buf("SCR%d" % i) for i in range(23)]
        QT = [sb("QT%d" % q, [128, 4, 512], BF16) for q in range(2)]; QTb = [[Buf("QT%d_%d" % (q, i)) for i in range(4)] for q in range(2)]
        SQb_t = sb("SQb", [128, 512], BF16); SQbb = Buf("SQb")
        QGb_t = sb("QGb", [128, 512], BF16); QGbb = Buf("QGb")
        RS = sb("RS", [128, 512], F32); RSb = Buf("RS")
        TA = sb("TA", [128, 512], F32); TAb = Buf("TA")
        PT = [sb("PT%d" % i, [128, 1024], BF16) for i in range(2)]; PTb = [Buf("PT0"), Buf("PT1")]; PTc = PTb[1]
        RD = TA; RDb = TAb
        CAT = sb("CAT", [128, 4, 512], BF16); CATb = [Buf("CAT%d" % i) for i in range(4)]
        CATL = [sb("CATL%d" % q, [128, 4, 512], BF16) for q in range(2)]; CATLb = [[Buf("CATL%d_%d" % (q, i)) for i in range(4)] for q in range(2)]
        WGUB = [sb("WGUB%d" % i, [128, 2048], BF16) for i in range(2)]; WGUBb = [Buf("WGUB0"), Buf("WGUB1")]
        WDC = [sb("WDC%d" % i, [128, 22 * 128], BF16) for i in range(2)]; WDCb = [Buf("WDC0"), Buf("WDC1")]
        WG = sb("WG", [128, 16, 128], BF16); WGb = Buf("WG")
        CM = sb("CM", [128, 3, 128], BF16); CMb = Buf("CM")
        IDF = sb("IDF", [128, 128], F32); IDFb = Buf("IDF")
        BROW = sb("BROW", [16, 128], BF16); BROWb = Buf("BROW")
        PV = sb("PV", [128, 46], F32); PVb = Buf("PV")
        GV = sb("GV", [128, 24], F32); GVb = Buf("GV")
        DV = sb("DV", [128, 40], F32); DVb = Buf("DV")
        RST = sb("RST", [128, 4, NBLK + 2], F32); RSTb = Buf("RST")
        SST = sb("SST", [128, 4, NBLK + 2], F32); SSTb = Buf("SST")
        CARF = sb("CARF", [128, 4], F32); CARFb = Buf("CARF")
        FS0 = sb("FS0", [128, 4], F32); FS8 = sb("FS8", [128, 4], F32); FSb = Buf("FS")
        LH0 = sb("LH0", [128, 4, 2], F32); LH8 = sb("LH8", [128, 4, 2], F32); LHb = Buf("LH")
        SSTL = sb("SSTL", [128, 4, max(H, 1)], F32); SSTLb = Buf("SSTL")
        RSTL = sb("RSTL", [128, 4, max(H, 1)], F32); RSTLb = Buf("RSTL")
        SELV = sb("SELV", [128, 2], F32); SELVb = Buf("SELV")
        CARB = sb("CARB", [128, 4], F32); CARBb = Buf("CARB")
        SM = sb("SM", [128, 8, 4], F32); SMb = [Buf("SM%d" % i) for i in range(8)]
        SSN = sb("SSN", [128, 16], F32); SSNb = Buf("SSN")
        RSN = sb("RSN", [128, 16], F32); RSNb = Buf("RSN")
        PSt = [st.enter_context(nc.psum_tensor("PS%d" % i, [128, 1024], F32)) for i in range(4)]
        BK = [PSt[i // 2][:, (i % 2) * 512:(i % 2) * 512 + 512] for i in range(8)]
        BKb = [Buf("BK%d" % i) for i in range(8)]

        PERM, BONES, IDENT = CM[:, 0, :], CM[:, 1, :], CM[:, 2, :]

        def f32t(t):
            return SCR[:, 2 * t:2 * t + 2, :].rearrange("p a b -> p (a b)").bitcast(F32)

        def f32b(t):
            return [SCRb[2 * t], SCRb[2 * t + 1]]

        def dma(out, in_, slot, reads=(), writes=()):
            P.op("sp", lambda e, o=out, i=in_: e.dma_start(out=o, in_=i), reads, writes, slot=slot)

        def mm(out, lhsT, rhs, start, stop, reads, writes):
            P.op("pe", lambda e, o=out, l=lhsT, r=rhs, s=start, t=stop: e.matmul(o, lhsT=l, rhs=r, start=s, stop=t),
                 reads, writes)

        def act(out, in_, func, reads, writes, **kw):
            P.op("act", lambda e, o=out, i=in_, f=func, k=kw: e.activation(out=o, in_=i, func=f, **k), reads, writes)

        def tt(eng, out, in0, in1, op, reads, writes):
            P.op(eng, lambda e, o=out, a=in0, b=in1, p=op: e.tensor_tensor(out=o, in0=a, in1=b, op=p), reads, writes)

        def ts(eng, out, in0, s1, s2, op0, op1, reads, writes):
            if op1 is None:
                P.op(eng, lambda e, o=out, a=in0, x=s1, p=op0: e.tensor_scalar(out=o, in0=a, scalar1=x, scalar2=None, op0=p),
                     reads, writes)
            else:
                P.op(eng, lambda e, o=out, a=in0, x=s1, y=s2, p=op0, q=op1:
                     e.tensor_scalar(out=o, in0=a, scalar1=x, scalar2=y, op0=p, op1=q), reads, writes)

        def stt(out, in0, scalar, in1, op0, op1, reads, writes):
            P.op("dve", lambda e, o=out, a=in0, s=scalar, b=in1, p=op0, q=op1:
                 e.scalar_tensor_tensor(out=o, in0=a, scalar=s, in1=b, op0=p, op1=q), reads, writes)

        def cp(eng, out, in_, reads, writes):
            P.op(eng, lambda e, o=out, i=in_: e.tensor_copy(out=o, in_=i), reads, writes)

        def mset(eng, ap, val, writes):
            P.op(eng, lambda e, a=ap, v=val: e.memset(a, v), (), writes)

        dbg_n = [0]

        def dump(name, ap, reads):
            if dbg is None or name not in dbg:
                return
            shp = list(ap.shape)
            d = nc.dram_tensor("dbg_" + name, shp, ap.dtype, kind="ExternalOutput").ap()
            dbg_out[name] = "dbg_" + name
            dbg_n[0] += 1
            dma(d, ap, "dbg%d" % dbg_n[0], reads=reads)

        dma(PV[:], pv_d, "pv", writes=[PVb])
        dma(GV[:], gv_d, "gv", writes=[GVb])
        if NSLOT > 1:
            dma(SELV[:], selv_d.partition_broadcast(128), "selv", writes=[SELVb])
        dma(GF[:], gfin_d.partition_broadcast(128), "gf", writes=[GFb])
        stg = [XB[:, 0:2, :].rearrange("p a b -> p (a b)"), XB[:, 2:4, :].rearrange("p a b -> p (a b)")]
        stgb = [[XBb[0], XBb[1]], [XBb[2], XBb[3]]]
        dma(stg[0][:, 0:2048], wgate_d, "stg0", writes=stgb[0])
        cp("dve", WG[:].rearrange("p a b -> p (a b)"), stg[0][:, 0:2048], stgb[0], [WGb])
        dma(stg[1][:, 0:384], cmat_d, "stg1", writes=stgb[1])
        cp("dve", CM[:].rearrange("p a b -> p (a b)"), stg[1][:, 0:384], stgb[1], [CMb])
        cp("dve", IDF[:], stg[1][:, 256:384], stgb[1], [IDFb])
        mset("pool", VT[:, :, 64:128], 1.0, [VTb])
        dma(stg[0][0:16, 0:128], brow_d.rearrange("o (k m) -> (o k) m", k=16), "stg0", writes=stgb[0])
        cp("dve", BROW[:], stg[0][0:16, 0:128], stgb[0], [BROWb])
        ts("dve", DV[:, 0:1], PV[:, 0:1], 0.125, None, ALU.mult, None, [PVb], [DVb])
        ts("dve", DV[:, 8:24], PV[:, 22:38], 0.5, None, ALU.mult, None, [PVb], [DVb])
        act(DV[:, 24:32], PV[:, 38:46], AF.Exp, [PVb], [DVb], scale=-1.0)
        act(DV[:, 24:32], DV[:, 24:32], AF.Ln, [DVb], [DVb], bias=1.0)
        ts("dve", DV[:, 24:32], DV[:, 24:32], -4.0, None, ALU.mult, None, [DVb], [DVb])
        ts("dve", DV[:, 32:40], DV[:, 24:32], 2.0, None, ALU.mult, None, [DVb], [DVb])
        GQ8, GK = DV[:, 0:1], PV[:, 1:2]
        si = 0
        for kc in range(8):
            s_ = si % 2; si += 1
            dma(stg[s_][:, 0:1792], win_d[kc * 128:(kc + 1) * 128, :], "stg%d" % s_, writes=stgb[s_])
            ts("dve" if kc % 2 == 0 else "pool", WIN[:, kc, :], stg[s_][:, 0:1792], GV[:, kc:kc + 1], None, ALU.mult, None,
               stgb[s_] + [GVb], [WINb])
        for kc in range(8):
            s_ = si % 2; si += 1
            dma(stg[s_][:, 0:1024], wout_d[kc * 128:(kc + 1) * 128, :], "stg%d" % s_, writes=stgb[s_])
            ts("dve" if kc % 2 == 0 else "pool", WOUT[:, kc, :], stg[s_][:, 0:1024], GV[:, 8 + kc:9 + kc], None, ALU.mult, None,
               stgb[s_] + [GVb], [WOUTb])
        WGSb = Buf("wgu_s")
        WDSb = Buf("wdn_s")

        def prep_ffn():
            stq = [QT[q][:].rearrange("p a b -> p (a b)").bitcast(F32) for q in range(2)]
            stqb = [QTb[0], QTb[1]]
            u = 0
            for j in range(22):
                b_ = j % 2
                for h in range(2):
                    q_ = u % 2; u += 1
                    dma(stq[q_][:, :], wgu_d[j][:, h * 1024:(h + 1) * 1024], "stq%d" % q_, writes=stqb[q_])
                    tt("dve", WGUB[b_][:, h * 1024:(h + 1) * 1024].rearrange("p (k c) -> p k c", k=4),
                       stq[q_][:, :].rearrange("p (k c) -> p k c", k=4),
                       bcast_last(GV[:, 16 + 4 * h:20 + 4 * h], 256), ALU.mult, stqb[q_] + [GVb], [WGUBb[b_]])
                    yield
                dma(wgu_s[j], WGUB[b_][:], "wgs%d" % b_, reads=[WGUBb[b_]], writes=[WGSb])
            for j in range(22):
                b_ = j % 2
                q_ = u % 2; u += 1
                dma(stq[q_][:, :], wdn_d[j], "stq%d" % q_, writes=stqb[q_])
                cp("pool", WDC[b_][:, 0:1024], stq[q_][:, :], stqb[q_], [WDCb[b_]])
                dma(wdn_s.rearrange("c p j e -> p c j e")[:, :, j, :], WDC[b_][:, 0:1024].rearrange("p (c e) -> p c e", c=8),
                    "wds%d" % b_, reads=[WDCb[b_]], writes=[WDSb])
                yield

        prep_gen = prep_ffn()

        smi = [0]

        def small():
            i = smi[0] % 7
            smi[0] += 1
            return SM[:, i, :], SMb[i]

        xni = [0]

        def norm_transpose(xtile, xbuf, np_, ti, bank):
            sm, smb = small()
            xn = xni[0] % 2
            xni[0] += 1
            act(XN[xn][:np_, :], xtile, AF.Square, [xbuf], [XNb[xn], smb], accum_out=sm[:np_, 0:1])
            act(sm[:np_, 1:2], sm[:np_, 0:1], AF.Ln, [smb], [smb], scale=1.0 / 1024, bias=EPS)
            act(sm[:np_, 2:3], sm[:np_, 1:2], AF.Exp, [smb], [smb], scale=-0.5)
            ts("dve", XN[xn][:np_, :], xtile, sm[:np_, 2:3], None, ALU.mult, None, [xbuf, smb], [XNb[xn]])
            tps = BK[bank].bitcast(BF16)
            for kc in range(8):
                P.op("pe", lambda e, o=tps[:, kc * 128:kc * 128 + np_], i=XN[xn][:np_, kc * 128:(kc + 1) * 128],
                     d=IDENT[:np_, :np_]: e.transpose(out=o, in_=i, identity=d), [XNb[xn], CMb], [BKb[bank]])
            cp("dve", XT[:, :, ti * 128:ti * 128 + np_],
               tps.rearrange("p (k c) -> p k c", k=8)[:, :, 0:np_], [BKb[bank]], [XTb[ti]])

        def proj_fm(c0, n, bank, ntile):
            for kc in range(8):
                mm(BK[bank][:, 0:n], WIN[:, kc, c0:c0 + 128], XT[:, kc, 0:n], kc == 0, kc == 7,
                   [WINb] + XTb[:ntile], [BKb[bank]])

        def norm_transpose_g(xtile, xbuf, np_, ti, bank):
            sm, smb = small()
            xn = xni[0] % 2
            xni[0] += 1
            act(XN[xn][:np_, :], xtile, AF.Square, [xbuf], [XNb[xn], smb], accum_out=sm[:np_, 0:1])
            act(sm[:np_, 1:2], sm[:np_, 0:1], AF.Ln, [smb], [smb], scale=1.0 / 1024, bias=EPS)
            act(sm[:np_, 2:3], sm[:np_, 1:2], AF.Exp, [smb], [smb], scale=-0.5)
            ts("dve", XN[xn][:np_, :], xtile, sm[:np_, 2:3], None, ALU.mult, None, [xbuf, smb], [XNb[xn]])
            yield
            tps = BK[bank].bitcast(BF16)
            for kc in range(8):
                P.op("pe", lambda e, o=tps[:, kc * 128:kc * 128 + np_], i=XN[xn][:np_, kc * 128:(kc + 1) * 128],
                     d=IDENT[:np_, :np_]: e.transpose(out=o, in_=i, identity=d), [XNb[xn], CMb], [BKb[bank]])
            yield
            cp("dve", XT[:, :, ti * 128:ti * 128 + np_],
               tps.rearrange("p (k c) -> p k c", k=8)[:, :, 0:np_], [BKb[bank]], [XTb[ti]])

        def qk_rope_g(bank, n, gvec, gbufs, dst, dstb, b2, b3):
            act(SQb_t[:, 0:n], BK[bank][:, 0:n], AF.Square, [BKb[bank]], [SQbb])
            act(QGb_t[:, 0:n], BK[bank][:, 0:n], AF.Identity, [BKb[bank]] + gbufs, [QGbb], scale=gvec)
            yield
            mm(BK[b2][:, 0:n], BONES, SQb_t[:, 0:n], True, True, [CMb, SQbb], [BKb[b2]])
            mm(BK[b3][:, 0:n], PERM, QGb_t[:, 0:n], True, True, [CMb, QGbb], [BKb[b3]])
            tt("dve", TA[:, 0:n], QGb_t[:, 0:n], ROC[:, 0:n], ALU.mult, [QGbb, ROCb], [TAb])
            yield
            act(RS[:, 0:n], BK[b2][:, 0:n], AF.Ln, [BKb[b2]], [RSb], scale=1.0 / 64, bias=EPS)
            act(RS[:, 0:n], RS[:, 0:n], AF.Exp, [RSb], [RSb], scale=-0.5)
            TB_, TBb_ = f32t(10), f32b(10)
            tt("dve", TB_[:, 0:n], BK[b3][:, 0:n], ROS[:, 0:n], ALU.mult, [BKb[b3], ROSb], TBb_)
            tt("dve", TA[:, 0:n], TA[:, 0:n], TB_[:, 0:n], ALU.add, [TAb] + TBb_, [TAb])
            yield
            tt("dve", dst, TA[:, 0:n], RS[:, 0:n], ALU.mult, [TAb, RSb], dstb)

        def proj_fm_g(c0, n, bank, ntile):
            for kc in range(8):
                mm(BK[bank][:, 0:n], WIN[:, kc, c0:c0 + 128], XT[:, kc, 0:n], kc == 0, kc == 7,
                   [WINb] + XTb[:ntile], [BKb[bank]])
                if kc == 3:
                    yield

        def qk_rope(bank, n, gvec, gbufs, dst, dstb, b2, b3):
            act(SQb_t[:, 0:n], BK[bank][:, 0:n], AF.Square, [BKb[bank]], [SQbb])
            act(QGb_t[:, 0:n], BK[bank][:, 0:n], AF.Identity, [BKb[bank]] + gbufs, [QGbb], scale=gvec)
            mm(BK[b2][:, 0:n], BONES, SQb_t[:, 0:n], True, True, [CMb, SQbb], [BKb[b2]])
            mm(BK[b3][:, 0:n], PERM, QGb_t[:, 0:n], True, True, [CMb, QGbb], [BKb[b3]])
            act(RS[:, 0:n], BK[b2][:, 0:n], AF.Ln, [BKb[b2]], [RSb], scale=1.0 / 64, bias=EPS)
            act(RS[:, 0:n], RS[:, 0:n], AF.Exp, [RSb], [RSb], scale=-0.5)
            tt("dve", TA[:, 0:n], QGb_t[:, 0:n], ROC[:, 0:n], ALU.mult, [QGbb, ROCb], [TAb])
            TB_, TBb_ = f32t(10), f32b(10)
            tt("dve", TB_[:, 0:n], BK[b3][:, 0:n], ROS[:, 0:n], ALU.mult, [BKb[b3], ROSb], TBb_)
            tt("dve", TA[:, 0:n], TA[:, 0:n], TB_[:, 0:n], ALU.add, [TAb] + TBb_, [TAb])
            tt("dve", dst, TA[:, 0:n], RS[:, 0:n], ALU.mult, [TAb, RSb], dstb)

        def cwb_of(cwslot):
            if cwslot == 22:
                return SCR[:, 22, :], [SCRb[22]]
            if cwslot == -1:
                return PT[0][:, 0:512], [PTb[0]]
            if cwslot == -2:
                return PT[1][:, 0:512], [PTb[1]]
            return PT[1][:, 512:1024], [PTc]

        def cw_of(cw):
            if cw < 100:
                return f32t(cw), f32b(cw)
            h = cw - 100
            return (CATL[0][:, 2 * h:2 * h + 2, :].rearrange("p a b -> p (a b)").bitcast(F32),
                    [CATLb[0][2 * h], CATLb[0][2 * h + 1]])

        def conv_chunk(c, n, cw=8, cwslot=22):
            CW, CWb_ = cw_of(cw)
            ts("dve", CW[:, 0:n], LB[:, c, 0:n], PV[:, 2 + c:3 + c], PV[:, 18 + c:19 + c], ALU.mult, ALU.add,
               [LBb, PVb], CWb_)
            for j in range(1, 4):
                stt(CW[:, 0:n], LB[:, c, j:j + n], PV[:, 2 + j * 4 + c:3 + j * 4 + c], CW[:, 0:n], ALU.mult, ALU.add,
                    [LBb, PVb] + CWb_, CWb_)
            cwb_ap, cwb_b = cwb_of(cwslot)
            cp("pool", cwb_ap[:, 0:n], CW[:, 0:n], CWb_, cwb_b)

        def gates(c, d, n, tset=None, cw=8, cwslot=22, split=False):
            o = (0 if d == 0 else 4) if tset is None else tset
            T0, T1, T2 = f32t(o), f32t(o + 1), f32t(o + 2)
            T0b, T1b, T2b = f32b(o), f32b(o + 1), f32b(o + 2)
            CW, CWb_ = cw_of(cw)
            cwb_ap, cwb_b = cwb_of(cwslot)
            ix = d * 4 + c
            ia, ixx = (d * 2 + 0) * 4 + c, (d * 2 + 1) * 4 + c
            TH = SCR[:, 2 * o:2 * o + 4, :].rearrange("p a b -> p (a b)").bitcast(F32).rearrange("p (t m) -> p t m", t=2)
            PS2 = PSt[3].rearrange("p (t m) -> p t m", t=2)

            def stage1():
                mm(BK[6][:, 0:n], WG[:, ia, :], cwb_ap[:, 0:n], True, False, [WGb] + cwb_b, [BKb[6]])
                mm(BK[6][:, 0:n], BROW[:, :], bcast_last(IDENT[0:16, ia:ia + 1], n)[:, 0, :], False, True, [BROWb, CMb], [BKb[6]])
                mm(BK[7][:, 0:n], WG[:, ixx, :], cwb_ap[:, 0:n], True, False, [WGb] + cwb_b, [BKb[7]])
                mm(BK[7][:, 0:n], BROW[:, :], bcast_last(IDENT[0:16, ixx:ixx + 1], n)[:, 0, :], False, True, [BROWb, CMb], [BKb[7]])
                if not split:
                    stage1_tanh()

            def stage1_tanh():
                act(TH[:, :, 0:n], PS2[:, :, 0:n], AF.Tanh, [BKb[6], BKb[7]], T0b + T1b, scale=0.5)

            stage1.tanh = stage1_tanh

            steps = [
                lambda: act(T2[:, 0:n], T0[:, 0:n], AF.Exp, T0b + [DVb], T2b, scale=DV[:, 24 + ix:25 + ix], bias=DV[:, 24 + ix:25 + ix]),
                lambda: act(T0[:, 0:n], T0[:, 0:n], AF.Exp, T0b + [DVb], T0b, scale=DV[:, 32 + ix:33 + ix], bias=DV[:, 32 + ix:33 + ix]),
                lambda: stt(T1[:, 0:n], T1[:, 0:n], 1.0, CW[:, 0:n], ALU.add, ALU.mult, T1b + CWb_, T1b),
                lambda: act(T0[:, 0:n], T0[:, 0:n], AF.Ln, T0b, T0b, scale=-1.0, bias=1.0 + 2.0 ** -23),
                lambda: act(T0[:, 0:n], T0[:, 0:n], AF.Exp, T0b, T0b, scale=0.5),
                lambda: stt(T1[:, 0:n], T0[:, 0:n], 0.5, T1[:, 0:n], ALU.mult, ALU.mult, T0b + T1b, T1b),
            ]
            return stage1, steps, (T1, T1b, T2, T2b)

        def scan(out, a, u, init, reads, writes):
            P.op("dve", lambda e, o=out, x=a, y=u, i=init: e.tensor_tensor_scan(out=o, data0=x, data1=y, initial=i,
                                                                               op0=ALU.mult, op1=ALU.add), reads, writes)

        def load_x(src, k):
            if k == 0:
                dma(XB[0:16, 0, :], meta, "xb0", writes=[XBb[0]])
            else:
                for i in range(4):
                    r0 = (k - 1) * 512 + i * 128
                    dma(XB[:, i, :], src[r0:r0 + 128, :], "xb%d" % i, writes=[XBb[i]])

        def load_block(src, k, with_rope=True, with_x=True, tabs=None, local=False):
            if with_x:
                load_x(src, k)
            if k == 0:
                n, s0 = 16, 0
            else:
                n, s0 = 512, 16 + (k - 1) * 512
            if with_rope:
                tc_, ts_ = tabs if tabs is not None else (ropec_d, ropes_d)
                t0 = (k - 1) * 512 if local else s0
                dma(ROC[:, 0:n], tc_[:, t0:t0 + n], "roc", writes=[ROCb])
                dma(ROS[:, 0:n], ts_[:, t0:t0 + n], "ros", writes=[ROSb])
            return n, s0

        for sl in range(NSLOT):
            xsrc = xs0 if sl == 0 else xs1
            local = (sl == 1)
            X2 = xloc if local else xs0
            Y2 = y1_d if local else y0_d
            TAB2 = (ropec1_d, ropes1_d) if local else (ropec_d, ropes_d)
            KLAST = H if local else NBLK
            mset("dve", CARB[:], 0.0, [CARBb])
            mset("dve", RST[:, :, NBLK + 1:NBLK + 2], 0.0, [RSTb])
            load_x(xsrc, NBLK)
            CWC = [(8, 22), (9, -1), (100, -2), (101, -3)]

            def p1_AB(k):
                n, s0 = load_block(xsrc, k, with_rope=False, with_x=False)
                ntile = 1 if k == 0 else 4
                for i in range(ntile):
                    norm_transpose(XB[:min(n, 128), i, :], XBb[i], min(n, 128), i, 4 + (i % 2))
                    yield
                if k >= 1:
                    load_x(xsrc, k - 1)
                dma(ROC[:, 0:n], ropec_d[:, s0:s0 + n], "roc", writes=[ROCb])
                dma(ROS[:, 0:n], ropes_d[:, s0:s0 + n], "ros", writes=[ROSb])
                if k >= 1:
                    if k == NBLK:
                        mset("dve", LB[:, :, 512:515], 0.0, [LBb])
                    else:
                        cp("dve", LB[:, :, 512:515], LB[:, :, 0:3], [LBb], [LBb])
                    for c in range(4):
                        proj_fm(768 + c * 128, 512, 5, 4)
                        act(LB[:, c, 0:512], BK[5][:, 0:512], AF.Copy, [BKb[5]], [LBb])
                        yield
                    cp("dve", RST[:, :, k:k + 1], LB[:, :, 0:1], [LBb], [RSTb])
                proj_fm(512, n, 0, ntile)
                yield
                qk_rope(0, n, GK, [PVb], KT[:, s0:s0 + n], [KTb], 1, 2)
                yield
                for i in range(ntile):
                    np_ = min(n, 128)
                    j = 0 if k == 0 else 1 + (k - 1) * 4 + i
                    bnk = 3 if i % 2 == 0 else 2
                    for kc in range(8):
                        mm(BK[bnk][:np_, 0:128], XT[:, kc, i * 128:i * 128 + np_], WIN[:, kc, 640:768], kc == 0, kc == 7,
                           [WINb, XTb[i]], [BKb[bnk]])
                    cp("dve", VT[:np_, j, 0:64], BK[bnk][:np_, 0:64], [BKb[bnk]], [VTb])
                    cp("dve", VT[:np_, j, 128:192], BK[bnk][:np_, 64:128], [BKb[bnk]], [VTb])
                    yield

            def p1_C(k):
                for c in range(4):
                    conv_chunk(c, 512, CWC[c][0], CWC[c][1])
                    yield
                for c0 in (0, 2):
                    cfg = [(c0, 4, 7), (c0 + 1, 0, 3)]
                    gs = []
                    for (c, tset, hb) in cfg:
                        st1, steps, res_ = gates(c, 1, 512, tset, CWC[c][0], CWC[c][1])
                        st1()
                        gs.append((steps, res_))
                        yield
                    for f_, g_ in zip(gs[0][0], gs[1][0]):
                        f_()
                        g_()
                        yield
                    for (c, tset, hb), (steps, (U, Ub, A, Ab)) in zip(cfg, gs):
                        if k == NBLK:
                            mset("dve", U[:, 510:512], 0.0, Ub)
                        HB, HBb = f32t(hb), f32b(hb)
                        scan(rev(HB[:, 0:512]), rev(A[:, 0:512]), rev(U[:, 0:512]), CARB[:, c:c + 1], Ab + Ub + [CARBb], HBb)
                        cp("dve", CARB[:, c:c + 1], HB[:, 0:1], HBb, [CARBb])
                        cp("dve", SST[:, c, k:k + 1], HB[:, 510:511], HBb, [SSTb])
                    yield

            for _ in p1_AB(NBLK):
                pass
            for k in range(NBLK, 0, -1):
                g1, g2 = p1_C(k), p1_AB(k - 1)
                a1 = a2 = True
                while a1 or a2:
                    if a1:
                        a1 = next(g1, "STOP") != "STOP"
                    if a2:
                        a2 = next(g2, "STOP") != "STOP"
                if sl == 0:
                    for _ in range(4):
                        next(prep_gen, None)
            dump("KT", KT[:], [KTb])
            dump("VT", VT[:].rearrange("p a b -> p (a b)"), [VTb])
            dump("SST", SST[:].rearrange("p a b -> p (a b)"), [SSTb])
            dump("RST", RST[:].rearrange("p a b -> p (a b)"), [RSTb])

            if sl == 0:
                for _ in prep_gen:
                    pass
            mset("dve", CARF[:], 0.0, [CARFb])

            def genA(k, lite=False):
                if lite:
                    n, s0 = load_block(xsrc, k, with_rope=False, with_x=True)
                else:
                    n, s0 = load_block(X2, k, with_rope=(k > 0), with_x=(k < 3), tabs=TAB2, local=local)
                ntile = 1 if k == 0 else 4
                nprev = 0 if k == 0 else (16 if k == 1 else 512)
                qp = k % 2
                for i in range(ntile):
                    yield from norm_transpose_g(XB[:min(n, 128), i, :], XBb[i], min(n, 128), i, 6 + (i % 2))
                    yield
                if k == 0:
                    mset("dve", LB[:, :, 0:2], 0.0, [LBb])
                elif local and not lite and k == 1:
                    pass
                else:
                    cp("dve", LB[:, :, 0:2], LB[:, :, nprev:nprev + 2], [LBb], [LBb])
                for c in range(4):
                    bk = 6 + (c % 2)
                    yield from proj_fm_g(768 + c * 128, n, bk, ntile)
                    yield
                    act(LB[:, c, 2:2 + n], BK[bk][:, 0:n], AF.Copy, [BKb[bk]], [LBb])
                if local and not lite:
                    cp("dve", LB[:, :, 2 + n:3 + n], RSTL[:, :, k - 1:k], [RSTLb, LBb], [LBb])
                else:
                    cp("dve", LB[:, :, 2 + n:3 + n], RST[:, :, k + 1:k + 2], [RSTb, LBb], [LBb])
                for c in range(4):
                    conv_chunk(c, n)
                    yield
                    if k > 0 and not lite:
                        yield from proj_fm_g(c * 128, n, 6, ntile)
                        yield
                        yield from qk_rope_g(6, n, GQ8, [DVb], QT[qp][:, c, 0:n], [QTb[qp][c]], 7, 6)
                        yield
                    HF, HFb = f32t(3), f32b(3)
                    HB, HBb = f32t(7), f32b(7)
                    GG, GGb = f32t(9), f32b(9)
                    GT, GTb = f32t(10), f32b(10)
                    s1f, stf, (U, Ub, A, Ab) = gates(c, 0, n, split=True)
                    s1f()
                    yield
                    s1f.tanh()
                    if k == 0 or lite:
                        for f_ in stf:
                            f_()
                        scan(HF[:, 0:n], A[:, 0:n], U[:, 0:n], CARF[:, c:c + 1], Ab + Ub + [CARFb], HFb)
                        cp("dve", CARF[:, c:c + 1], HF[:, n - 1:n], HFb, [CARFb])
                        yield
                        continue
                    s1b, stb, (U2, U2b, A2, A2b) = gates(c, 1, n, split=True)
                    s1b()
                    yield
                    s1b.tanh()
                    for f_, g_ in zip(stf, stb):
                        f_()
                        g_()
                    yield
                    scan(HF[:, 0:n], A[:, 0:n], U[:, 0:n], CARF[:, c:c + 1], Ab + Ub + [CARFb], HFb)
                    cp("dve", CARF[:, c:c + 1], HF[:, n - 1:n], HFb, [CARFb])
                    sst_ap, sst_b = (SSTL[:, c, k - 1:k], SSTLb) if local else (SST[:, c, k:k + 1], SSTb)
                    scan(rev(HB[:, 0:n]), rev(A2[:, 0:n]), rev(U2[:, 0:n]), sst_ap, A2b + U2b + [sst_b], HBb)
                    yield from proj_fm_g(1280 + c * 128, n, 6, ntile)
                    yield
                    act(GT[:, 0:n], BK[6][:, 0:n], AF.Square, [BKb[6]], GTb)
                    tt("pool", HF[:, 0:n], HF[:, 0:n], HB[:, 0:n], ALU.add, HFb + HBb, HFb)
                    ts("dve", GT[:, 0:n], GT[:, 0:n], 0.044715, 1.0, ALU.mult, ALU.add, GTb, GTb)
                    tt("dve", GT[:, 0:n], GT[:, 0:n], BK[6][:, 0:n], ALU.mult, GTb + [BKb[6]], GTb)
                    act(GT[:, 0:n], GT[:, 0:n], AF.Tanh, GTb, GTb, scale=0.7978845608028654)
                    stt(GG[:, 0:n], GT[:, 0:n], 1.0, BK[6][:, 0:n], ALU.add, ALU.mult, GTb + [BKb[6]], GGb)
                    stt(CATL[qp][:, c, 0:n], GG[:, 0:n], 0.5, HF[:, 0:n], ALU.mult, ALU.mult, GGb + HFb, [CATLb[qp][c]])
                    yield

            def genLite(k):
                n, s0 = load_block(xsrc, k, with_rope=False, with_x=(k == 0))
                ntile = 1 if k == 0 else 4
                nprev = 0 if k == 0 else (16 if k == 1 else 512)
                for i in range(ntile):
                    norm_transpose(XB[:min(n, 128), i, :], XBb[i], min(n, 128), i, 4 + (i % 2))
                if k + 1 <= H:
                    load_x(xsrc, k + 1)
                if k == 0:
                    mset("dve", LB[:, :, 0:2], 0.0, [LBb])
                else:
                    cp("dve", LB[:, :, 0:2], LB[:, :, nprev:nprev + 2], [LBb], [LBb])
                for c in range(4):
                    bk = 2 + (c % 2)
                    proj_fm(768 + c * 128, n, bk, ntile)
                    act(LB[:, c, 2:2 + n], BK[bk][:, 0:n], AF.Copy, [BKb[bk]], [LBb])
                cp("dve", LB[:, :, 2 + n:3 + n], RST[:, :, k + 1:k + 2], [RSTb, LBb], [LBb])
                for c0 in (0, 2):
                    cfg = [(c0, 0, 8, 22, 3), (c0 + 1, 4, 9, -1, 7)]
                    for (c, tset, cw, cws, hf) in cfg:
                        conv_chunk(c, n, cw, cws)
                    gs = []
                    for (c, tset, cw, cws, hf) in cfg:
                        st1, steps, res_ = gates(c, 0, n, tset, cw, cws)
                        st1()
                        gs.append((steps, res_))
                    for f_, g_ in zip(gs[0][0], gs[1][0]):
                        f_()
                        g_()
                    for (c, tset, cw, cws, hf), (steps, (U, Ub, A, Ab)) in zip(cfg, gs):
                        HF, HFb = f32t(hf), f32b(hf)
                        scan(HF[:, 0:n], A[:, 0:n], U[:, 0:n], CARF[:, c:c + 1], Ab + Ub + [CARFb], HFb)
                        cp("dve", CARF[:, c:c + 1], HF[:, n - 1:n], HFb, [CARFb])
                yield

            def genATT(k):
                qp = k % 2
                for c in range(4):
                    def qk(j):
                        kp = 16 if j == 0 else 128
                        c0 = 0 if j == 0 else 16 + (j - 1) * 128
                        par = j % 2
                        S = PSt[par]
                        mm(S[:kp, 0:512], KT[0:64, c0:c0 + kp], QT[qp][0:64, c, :], True, True, [KTb, QTb[qp][c]],
                           [BKb[2 * par], BKb[2 * par + 1]])
                        mm(S[:kp, 512:1024], KT[64:128, c0:c0 + kp], QT[qp][64:128, c, :], True, True, [KTb, QTb[qp][c]],
                           [BKb[2 * par], BKb[2 * par + 1]])

                    def ex(j):
                        kp = 16 if j == 0 else 128
                        par = j % 2
                        S = PSt[par]
                        act(PT[par][:kp, :], S[:kp, :], AF.Exp, [BKb[2 * par], BKb[2 * par + 1]], [PTb[par]])

                    def pv(j):
                        kp = 16 if j == 0 else 128
                        par = j % 2
                        st_, sp_ = (j == 0), (j == NKT - 1)
                        mm(BK[4], VT[:kp, j, 0:128], PT[par][:kp, 0:512], st_, sp_, [VTb, PTb[par]], [BKb[4]])
                        mm(BK[5], VT[:kp, j, 64:192], PT[par][:kp, 512:1024], st_, sp_, [VTb, PTb[par]], [BKb[5]])

                    qk(0)
                    for j in range(NKT):
                        if j + 1 < NKT:
                            qk(j + 1)
                        ex(j)
                        if j >= 1:
                            pv(j - 1)
                        yield
                    pv(NKT - 1)
                    P.op("dve", lambda e: e.reciprocal(out=RD[0:64, :], in_=BK[4][64:128, :]), [BKb[4]], [RDb])
                    P.op("dve", lambda e: e.reciprocal(out=RD[64:128, :], in_=BK[5][0:64, :]), [BKb[5]], [RDb])
                    tt("dve", CAT[0:64, c, :], BK[4][0:64, :], RD[0:64, :], ALU.mult, [BKb[4], RDb], [CATb[c]])
                    tt("dve", CAT[64:128, c, :], BK[5][64:128, :], RD[64:128, :], ALU.mult, [BKb[5], RDb], [CATb[c]])

            def genC(k):
                qp = k % 2
                for i in range(4):
                    r0 = (k - 1) * 512 + i * 128
                    dma(XB[:, i, :], X2[r0:r0 + 128, :], "xb%d" % i, writes=[XBb[i]])

                def catc(ch):
                    return (CAT[:, ch, :], CATb[ch]) if ch < 4 else (CATL[qp][:, ch - 4, :], CATLb[qp][ch - 4])
                for i in range(4):
                    for g in range(2):
                        bnk = 6 + g
                        for cc in range(4):
                            ct, cb = catc(g * 4 + cc)
                            mm(BK[bnk][:, 0:128], ct[:, i * 128:(i + 1) * 128], ct[:, i * 128:(i + 1) * 128],
                               cc == 0, cc == 3, [cb], [BKb[bnk]])
                        tmp, tmpb = (TA, TAb) if g == 0 else (RS, RSb)
                        tt("dve", tmp[:, 0:128], BK[bnk][:, 0:128], IDF[:], ALU.mult, [BKb[bnk], IDFb], [tmpb])
                        P.op("dve", lambda e, o=SSN[:, i * 2 + g:i * 2 + g + 1], t_=tmp: e.reduce_sum(out=o, in_=t_[:, 0:128], axis=AX.X),
                             [tmpb], [SSNb])
                    yield
                act(RSN[:, 0:8], SSN[:, 0:8], AF.Ln, [SSNb], [RSNb], scale=1.0 / 512, bias=EPS)
                act(RSN[:, 0:8], RSN[:, 0:8], AF.Exp, [RSNb], [RSNb], scale=-0.5)
                for i in range(4):
                    for hc in range(2):
                        for cc in range(4):
                            ct, cb = catc(cc)
                            mm(BK[6], ct[:, i * 128:(i + 1) * 128], WOUT[:, cc, hc * 512:(hc + 1) * 512], cc == 0, cc == 3,
                               [cb, WOUTb], [BKb[6]])
                        yield
                        for cc in range(4, 8):
                            ct, cb = catc(cc)
                            mm(BK[7], ct[:, i * 128:(i + 1) * 128], WOUT[:, cc, hc * 512:(hc + 1) * 512], cc == 4, cc == 7,
                               [cb, WOUTb], [BKb[7]])
                        xh = XB[:, i, hc * 512:(hc + 1) * 512]
                        stt(xh, BK[6], RSN[:, 2 * i:2 * i + 1], xh, ALU.mult, ALU.add, [BKb[6], RSNb, XBb[i]], [XBb[i]])
                        stt(xh, BK[7], RSN[:, 2 * i + 1:2 * i + 2], xh, ALU.mult, ALU.add, [BKb[7], RSNb, XBb[i]], [XBb[i]])
                        yield
                for i in range(4):
                    yield from norm_transpose_g(XB[:, i, :], XBb[i], 128, i, 6 + (i % 2))
                    yield
                for j in range(22):
                    b_ = j % 2
                    dma(WGUB[b_][:], wgu_s[j], "wgu%d" % b_, reads=[WGSb], writes=[WGUBb[b_]])
                    for kc in range(8):
                        mm(BK[6], WGUB[b_][:, kc * 256:kc * 256 + 128], XT[:, kc, :], kc == 0, kc == 7,
                           [WGUBb[b_]] + XTb, [BKb[6]])
                        if kc % 4 == 3:
                            yield
                    for kc in range(8):
                        mm(BK[7], WGUB[b_][:, kc * 256 + 128:kc * 256 + 256], XT[:, kc, :], kc == 0, kc == 7,
                           [WGUBb[b_]] + XTb, [BKb[7]])
                        if kc == 3:
                            yield
                    tmp, tmpb = (TA, TAb) if b_ == 0 else (RS, RSb)
                    act(tmp[:], BK[6], AF.Tanh, [BKb[6]], [tmpb], scale=0.5)
                    stt(tmp[:], tmp[:], 1.0, BK[6], ALU.add, ALU.mult, [tmpb, BKb[6]], [tmpb])
                    stt(SCR[:, j, :], tmp[:], 0.5, BK[7], ALU.mult, ALU.mult, [tmpb, BKb[7]], [SCRb[j]])
                    yield
                for cc in range(8):
                    b_ = cc % 2
                    dma(WDC[b_][:], wdn_s[cc].rearrange("p j e -> p (j e)"), "wdn%d" % b_, reads=[WDSb], writes=[WDCb[b_]])
                    for j in range(22):
                        mm(BK[6], WDC[b_][:, j * 128:(j + 1) * 128], SCR[:, j, :], j == 0, j == 21, [WDCb[b_], SCRb[j]], [BKb[6]])
                        if j % 4 == 3:
                            yield
                    yield
                    act(RS[:], BK[6], AF.Copy, [BKb[6]], [RSb])
                    for i in range(4):
                        P.op("pe", lambda e, o=BK[7][:, i * 128:(i + 1) * 128], i_=RS[:, i * 128:(i + 1) * 128]:
                             e.transpose(out=o, in_=i_, identity=IDF[:]), [RSb, IDFb], [BKb[7]])
                    tt("dve", XB[:, :, cc * 128:(cc + 1) * 128], BK[7].rearrange("p (i e) -> p i e", i=4),
                       XB[:, :, cc * 128:(cc + 1) * 128], ALU.add, [BKb[7]] + XBb, XBb)
                    yield
                for i in range(4):
                    sm, smb = small()
                    xn = xni[0] % 2
                    xni[0] += 1
                    act(XN[xn][:], XB[:, i, :], AF.Square, [XBb[i]], [XNb[xn], smb], accum_out=sm[:, 0:1])
                    act(sm[:, 1:2], sm[:, 0:1], AF.Ln, [smb], [smb], scale=1.0 / 1024, bias=EPS)
                    act(sm[:, 2:3], sm[:, 1:2], AF.Exp, [smb], [smb], scale=-0.5)
                    stt(XB[:, i, :], XB[:, i, :], sm[:, 2:3], GF[:], ALU.mult, ALU.mult, [XBb[i], smb, GFb], [XBb[i]])
                    r0 = (k - 1) * 512 + i * 128
                    dma(Y2[r0:r0 + 128, :], XB[:, i, :], "y%d" % i, reads=[XBb[i]])
                    if k + 2 <= KLAST:
                        r2 = (k + 1) * 512 + i * 128
                        dma(XB[:, i, :], X2[r2:r2 + 128, :], "xb%d" % i, writes=[XBb[i]])
                    yield

            def lane(w):
                if w - 1 >= 1:
                    yield from genC(w - 1)
                if w + 1 <= KLAST:
                    yield from genA(w + 1)

            def drain(g):
                for _ in g:
                    pass

            def count(g):
                P.dry = True
                n_ = sum(1 for _ in g)
                P.dry = False
                return n_

            if local:
                for k in range(0, H + 1):
                    drain(genLite(k))
                    if k == 0:
                        cp("dve", FS0[:], CARF[:], [CARFb], [FSb])
                        cp("dve", LH0[:], LB[:, :, 16:18], [LBb], [LHb])
                    if k == H:
                        cp("dve", FS8[:], CARF[:], [CARFb], [FSb])
                        cp("dve", LH8[:], LB[:, :, 512:514], [LBb], [LHb])
                sa, sb_ = SELV[:, 0:1], SELV[:, 1:2]
                ts("dve", CARF[:], FS0[:], sa, None, ALU.mult, None, [FSb, SELVb], [CARFb])
                stt(CARF[:], FS8[:], sb_, CARF[:], ALU.mult, ALU.add, [FSb, SELVb, CARFb], [CARFb])
                ts("dve", LB[:, :, 0:2], LH0[:], sa, None, ALU.mult, None, [LHb, SELVb], [LBb])
                stt(LB[:, :, 0:2], LH8[:], sb_, LB[:, :, 0:2], ALU.mult, ALU.add, [LHb, SELVb, LBb], [LBb])
                ts("dve", SSTL[:], SST[:, :, 1:1 + H], sa, None, ALU.mult, None, [SSTb, SELVb], [SSTLb])
                stt(SSTL[:], SST[:, :, 1 + H:1 + 2 * H], sb_, SSTL[:], ALU.mult, ALU.add, [SSTb, SELVb, SSTLb], [SSTLb])
                ts("dve", RSTL[:], RST[:, :, 2:2 + H], sa, None, ALU.mult, None, [RSTb, SELVb], [RSTLb])
                stt(RSTL[:], RST[:, :, 2 + H:2 + 2 * H], sb_, RSTL[:], ALU.mult, ALU.add, [RSTb, SELVb, RSTLb], [RSTLb])
            else:
                drain(genA(0))
            drain(genA(1))
            for w in range(1, KLAST + 1):
                nl = count(lane(w))
                na = 4 * NKT
                ln = lane(w)
                done = 0
                for step, _ in enumerate(genATT(w), 1):
                    tgt = (step * nl) // na
                    while done < tgt:
                        next(ln, None)
                        done += 1
                drain(ln)
            drain(genC(KLAST))

        P.emit(st, final_slots=["y0", "y1", "y2", "y3"] + ["dbg%d" % (i + 1) for i in range(dbg_n[0])])
    return nc, dbg_out


def rope_tables(L):
    f32 = np.float32
    t = np.arange(L - 16)
    row = (t // 64).astype(f32)
    col = (t % 64).astype(f32)
    freqs = (f32(10000.0) ** (-np.arange(0, 32, 2, dtype=f32) / f32(32))).astype(f32)
    C = np.ones((128, L), f32)
    S = np.zeros((128, L), f32)
    for p in range(128):
        dm = p % 64
        pos = row if dm // 32 == 0 else col
        ang = (pos * freqs[dm % 16]).astype(f32)
        C[p, 16:] = np.cos(ang)
        S[p, 16:] = np.sin(ang)
    return C, S


def host_layout(inp):
    f32 = np.float32
    w_in = np.asarray(inp["w_in"])[0]
    qcols = np.array([(c + 4 * h) * 64 + d for c in range(4) for h in range(2) for d in range(64)])
    win = np.ascontiguousarray(np.concatenate([w_in[:, qcols], w_in[:, 512:]], axis=1), dtype=f32)
    rowmap = np.array([(kc + 4 * (p // 64)) * 64 + p % 64 if kc < 4 else 512 + (kc - 4) * 128 + p
                       for kc in range(8) for p in range(128)])
    w_out = np.asarray(inp["w_out"])[0]
    wout = np.ascontiguousarray(w_out[rowmap, :], dtype=f32)
    gcat = np.concatenate([np.asarray(inp["attn_out_g"])[0], np.asarray(inp["lru_out_g"])[0]])[rowmap]
    pk = lambda v: np.asarray(v, f32).reshape(8, 128).T
    gv = np.ascontiguousarray(np.concatenate([pk(np.asarray(inp["norm_mix_g"])[0]), pk(gcat),
                                              pk(np.asarray(inp["norm_ffn_g"])[0])], axis=1), dtype=f32)
    wgu_o = np.asarray(inp["w_gate_up"])[0]
    g4 = wgu_o[:, :2816].reshape(8, 128, 22, 128)
    u4 = wgu_o[:, 2816:].reshape(8, 128, 22, 128)
    wgu = np.stack([g4, u4], axis=0).transpose(3, 2, 1, 0, 4)
    wgu = np.ascontiguousarray(wgu.reshape(22, 128, 2048), dtype=f32)
    wdn = np.ascontiguousarray(np.asarray(inp["w_down"])[0].reshape(22, 128, 1024), dtype=f32)
    pv = np.zeros((128, 46), f32)
    p = np.arange(128)
    pv[:, 0] = np.asarray(inp["q_norm_g"])[0][p % 64]
    pv[:, 1] = np.asarray(inp["k_norm_g"])[0][p % 64]
    cw = np.asarray(inp["conv_w"])[0]
    cb = np.asarray(inp["conv_b"])[0]
    for c in range(4):
        for j in range(4):
            pv[:, 2 + j * 4 + c] = cw[j, c * 128 + p]
        pv[:, 18 + c] = cb[c * 128 + p]
        for d in range(2):
            pv[:, 22 + d * 4 + c] = np.asarray(inp["lru_b_a"])[0][d, c * 128 + p]
            pv[:, 30 + d * 4 + c] = np.asarray(inp["lru_b_x"])[0][d, c * 128 + p]
            pv[:, 38 + d * 4 + c] = np.asarray(inp["lru_lam"])[0][d, c * 128 + p]
    wgate = np.zeros((128, 16, 128), f32)
    wa = np.asarray(inp["lru_w_a"])[0]
    wx = np.asarray(inp["lru_w_x"])[0]
    for d in range(2):
        for ax, w in enumerate((wa, wx)):
            for c in range(4):
                for hb in range(2):
                    wgate[hb * 64:(hb + 1) * 64, (d * 2 + ax) * 4 + c, hb * 64:(hb + 1) * 64] = w[d, 2 * c + hb]
    brow = np.zeros((1, 16, 128), f32)
    for d in range(2):
        for ax, bb in enumerate((np.asarray(inp['lru_b_a'])[0], np.asarray(inp['lru_b_x'])[0])):
            for c in range(4):
                brow[0, (d * 2 + ax) * 4 + c, :] = bb[d, c * 128:(c + 1) * 128]
    cmat = np.zeros((128, 3, 128), f32)
    for m in range(128):
        if (m % 64) % 32 < 16:
            cmat[m + 16, 0, m] = -1.0
        else:
            cmat[m - 16, 0, m] = 1.0
        cmat[(m // 64) * 64:(m // 64) * 64 + 64, 1, m] = 1.0
        cmat[m, 2, m] = 1.0
    return dict(win=win, wout=wout, wgu=wgu, wdn=wdn, gv=gv,
                gfin=np.asarray(inp["final_norm_g"], f32).reshape(1, 1024),
                pv=pv, brow=brow.reshape(1, 2048), wgate=wgate.reshape(128, 2048), cmat=cmat.reshape(128, 384),
                meta=np.ascontiguousarray(np.asarray(inp["meta_tokens"], f32)))


_CACHE = {}


def run(cores, shared, NBLK, NSLOT, dbg=None):
    key = (NBLK, NSLOT)
    if key not in _CACHE:
        _CACHE[key] = build(NBLK, NSLOT, dbg)
    nc, dbg_out = _CACHE[key]
    C, S = rope_tables(NBLK * 512 + 16)
    TH = (NBLK // 2) * 512
    in_maps = []
    for cd in cores:
        m = dict(shared)
        m["xs0"] = np.ascontiguousarray(cd["x0"], dtype=np.float32)
        m["ropec"] = C
        m["ropes"] = S
        if NSLOT > 1:
            hh = cd["half"]
            x1 = np.asarray(cd["x1"], dtype=np.float32)
            m["xs1"] = np.ascontiguousarray(x1)
            m["xloc"] = np.ascontiguousarray(x1[hh * TH:(hh + 1) * TH])
            m["ropec1"] = np.ascontiguousarray(C[:, 16 + hh * TH:16 + (hh + 1) * TH])
            m["ropes1"] = np.ascontiguousarray(S[:, 16 + hh * TH:16 + (hh + 1) * TH])
            m["selv"] = np.array([[1.0 - hh, float(hh)]], dtype=np.float32)
        in_maps.append(m)
    res = run_bass_kernel_spmd(nc, in_maps, core_ids=list(range(len(cores))))
    return res.results, dbg_out


def kernel(**inputs):
    xp = np.asarray(inputs["x_prompt"], np.float32)
    xsm = np.asarray(inputs["x_sample"], np.float32)
    shared = host_layout(inputs)
    cores = [dict(x0=xp[ci], x1=xsm[ci // 2], half=ci % 2) for ci in range(8)]
    results, _ = run(cores, shared, 16, 2)
    yp = np.stack([results[ci]["y0"] for ci in range(8)], axis=0)
    ys = np.stack([np.concatenate([results[2 * j]["y1"], results[2 * j + 1]["y1"]], axis=0) for j in range(4)], axis=0)
    return (yp, ys)
```
